# Optimizing a Trainium2 kernel written in Bass

```python
import jax
import jax.numpy as jnp
from jax import lax

D_MODEL = 1024
BATCH = 8
SEQ = 2048
DEPTH = 2

CHUNK = 64
N_META = 16
D_CONV = 512
CONV_WIDTH = 31
N_HEADS = 8
Q_LORA = 256
KV_LORA = 128
NOPE_DIM = 64
ROPE_DIM = 32
QK_DIM = NOPE_DIM + ROPE_DIM
V_DIM = 64
ROPE_THETA = 10000.0
Q_BLOCK = 128
N_IN = 2 * D_CONV + Q_LORA + KV_LORA + ROPE_DIM + 2 * D_MODEL
PEER_HEADS = 8
PEER_QDIM = 256
PEER_HALF = PEER_QDIM // 2
N_KEYS = 128
N_EXPERTS = N_KEYS * N_KEYS
PEER_TOPK = 16
PEER_BLOCK = 256

EPS = 1e-6
MASK_VALUE = -1e30
PAD_CHUNK = 2 ** 30

kernel_name = 'hybrid_conv_mla_peer_trunk'


def _rmsnorm(x, g):
    xf = x.astype(jnp.float32)
    y = xf * lax.rsqrt(jnp.mean(xf * xf, axis=-1, keepdims=True) + EPS)
    return (y * g.astype(jnp.float32)).astype(x.dtype)


def _layernorm(x, g, b):
    xf = x.astype(jnp.float32)
    mu = jnp.mean(xf, axis=-1, keepdims=True)
    xc = xf - mu
    y = xc * lax.rsqrt(jnp.mean(xc * xc, axis=-1, keepdims=True) + EPS)
    return (y * g.astype(jnp.float32) + b.astype(jnp.float32)).astype(x.dtype)


def _rope_tables(length):
    pos = jnp.arange(length, dtype=jnp.float32)
    inv = 1.0 / (ROPE_THETA ** (jnp.arange(0, ROPE_DIM, 2, dtype=jnp.float32) / ROPE_DIM))
    ang = pos[:, None] * inv[None, :]
    ang = jnp.concatenate([ang, ang], axis=-1)
    return jnp.cos(ang), jnp.sin(ang)


def _apply_rope(x, cos, sin):
    half = ROPE_DIM // 2
    x1, x2 = x[..., :half], x[..., half:]
    rot = jnp.concatenate([-x2, x1], axis=-1)
    c = cos[None, :, None, :].astype(x.dtype)
    s = sin[None, :, None, :].astype(x.dtype)
    return x * c + rot * s


def _chunk_ids(length):
    p = jnp.arange(length, dtype=jnp.int32)
    return jnp.where(p < N_META, 0, 1 + (p - N_META) // CHUNK).astype(jnp.int32)


def _chunk_causal_attention(q, k, v, chunk):
    B, L, H, dk = q.shape
    dv = v.shape[-1]
    l_pad = -(-L // Q_BLOCK) * Q_BLOCK
    pad = l_pad - L
    padw = ((0, 0), (0, pad), (0, 0), (0, 0))
    q = jnp.pad(q, padw)
    k = jnp.pad(k, padw).astype(jnp.float32)
    v = jnp.pad(v, padw)
    cid = jnp.pad(chunk, (0, pad), constant_values=PAD_CHUNK)
    scale = dk ** -0.5

    def block(i):
        qs = lax.dynamic_slice_in_dim(q, i * Q_BLOCK, Q_BLOCK, axis=1).astype(jnp.float32)
        qc = lax.dynamic_slice_in_dim(cid, i * Q_BLOCK, Q_BLOCK)
        s = jnp.einsum('bqhd,bkhd->bhqk', qs, k) * scale
        mask = cid[None, :] <= qc[:, None]
        s = jnp.where(mask[None, None], s, MASK_VALUE)
        p = jax.nn.softmax(s, axis=-1)
        return jnp.einsum('bhqk,bkhd->bqhd', p, v.astype(jnp.float32)).astype(v.dtype)

    out = lax.map(block, jnp.arange(l_pad // Q_BLOCK))
    out = jnp.transpose(out, (1, 0, 2, 3, 4)).reshape(B, l_pad, H, dv)
    return out[:, :L]


def _causal_depthwise_conv(x, w, b):
    C = x.shape[-1]
    xp = jnp.pad(x, ((0, 0), (CONV_WIDTH - 1, 0), (0, 0)))
    y = lax.conv_general_dilated(xp, w[:, None, :].astype(x.dtype), window_strides=(1,),
                                 padding='VALID', dimension_numbers=('NWC', 'WIO', 'NWC'),
                                 feature_group_count=C)
    return y + b.astype(x.dtype)


def _mixer_block(x, mix_g, w_in, conv_w, conv_b, conv_ln_g, conv_ln_b, w_conv_out,
                 q_a_g, w_uq, kv_a_g, w_ukv, q_norm_g, k_norm_g, w_mla_out, w_out,
                 cos, sin, chunk):
    B, L, _ = x.shape
    h = _rmsnorm(x, mix_g)
    z = h @ w_in
    o0 = 2 * D_CONV
    o1 = o0 + Q_LORA
    o2 = o1 + KV_LORA
    o3 = o2 + ROPE_DIM
    conv_in, c_q, c_kv, k_rope, gate_logits = (z[..., :o0], z[..., o0:o1], z[..., o1:o2],
                                              z[..., o2:o3], z[..., o3:])

    u = conv_in[..., :D_CONV] * jax.nn.sigmoid(conv_in[..., D_CONV:])
    u = _causal_depthwise_conv(u, conv_w, conv_b)
    u = jax.nn.silu(_layernorm(u, conv_ln_g, conv_ln_b))
    y_conv = u @ w_conv_out

    q = (_rmsnorm(c_q, q_a_g) @ w_uq).reshape(B, L, N_HEADS, QK_DIM)
    kv = (_rmsnorm(c_kv, kv_a_g) @ w_ukv).reshape(B, L, N_HEADS, NOPE_DIM + V_DIM)
    k_nope, v = kv[..., :NOPE_DIM], kv[..., NOPE_DIM:]
    k_r = jnp.broadcast_to(k_rope[:, :, None, :], (B, L, N_HEADS, ROPE_DIM))
    k = jnp.concatenate([k_nope, k_r], axis=-1)
    q = _rmsnorm(q, q_norm_g)
    k = _rmsnorm(k, k_norm_g)
    q = jnp.concatenate([q[..., :NOPE_DIM], _apply_rope(q[..., NOPE_DIM:], cos, sin)], axis=-1)
    k = jnp.concatenate([k[..., :NOPE_DIM], _apply_rope(k[..., NOPE_DIM:], cos, sin)], axis=-1)
    o = _chunk_causal_attention(q, k, v, chunk).reshape(B, L, N_HEADS * V_DIM)
    y_mla = o @ w_mla_out

    gates = jax.nn.sigmoid(gate_logits)
    merged = gates[..., :D_MODEL] * y_conv + gates[..., D_MODEL:] * y_mla
    return x + merged @ w_out


def _peer(h, wq, keys, u_table, v_table):
    B, L, D = h.shape
    T = B * L
    hf = h.reshape(T, D)
    q = (hf @ wq).reshape(T, PEER_HEADS, 2, PEER_HALF)
    s = jnp.einsum('thpd,hpnd->thpn', q, keys)
    sv, si = lax.top_k(s, PEER_TOPK)
    cand = (sv[:, :, 0, :, None] + sv[:, :, 1, None, :]).reshape(T, PEER_HEADS, PEER_TOPK * PEER_TOPK)
    cidx = (si[:, :, 0, :, None] * N_KEYS + si[:, :, 1, None, :]).reshape(T, PEER_HEADS, PEER_TOPK * PEER_TOPK)
    top_s, top_pos = lax.top_k(cand, PEER_TOPK)
    eidx = jnp.take_along_axis(cidx, top_pos, axis=-1)
    gw = jax.nn.softmax(top_s.astype(jnp.float32), axis=-1).astype(h.dtype)
    eidx = eidx.reshape(T, PEER_HEADS * PEER_TOPK)
    gw = gw.reshape(T, PEER_HEADS * PEER_TOPK)

    t_pad = -(-T // PEER_BLOCK) * PEER_BLOCK
    pad = t_pad - T
    nb = t_pad // PEER_BLOCK
    xs = jnp.pad(hf, ((0, pad), (0, 0))).reshape(nb, PEER_BLOCK, D)
    ids = jnp.pad(eidx, ((0, pad), (0, 0))).reshape(nb, PEER_BLOCK, PEER_HEADS * PEER_TOPK)
    ws = jnp.pad(gw, ((0, pad), (0, 0))).reshape(nb, PEER_BLOCK, PEER_HEADS * PEER_TOPK)

    def block(args):
        xb, ib, wb = args
        a = jnp.einsum('ted,td->te', u_table[ib], xb)
        act = jax.nn.gelu(a) * wb
        return jnp.einsum('te,ted->td', act, v_table[ib])

    out = lax.map(block, (xs, ids, ws)).reshape(t_pad, D)[:T]
    return out.reshape(B, L, D)


def setup_inputs(seed: int = 0) -> dict:
    key = jax.random.key(seed)
    ks = jax.random.split(key, 24)
    f32 = jnp.float32

    def w(k, shape, fan_in):
        return jax.random.normal(k, shape, f32) * (fan_in ** -0.5)

    def gain(k, shape):
        return 1.0 + 0.02 * jax.random.normal(k, shape, f32)

    def bias(k, shape):
        return 0.02 * jax.random.normal(k, shape, f32)

    return {
        'x': jax.random.normal(ks[0], (BATCH, SEQ, D_MODEL), f32),
        'meta_tokens': jax.random.normal(ks[1], (N_META, D_MODEL), f32),
        'mix_norm_g': gain(ks[2], (DEPTH, D_MODEL)),
        'w_in': w(ks[3], (DEPTH, D_MODEL, N_IN), D_MODEL),
        'conv_w': w(ks[4], (DEPTH, CONV_WIDTH, D_CONV), CONV_WIDTH),
        'conv_b': bias(ks[5], (DEPTH, D_CONV)),
        'conv_ln_g': gain(ks[6], (DEPTH, D_CONV)),
        'conv_ln_b': bias(ks[7], (DEPTH, D_CONV)),
        'w_conv_out': w(ks[8], (DEPTH, D_CONV, D_MODEL), D_CONV),
        'q_a_norm_g': gain(ks[9], (DEPTH, Q_LORA)),
        'w_uq': w(ks[10], (DEPTH, Q_LORA, N_HEADS * QK_DIM), Q_LORA),
        'kv_a_norm_g': gain(ks[11], (DEPTH, KV_LORA)),
        'w_ukv': w(ks[12], (DEPTH, KV_LORA, N_HEADS * (NOPE_DIM + V_DIM)), KV_LORA),
        'q_norm_g': gain(ks[13], (DEPTH, QK_DIM)),
        'k_norm_g': gain(ks[14], (DEPTH, QK_DIM)),
        'w_mla_out': w(ks[15], (DEPTH, N_HEADS * V_DIM, D_MODEL), N_HEADS * V_DIM),
        'w_out': w(ks[16], (DEPTH, D_MODEL, D_MODEL), D_MODEL),
        'ffn_norm_g': gain(ks[17], (DEPTH, D_MODEL)),
        'peer_wq': w(ks[18], (DEPTH, D_MODEL, PEER_HEADS * PEER_QDIM), D_MODEL),
        'peer_keys': w(ks[19], (DEPTH, PEER_HEADS, 2, N_KEYS, PEER_HALF), PEER_HALF),
        'peer_u': w(ks[20], (DEPTH, N_EXPERTS, D_MODEL), D_MODEL),
        'peer_v': w(ks[21], (DEPTH, N_EXPERTS, D_MODEL), D_MODEL),
    }


def reference(x, meta_tokens, mix_norm_g, w_in, conv_w, conv_b, conv_ln_g, conv_ln_b,
              w_conv_out, q_a_norm_g, w_uq, kv_a_norm_g, w_ukv, q_norm_g, k_norm_g,
              w_mla_out, w_out, ffn_norm_g, peer_wq, peer_keys, peer_u, peer_v):
    B = x.shape[0]
    meta = jnp.broadcast_to(meta_tokens[None].astype(x.dtype), (B, N_META, D_MODEL))
    h = jnp.concatenate([meta, x], axis=1)
    L = h.shape[1]
    cos, sin = _rope_tables(L)
    chunk = _chunk_ids(L)
    for l in range(DEPTH):
        h = _mixer_block(h, mix_norm_g[l], w_in[l], conv_w[l], conv_b[l], conv_ln_g[l],
                         conv_ln_b[l], w_conv_out[l], q_a_norm_g[l], w_uq[l], kv_a_norm_g[l],
                         w_ukv[l], q_norm_g[l], k_norm_g[l], w_mla_out[l], w_out[l],
                         cos, sin, chunk)
        h = h + _peer(_rmsnorm(h, ffn_norm_g[l]), peer_wq[l], peer_keys[l], peer_u[l], peer_v[l])
    return h[:, N_META:]
```

```python
import contextlib
import numpy as np
import concourse.bass as bass
import concourse.mybir as mybir
from concourse.bass_utils import run_bass_kernel_spmd

F32 = mybir.dt.float32
BF16 = mybir.dt.bfloat16
I32 = mybir.dt.int32
U32 = mybir.dt.uint32
ALU = mybir.AluOpType
AF = mybir.ActivationFunctionType
AX = mybir.AxisListType

D = 1024
SEQ = 2048
NMETA = 16
L = SEQ + NMETA
DEPTH = 2
NB = 256
NRB = SEQ // NB
NSLOT = 6
NWT = 52
NKT = 1 + SEQ // 128
NEXP = 16384
NG = 6
NVEC = 160
V_MIXG, V_FFNG, V_CONVW, V_CONVB, V_LNG, V_LNB, V_QAG, V_KVAG, V_QNG, V_KNG = (
    0, 8, 16, 140, 144, 148, 152, 154, 155, 156)
EPS = 1e-6
SCALE = 96.0 ** -0.5


class Buf:
    __slots__ = ("name", "w", "r", "dsem", "dval")

    def __init__(self, name):
        self.name = name
        self.w = None
        self.r = {}
        self.dsem = None
        self.dval = 0


class Sched:
    def __init__(self, nc, stack):
        self.nc = nc
        self.stack = stack
        self.eng = {"pe": nc.tensor, "act": nc.scalar, "dve": nc.vector,
                    "pool": nc.gpsimd, "sp": nc.sync}
        self.sems = {}
        self.cnt = {}
        self.known = {}
        self.snap = {}
        for e in self.eng:
            self.sems[e] = stack.enter_context(nc.semaphore("s_" + e))
            self.cnt[e] = 0
            self.known[e] = {}
            self.snap[e] = {}
        self.ndsem = 0
        self.nwait = 0
        self.nins = 0

    def new_dsem(self):
        k = "d%d" % self.ndsem
        self.ndsem += 1
        self.sems[k] = self.stack.enter_context(self.nc.semaphore("s_" + k))
        return k

    def _need(self, e, deps, key, val):
        if self.known[e].get(key, 0) >= val:
            return
        if deps.get(key, 0) < val:
            deps[key] = val

    def _collect(self, e, R, W):
        deps = {}
        for b in R:
            if b.w is not None:
                k, v = b.w
                if k == e and e == "pe":
                    continue
                self._need(e, deps, k, v)
        for b in W:
            if b.w is not None:
                k, v = b.w
                if k != e:
                    self._need(e, deps, k, v)
            for k, v in b.r.items():
                if k != e:
                    self._need(e, deps, k, v)
        return deps

    def _emit_waits(self, e, deps):
        for k, v in deps.items():
            self.eng[e].wait_ge(self.sems[k], v)
            self.nwait += 1
            kn = self.known[e]
            if kn.get(k, 0) < v:
                kn[k] = v
            sn = self.snap.get(k, {}).get(v)
            if sn:
                for kk, vv in sn.items():
                    if kn.get(kk, 0) < vv:
                        kn[kk] = vv

    def op(self, e, fn, R=(), W=()):
        deps = self._collect(e, R, W)
        self._emit_waits(e, deps)
        ins = fn(self.eng[e])
        self.cnt[e] += 1
        n = self.cnt[e]
        ins.then_inc(self.sems[e], 1)
        self.nins += 1
        self.snap[e][n] = dict(self.known[e])
        for b in R:
            b.r[e] = n
        for b in W:
            b.w = (e, n)
            b.r = {}
        return ins

    def dma(self, q, out, in_, R=(), W=(), sembuf=None, indirect=None):
        deps = self._collect(q, R, W)
        self._emit_waits(q, deps)
        sb = sembuf if sembuf is not None else (W[0] if W else R[0])
        if sb.dsem is None:
            sb.dsem = self.new_dsem()
        if indirect is not None:
            ins = self.eng[q].indirect_dma_start(out=out, out_offset=None, in_=in_,
                                                 in_offset=indirect)
        else:
            ins = self.eng[q].dma_start(out=out, in_=in_)
        sb.dval += 16
        ins.then_inc(self.sems[sb.dsem], 16)
        tok = (sb.dsem, sb.dval)
        for b in R:
            b.r[tok[0]] = tok[1]
        for b in W:
            b.w = tok
            b.r = {}
        return tok

    def wait_all(self, e, bufs):
        deps = {}
        for b in bufs:
            if b.w is not None:
                self._need(e, deps, b.w[0], b.w[1])
            for k, v in b.r.items():
                self._need(e, deps, k, v)
        self._emit_waits(e, deps)

    def barrier(self, bufs=()):
        for e in self.eng:
            deps = {}
            for f in self.eng:
                if f != e and self.cnt[f] > 0:
                    self._need(e, deps, f, self.cnt[f])
            for b in bufs:
                if b.w is not None:
                    self._need(e, deps, b.w[0], b.w[1])
                for k, v in b.r.items():
                    self._need(e, deps, k, v)
            self._emit_waits(e, deps)


class Alloc:
    def __init__(self, nc, stack):
        self.nc = nc
        self.stack = stack
        self.n = [0]

    def sub(self, stack):
        a = Alloc(self.nc, stack)
        a.n = self.n
        return a

    def sb(self, shape, dt, name="t"):
        self.n[0] += 1
        nm = "%s_%d" % (name, self.n[0])
        t = self.stack.enter_context(self.nc.sbuf_tensor(nm, list(shape), dt))
        return t, Buf(nm)

    def ps(self, shape, dt, name="p"):
        self.n[0] += 1
        nm = "%s_%d" % (name, self.n[0])
        t = self.stack.enter_context(self.nc.psum_tensor(nm, list(shape), dt))
        return t, Buf(nm)


def build_program(n_layers=DEPTH, do_mixer=True, do_peer=True):
    nc = bass.Bass("TRN2", target_bir_lowering=False)

    def din(name, shape, dt=F32):
        return nc.dram_tensor(name, list(shape), dt, kind="ExternalInput").ap()

    xT_d = din("xT", [D, SEQ])
    meta_d = din("metaT", [D, NMETA])
    wst_d = din("wstream", [DEPTH, NWT, 128, 1024])
    wuq_d = din("wuq", [DEPTH, 128, 2 * 768])
    wkn_d = din("wkn", [DEPTH, 128, 512])
    wv_d = din("wv", [DEPTH, 128, 512])
    vecs_d = din("vecs", [DEPTH, 128, NVEC])
    wq_d = din("wq", [DEPTH, 128, 8 * 2048])
    keys_d = din("keysT", [DEPTH, 128, 16 * 128])
    pu_d = [din("peer_u%d" % l, [NEXP, D]) for l in range(DEPTH)]
    pv_d = [din("peer_v%d" % l, [NEXP, D]) for l in range(DEPTH)]
    cos_d = din("cosT", [96, L])
    sin_d = din("sinT", [96, L])
    cst_d = din("consts", [128, 128 + 96 + 16])
    out_d = nc.dram_tensor("outT", [D, SEQ], F32, kind="ExternalOutput").ap()

    with contextlib.ExitStack() as st0:
        S = Sched(nc, st0)
        A0 = Alloc(nc, st0)

        x_t, _ = A0.sb([128, 8, L], F32, "x")
        xb = [[Buf("x%d_%d" % (b, c)) for c in range(8)] for b in range(1 + NRB)]
        cst_t, cst_b = A0.sb([128, 240], F32, "cst")
        identf = cst_t[:, 0:128]
        iota16 = cst_t[:, 224:240]
        identb_t, identb_b = A0.sb([128, 128], BF16, "identb")
        prot_t, prot_b = A0.sb([128, 96], BF16, "prot")
        ones_t, ones_b = A0.sb([128, 128], BF16, "ones")
        eps_t, eps_b = A0.sb([128, 1], F32, "eps")
        vecs_t = []
        vecs_b = []
        for l in range(DEPTH):
            t, b = A0.sb([128, NVEC], F32, "vecs")
            vecs_t.append(t)
            vecs_b.append(b)
        pst = []
        psb = []
        for i in range(8):
            t, b = A0.ps([128, 512], F32, "bank")
            pst.append(t)
            psb.append(b)
        psrr = [0]

        def ps_next():
            i = 2 + psrr[0] % 6
            psrr[0] += 1
            return pst[i], psb[i]

        S.dma("sp", cst_t[:], cst_d, W=[cst_b])
        for l in range(DEPTH):
            S.dma("sp", vecs_t[l][:], vecs_d[l], W=[vecs_b[l]])
        for c in range(8):
            S.dma("sp", x_t[:, c, 0:NMETA], meta_d[c * 128:(c + 1) * 128, :], W=[xb[0][c]])
        for c in range(8):
            wl = [xb[1 + b][c] for b in range(NRB)]
            S.dma("sp", x_t[:, c, NMETA:L], xT_d[c * 128:(c + 1) * 128, :], W=wl, sembuf=wl[0])
        S.op("pool", lambda e: e.memset(ones_t[:], 1.0), W=[ones_b])
        S.op("pool", lambda e: e.memset(eps_t[:], EPS), W=[eps_b])
        S.op("dve", lambda e: e.tensor_copy(out=identb_t[:], in_=identf), R=[cst_b], W=[identb_b])
        S.op("dve", lambda e: e.tensor_copy(out=prot_t[:], in_=cst_t[:, 128:224]), R=[cst_b], W=[prot_b])

        blocks = [(0, NMETA)] + [(NMETA + NB * b, NB) for b in range(NRB)]
        ktiles = [(0, NMETA)] + [(NMETA + 128 * m, 128) for m in range(SEQ // 128)]

        def rms_feature(A, src_fn, src_bufs, nchunk, npart, n, dim, gcol, vt, vb, dst_fn, dst_bufs,
                        sq_t, sq_b, ln_t, ln_b, rs_t, rs_b):
            pt, pb = ps_next()
            for c in range(nchunk):
                S.op("act", lambda e, c=c: e.activation(out=sq_t[0:npart, c % 2, 0:n], in_=src_fn(c),
                                                        func=AF.Square),
                     R=[src_bufs[c]], W=[sq_b[c % 2]])
                S.op("pe", lambda e, c=c: e.matmul(pt[0:npart, 0:n], lhsT=ones_t[0:npart, 0:npart],
                                                   rhs=sq_t[0:npart, c % 2, 0:n],
                                                   start=(c == 0), stop=(c == nchunk - 1)),
                     R=[ones_b, sq_b[c % 2]], W=[pb])
            S.op("act", lambda e: e.activation(out=ln_t[0:npart, 0:n], in_=pt[0:npart, 0:n], func=AF.Ln,
                                               bias=eps_t[0:npart, :], scale=1.0 / dim),
                 R=[pb, eps_b], W=[ln_b])
            S.op("act", lambda e: e.activation(out=rs_t[0:npart, 0:n], in_=ln_t[0:npart, 0:n], func=AF.Exp,
                                               scale=-0.5),
                 R=[ln_b], W=[rs_b])
            for c in range(nchunk):
                S.op("dve", lambda e, c=c: e.scalar_tensor_tensor(
                    out=dst_fn(c), in0=src_fn(c), scalar=vt[0:npart, gcol + c:gcol + c + 1],
                    in1=rs_t[0:npart, 0:n], op0=ALU.mult, op1=ALU.mult),
                    R=[src_bufs[c], vb, rs_b], W=[dst_bufs[c]])

        for l in range(n_layers):
            vt = vecs_t[l]
            vb = vecs_b[l]
            if do_mixer:
              with contextlib.ExitStack() as stm:
                A = A0.sub(stm)
                kst_t, _ = A.sb([128, 8, L], BF16, "kst")
                kstb = [[Buf("k%d_%d" % (b, h)) for h in range(8)] for b in range(1 + NRB)]
                vst_t, _ = A.sb([128, NKT, 512], BF16, "vst")
                vstb = [Buf("v%d" % i) for i in range(NKT)]
                ring_t, _ = A.sb([128, NSLOT, 1024], BF16, "ring")
                ringb = [Buf("ring%d" % i) for i in range(NSLOT)]
                wuq_t, wuq_b = A.sb([128, 2 * 768], BF16, "wuq")
                wkn_t, wkn_b = A.sb([128, 512], BF16, "wkn")
                wv_t, wv_b = A.sb([128, 512], BF16, "wv")
                cos_t, cos_b = A.sb([128, NB], F32, "cos")
                sin_t, sin_b = A.sb([128, NB], F32, "sin")
                hT_t, _ = A.sb([128, 8, NB], BF16, "hT")
                hTb = [Buf("hT%d" % c) for c in range(8)]
                sq_t, _ = A.sb([128, 2, NB], BF16, "sq")
                sqb = [Buf("sq0"), Buf("sq1")]
                ln_t, ln_b = A.sb([128, NB], F32, "ln")
                rs_t, rs_b = A.sb([128, NB], F32, "rs")
                sig_t, _ = A.sb([128, 2, NB], F32, "sig")
                sigb = [Buf("sig0"), Buf("sig1")]
                ub_t, _ = A.sb([128, 4, 30 + NB], F32, "ubuf")
                ubb = [Buf("ub%d" % c) for c in range(4)]
                halo_t, _ = A.sb([128, 4, 30], F32, "halo")
                halob = [Buf("halo%d" % c) for c in range(4)]
                y_t, _ = A.sb([128, 4, NB], F32, "y")
                yb = [Buf("y%d" % c) for c in range(4)]
                yh_t, _ = A.sb([128, 4, NB], BF16, "yh")
                yhb = [Buf("yh%d" % c) for c in range(4)]
                ysq_t, _ = A.sb([128, 4, NB], BF16, "ysq")
                ysqb = [Buf("ysq%d" % c) for c in range(4)]
                mu_t, mu_b = A.sb([128, NB], F32, "mu")
                var_t, var_b = A.sb([128, NB], F32, "var")
                actc_t, _ = A.sb([128, 4, NB], BF16, "actc")
                actcb = [Buf("actc%d" % c) for c in range(4)]
                cq_t, _ = A.sb([128, 2, NB], F32, "cq")
                cqb = [Buf("cq0"), Buf("cq1")]
                cqn_t, _ = A.sb([128, 2, NB], BF16, "cqn")
                cqnb = [Buf("cqn0"), Buf("cqn1")]
                ckv_t, ckv_b = A.sb([128, NB], F32, "ckv")
                ckvn_t, ckvn_b = A.sb([128, NB], BF16, "ckvn")
                hc_t, _ = A.sb([128, 2, NB], F32, "hc")
                hcb = [Buf("hc0"), Buf("hc1")]
                hn_t, _ = A.sb([128, 2, NB], F32, "hn")
                hnb = [Buf("hn0"), Buf("hn1")]
                hnh_t, _ = A.sb([128, 2, NB], BF16, "hnh")
                hnhb = [Buf("hnh0"), Buf("hnh1")]
                t1_t, _ = A.sb([128, 2, NB], F32, "t1")
                t1b = [Buf("t1_0"), Buf("t1_1")]
                t2_t, _ = A.sb([128, 2, NB], F32, "t2")
                t2b = [Buf("t2_0"), Buf("t2_1")]
                q_t, _ = A.sb([128, 8, NB], BF16, "qblk")
                qb = [Buf("q%d" % h) for h in range(8)]
                pT_t, _ = A.sb([128, 4, NB], BF16, "pT")
                pTb = [Buf("pT%d" % i) for i in range(4)]
                rden_t, _ = A.sb([64, 2, NB], F32, "rden")
                rdenb = [Buf("rden0"), Buf("rden1")]
                oT_t, _ = A.sb([64, 8, NB], BF16, "oT")
                oTb = [Buf("oT%d" % h) for h in range(8)]
                m1_t, _ = A.sb([128, 2, NB], F32, "m1")
                m1b = [Buf("m1_0"), Buf("m1_1")]
                mg_t, _ = A.sb([128, 8, NB], BF16, "merged")
                mgb = [Buf("mg%d" % c) for c in range(8)]

                S.dma("pool", wuq_t[:], wuq_d[l], W=[wuq_b])
                S.dma("pool", wkn_t[:], wkn_d[l], W=[wkn_b])
                S.dma("pool", wv_t[:], wv_d[l], W=[wv_b])
                for c in range(4):
                    S.op("pool", lambda e, c=c: e.memset(halo_t[:, c, :], 0.0), W=[halob[c]])

                wstate = {"issued": 0, "used": 0}
                total_tiles = NWT * len(blocks)

                def w_issue():
                    i = wstate["issued"]
                    if i >= total_tiles:
                        return
                    s = i % NSLOT
                    S.dma("pool", ring_t[:, s, :], wst_d[l, i % NWT], W=[ringb[s]])
                    wstate["issued"] += 1

                def w_take():
                    i = wstate["used"]
                    wstate["used"] += 1
                    s = i % NSLOT
                    return ring_t[:, s, :], ringb[s]

                for _ in range(NSLOT):
                    w_issue()

                for bi, (c0, n) in enumerate(blocks):
                    xs = lambda c: x_t[:, c, c0:c0 + n]
                    S.dma("sp", cos_t[64:96, 0:n], cos_d[64:96, c0:c0 + n], W=[cos_b])
                    S.dma("sp", sin_t[64:96, 0:n], sin_d[64:96, c0:c0 + n], W=[sin_b])
                    rms_feature(A, xs, xb[bi], 8, 128, n, float(D), V_MIXG, vt, vb,
                                lambda c: hT_t[:, c, 0:n], hTb, sq_t, sqb, ln_t, ln_b, rs_t, rs_b)

                    def zproj(M):
                        wt, wb = w_take()
                        pt, pb = ps_next()
                        for kc in range(8):
                            S.op("pe", lambda e, kc=kc: e.matmul(pt[0:M, 0:n], lhsT=wt[:, kc * 128:kc * 128 + M],
                                                                 rhs=hT_t[:, kc, 0:n],
                                                                 start=(kc == 0), stop=(kc == 7)),
                                 R=[wb, hTb[kc]], W=[pb])
                        w_issue()
                        return pt, pb

                    for c in range(4):
                        pa, pab = zproj(128)
                        pg, pgb = zproj(128)
                        sg = c % 2
                        S.op("act", lambda e: e.activation(out=sig_t[:, sg, 0:n], in_=pg[:, 0:n], func=AF.Sigmoid),
                             R=[pgb], W=[sigb[sg]])
                        S.op("pool", lambda e, c=c: e.tensor_copy(out=ub_t[:, c, 0:30], in_=halo_t[:, c, :]),
                             R=[halob[c]], W=[ubb[c]])
                        S.op("dve", lambda e, c=c: e.tensor_tensor(out=ub_t[:, c, 30:30 + n], in0=pa[:, 0:n],
                                                                   in1=sig_t[:, sg, 0:n], op=ALU.mult),
                             R=[pab, sigb[sg]], W=[ubb[c]])
                        S.op("pool", lambda e, c=c: e.tensor_copy(out=halo_t[:, c, :], in_=ub_t[:, c, n:n + 30]),
                             R=[ubb[c]], W=[halob[c]])
                    for c in range(2):
                        pq, pqb = zproj(128)
                        S.op("act", lambda e, c=c: e.activation(out=cq_t[:, c, 0:n], in_=pq[:, 0:n], func=AF.Copy),
                             R=[pqb], W=[cqb[c]])
                    pk, pkb = zproj(128)
                    S.op("act", lambda e: e.activation(out=ckv_t[:, 0:n], in_=pk[:, 0:n], func=AF.Copy),
                         R=[pkb], W=[ckv_b])
                    pr, prb = zproj(96)
                    kr_t, kr_b = ln_t, ln_b
                    if "krope" not in wstate:
                        wstate["krope"] = A.sb([128, NB], F32, "krope")
                    kr_t, kr_b = wstate["krope"]
                    S.op("act", lambda e: e.activation(out=kr_t[64:96, 0:n], in_=pr[64:96, 0:n], func=AF.Copy),
                         R=[prb], W=[kr_b])
                    rms_feature(A, lambda c: cq_t[:, c, 0:n], cqb, 2, 128, n, 256.0, V_QAG, vt, vb,
                                lambda c: cqn_t[:, c, 0:n], cqnb, sq_t, sqb, ln_t, ln_b, rs_t, rs_b)
                    rms_feature(A, lambda c: ckv_t[:, 0:n], [ckv_b], 1, 128, n, 128.0, V_KVAG, vt, vb,
                                lambda c: ckvn_t[:, 0:n], [ckvn_b], sq_t, sqb, ln_t, ln_b, rs_t, rs_b)

                    for c in range(4):
                        S.op("dve", lambda e, c=c: e.tensor_scalar(
                            out=y_t[:, c, 0:n], in0=ub_t[:, c, 0:n],
                            scalar1=vt[:, V_CONVW + c * 31:V_CONVW + c * 31 + 1],
                            scalar2=vt[:, V_CONVB + c:V_CONVB + c + 1], op0=ALU.mult, op1=ALU.add),
                            R=[ubb[c], vb], W=[yb[c]])
                        for k in range(1, 31):
                            S.op("dve", lambda e, c=c, k=k: e.scalar_tensor_tensor(
                                out=y_t[:, c, 0:n], in0=ub_t[:, c, k:k + n],
                                scalar=vt[:, V_CONVW + c * 31 + k:V_CONVW + c * 31 + k + 1],
                                in1=y_t[:, c, 0:n], op0=ALU.mult, op1=ALU.add),
                                R=[ubb[c], vb, yb[c]], W=[yb[c]])
                    p1, p1b = ps_next()
                    p2, p2b = ps_next()
                    for c in range(4):
                        S.op("act", lambda e, c=c: e.activation(out=yh_t[:, c, 0:n], in_=y_t[:, c, 0:n], func=AF.Copy),
                             R=[yb[c]], W=[yhb[c]])
                        S.op("act", lambda e, c=c: e.activation(out=ysq_t[:, c, 0:n], in_=y_t[:, c, 0:n], func=AF.Square),
                             R=[yb[c]], W=[ysqb[c]])
                    for c in range(4):
                        S.op("pe", lambda e, c=c: e.matmul(p1[:, 0:n], lhsT=ones_t[:], rhs=yh_t[:, c, 0:n],
                                                           start=(c == 0), stop=(c == 3)),
                             R=[ones_b, yhb[c]], W=[p1b])
                    for c in range(4):
                        S.op("pe", lambda e, c=c: e.matmul(p2[:, 0:n], lhsT=ones_t[:], rhs=ysq_t[:, c, 0:n],
                                                           start=(c == 0), stop=(c == 3)),
                             R=[ones_b, ysqb[c]], W=[p2b])
                    S.op("dve", lambda e: e.tensor_scalar(out=mu_t[:, 0:n], in0=p1[:, 0:n], scalar1=1.0 / 512,
                                                          scalar2=None, op0=ALU.mult),
                         R=[p1b], W=[mu_b])
                    S.op("dve", lambda e: e.tensor_tensor(out=var_t[:, 0:n], in0=mu_t[:, 0:n], in1=mu_t[:, 0:n],
                                                          op=ALU.mult),
                         R=[mu_b], W=[var_b])
                    S.op("dve", lambda e: e.scalar_tensor_tensor(out=var_t[:, 0:n], in0=p2[:, 0:n], scalar=1.0 / 512,
                                                                 in1=var_t[:, 0:n], op0=ALU.mult, op1=ALU.subtract),
                         R=[p2b, var_b], W=[var_b])
                    S.op("act", lambda e: e.activation(out=ln_t[:, 0:n], in_=var_t[:, 0:n], func=AF.Ln,
                                                       bias=eps_t[:, :], scale=1.0),
                         R=[var_b, eps_b], W=[ln_b])
                    S.op("act", lambda e: e.activation(out=rs_t[:, 0:n], in_=ln_t[:, 0:n], func=AF.Exp, scale=-0.5),
                         R=[ln_b], W=[rs_b])
                    for c in range(4):
                        S.op("dve", lambda e, c=c: e.tensor_tensor(out=y_t[:, c, 0:n], in0=y_t[:, c, 0:n],
                                                                   in1=mu_t[:, 0:n], op=ALU.subtract),
                             R=[yb[c], mu_b], W=[yb[c]])
                        S.op("dve", lambda e, c=c: e.tensor_tensor(out=y_t[:, c, 0:n], in0=y_t[:, c, 0:n],
                                                                   in1=rs_t[:, 0:n], op=ALU.mult),
                             R=[yb[c], rs_b], W=[yb[c]])
                        S.op("act", lambda e, c=c: e.activation(
                            out=actc_t[:, c, 0:n], in_=y_t[:, c, 0:n], func=AF.Silu,
                            bias=vt[:, V_LNB + c:V_LNB + c + 1], scale=vt[:, V_LNG + c:V_LNG + c + 1]),
                            R=[yb[c], vb], W=[actcb[c]])

                    bkt = [0] if bi == 0 else [1 + 2 * (bi - 1), 2 + 2 * (bi - 1)]
                    for j, kt in enumerate(bkt):
                        kc0, nk = ktiles[kt]
                        off = kc0 - c0
                        pv_, pvb_ = ps_next()
                        S.op("pe", lambda e: e.matmul(pv_[0:nk, 0:512], lhsT=ckvn_t[:, off:off + nk], rhs=wv_t[:],
                                                      start=True, stop=True),
                             R=[ckvn_b, wv_b], W=[pvb_])
                        S.op("act", lambda e: e.activation(out=vst_t[0:nk, kt, :], in_=pv_[0:nk, 0:512], func=AF.Copy),
                             R=[pvb_], W=[vstb[kt]])

                    def head_norm_rope(src_ps, src_pb, gcol, dst_fn, dst_buf, hb, extra_fn=None):
                        if extra_fn is None:
                            S.op("act", lambda e: e.activation(out=hc_t[0:96, hb, 0:n], in_=src_ps[0:96, 0:n],
                                                               func=AF.Copy),
                                 R=[src_pb], W=[hcb[hb]])
                        else:
                            S.op("act", lambda e: e.activation(out=hc_t[0:64, hb, 0:n], in_=src_ps[0:64, 0:n],
                                                               func=AF.Copy),
                                 R=[src_pb], W=[hcb[hb]])
                            extra_fn(hb)
                        pt, pb = ps_next()
                        S.op("act", lambda e: e.activation(out=sq_t[0:96, hb, 0:n], in_=hc_t[0:96, hb, 0:n],
                                                           func=AF.Square),
                             R=[hcb[hb]], W=[sqb[hb]])
                        S.op("pe", lambda e: e.matmul(pt[0:96, 0:n], lhsT=ones_t[0:96, 0:96], rhs=sq_t[0:96, hb, 0:n],
                                                      start=True, stop=True),
                             R=[ones_b, sqb[hb]], W=[pb])
                        S.op("act", lambda e: e.activation(out=t1_t[0:96, hb, 0:n], in_=pt[0:96, 0:n], func=AF.Ln,
                                                           bias=eps_t[0:96, :], scale=1.0 / 96),
                             R=[pb, eps_b], W=[t1b[hb]])
                        S.op("act", lambda e: e.activation(out=t2_t[0:96, hb, 0:n], in_=t1_t[0:96, hb, 0:n],
                                                           func=AF.Exp, scale=-0.5),
                             R=[t1b[hb]], W=[t2b[hb]])
                        S.op("dve", lambda e: e.scalar_tensor_tensor(
                            out=hn_t[0:96, hb, 0:n], in0=hc_t[0:96, hb, 0:n], scalar=vt[0:96, gcol:gcol + 1],
                            in1=t2_t[0:96, hb, 0:n], op0=ALU.mult, op1=ALU.mult),
                            R=[hcb[hb], vb, t2b[hb]], W=[hnb[hb]])
                        S.op("act", lambda e: e.activation(out=dst_fn(0, 64), in_=hn_t[0:64, hb, 0:n], func=AF.Copy),
                             R=[hnb[hb]], W=[dst_buf])
                        S.op("dve", lambda e: e.tensor_copy(out=hnh_t[64:96, hb, 0:n], in_=hn_t[64:96, hb, 0:n]),
                             R=[hnb[hb]], W=[hnhb[hb]])
                        pr2, pr2b = ps_next()
                        S.op("pe", lambda e: e.matmul(pr2[0:96, 0:n], lhsT=prot_t[64:96, 0:96],
                                                      rhs=hnh_t[64:96, hb, 0:n], start=True, stop=True),
                             R=[prot_b, hnhb[hb]], W=[pr2b])
                        S.op("dve", lambda e: e.tensor_tensor(out=t1_t[64:96, hb, 0:n], in0=hn_t[64:96, hb, 0:n],
                                                              in1=cos_t[64:96, 0:n], op=ALU.mult),
                             R=[hnb[hb], cos_b, t1b[hb]], W=[t1b[hb]])
                        S.op("dve", lambda e: e.tensor_tensor(out=t2_t[64:96, hb, 0:n], in0=pr2[64:96, 0:n],
                                                              in1=sin_t[64:96, 0:n], op=ALU.mult),
                             R=[pr2b, sin_b, t2b[hb]], W=[t2b[hb]])
                        S.op("dve", lambda e: e.tensor_tensor(out=dst_fn(64, 96), in0=t1_t[64:96, hb, 0:n],
                                                              in1=t2_t[64:96, hb, 0:n], op=ALU.add),
                             R=[t1b[hb], t2b[hb]], W=[dst_buf])

                    for h in range(8):
                        hb = h % 2
                        pkh, pkhb = ps_next()
                        S.op("pe", lambda e, h=h: e.matmul(pkh[0:64, 0:n], lhsT=wkn_t[:, h * 64:(h + 1) * 64],
                                                           rhs=ckvn_t[:, 0:n], start=True, stop=True),
                             R=[wkn_b, ckvn_b], W=[pkhb])

                        def add_rope(hb_):
                            S.op("pool", lambda e: e.tensor_copy(out=hc_t[64:96, hb_, 0:n], in_=kr_t[64:96, 0:n]),
                                 R=[kr_b], W=[hcb[hb_]])
                        head_norm_rope(pkh, pkhb, V_KNG,
                                       lambda a, b_, h=h: kst_t[a:b_, h, c0:c0 + n], kstb[bi][h], hb, add_rope)
                        pqh, pqhb = ps_next()
                        for kc in range(2):
                            S.op("pe", lambda e, h=h, kc=kc: e.matmul(
                                pqh[0:96, 0:n], lhsT=wuq_t[:, kc * 768 + h * 96:kc * 768 + (h + 1) * 96],
                                rhs=cqn_t[:, kc, 0:n], start=(kc == 0), stop=(kc == 1)),
                                R=[wuq_b, cqnb[kc]], W=[pqhb])
                        head_norm_rope(pqh, pqhb, V_QNG,
                                       lambda a, b_, h=h: q_t[a:b_, h, 0:n], qb[h], hb)

                    if bi == 0:
                        vis = [(0, 0, False)]
                    else:
                        B = bi - 1
                        vis = [(0, 0, False)] + [(1 + m, 0, False) for m in range(2 * B)]
                        vis += [(1 + 2 * B + j, 128 * j, True) for j in range(2)]
                    pti = [0]
                    for h in range(8):
                        pnum, pnumb = pst[0], psb[0]
                        pden, pdenb = pst[1], psb[1]
                        pend = None
                        for vi in range(len(vis) + 1):
                            cur = None
                            if vi < len(vis):
                                kt, q0, diag = vis[vi]
                                kc0, nk = ktiles[kt]
                                kbi = 0 if kt == 0 else 1 + (kt - 1) // 2
                                sps, spsb = ps_next()
                                S.op("pe", lambda e: e.matmul(sps[0:nk, q0:n], lhsT=kst_t[0:96, h, kc0:kc0 + nk],
                                                              rhs=q_t[0:96, h, q0:n], start=True, stop=True),
                                     R=[kstb[kbi][h], qb[h]], W=[spsb])
                                pi = pti[0] % 4
                                pti[0] += 1
                                S.op("act", lambda e: e.activation(out=pT_t[0:nk, pi, q0:n], in_=sps[0:nk, q0:n],
                                                                   func=AF.Exp, scale=SCALE),
                                     R=[spsb], W=[pTb[pi]])
                                if diag:
                                    S.op("pool", lambda e: e.memset(pT_t[64:128, pi, q0:q0 + 64], 0.0), W=[pTb[pi]])
                                cur = (kt, q0, nk, pi, vi)
                            if pend is not None:
                                kt_, q0_, nk_, pi_, vi_ = pend
                                first = (vi_ == 0)
                                last = (vi_ == len(vis) - 1)
                                S.op("pe", lambda e: e.matmul(pnum[0:64, q0_:n], lhsT=vst_t[0:nk_, kt_, h * 64:(h + 1) * 64],
                                                              rhs=pT_t[0:nk_, pi_, q0_:n], start=first, stop=last),
                                     R=[vstb[kt_], pTb[pi_]], W=[pnumb])
                                S.op("pe", lambda e: e.matmul(pden[0:64, q0_:n], lhsT=ones_t[0:nk_, 0:64],
                                                              rhs=pT_t[0:nk_, pi_, q0_:n], start=first, stop=last),
                                     R=[ones_b, pTb[pi_]], W=[pdenb])
                            pend = cur
                        rb = h % 2
                        S.op("dve", lambda e: e.reciprocal(out=rden_t[0:64, rb, 0:n], in_=pden[0:64, 0:n]),
                             R=[pdenb], W=[rdenb[rb]])
                        S.op("dve", lambda e, h=h: e.tensor_tensor(out=oT_t[0:64, h, 0:n], in0=pnum[0:64, 0:n],
                                                                   in1=rden_t[0:64, rb, 0:n], op=ALU.mult),
                             R=[pnumb, rdenb[rb]], W=[oTb[h]])

                    for c in range(8):
                        pg1, pg1b = zproj(128)
                        S.op("act", lambda e: e.activation(out=sig_t[:, 0, 0:n], in_=pg1[:, 0:n], func=AF.Sigmoid),
                             R=[pg1b], W=[sigb[0]])
                        wt, wb = w_take()
                        pc, pcb = ps_next()
                        for kc in range(4):
                            S.op("pe", lambda e, kc=kc: e.matmul(pc[:, 0:n], lhsT=wt[:, kc * 128:(kc + 1) * 128],
                                                                 rhs=actc_t[:, kc, 0:n], start=(kc == 0), stop=(kc == 3)),
                                 R=[wb, actcb[kc]], W=[pcb])
                        w_issue()
                        S.op("dve", lambda e: e.tensor_tensor(out=m1_t[:, 0, 0:n], in0=pc[:, 0:n], in1=sig_t[:, 0, 0:n],
                                                              op=ALU.mult),
                             R=[pcb, sigb[0]], W=[m1b[0]])
                        pg2, pg2b = zproj(128)
                        S.op("act", lambda e: e.activation(out=sig_t[:, 1, 0:n], in_=pg2[:, 0:n], func=AF.Sigmoid),
                             R=[pg2b], W=[sigb[1]])
                        wt2, wb2 = w_take()
                        pm, pmb = ps_next()
                        for h in range(8):
                            S.op("pe", lambda e, h=h: e.matmul(pm[:, 0:n], lhsT=wt2[0:64, h * 128:(h + 1) * 128],
                                                               rhs=oT_t[0:64, h, 0:n], start=(h == 0), stop=(h == 7)),
                                 R=[wb2, oTb[h]], W=[pmb])
                        w_issue()
                        S.op("dve", lambda e: e.tensor_tensor(out=m1_t[:, 1, 0:n], in0=pm[:, 0:n], in1=sig_t[:, 1, 0:n],
                                                              op=ALU.mult),
                             R=[pmb, sigb[1]], W=[m1b[1]])
                        S.op("dve", lambda e, c=c: e.tensor_tensor(out=mg_t[:, c, 0:n], in0=m1_t[:, 0, 0:n],
                                                                   in1=m1_t[:, 1, 0:n], op=ALU.add),
                             R=[m1b[0], m1b[1]], W=[mgb[c]])
                    for c in range(8):
                        wt, wb = w_take()
                        po, pob = ps_next()
                        for kc in range(8):
                            S.op("pe", lambda e, kc=kc: e.matmul(po[:, 0:n], lhsT=wt[:, kc * 128:(kc + 1) * 128],
                                                                 rhs=mg_t[:, kc, 0:n], start=(kc == 0), stop=(kc == 7)),
                                 R=[wb, mgb[kc]], W=[pob])
                        w_issue()
                        S.op("dve", lambda e, c=c: e.tensor_tensor(out=x_t[:, c, c0:c0 + n], in0=x_t[:, c, c0:c0 + n],
                                                                   in1=po[:, 0:n], op=ALU.add),
                             R=[pob, xb[bi][c]], W=[xb[bi][c]])
                S.barrier()

            if do_peer:
              with contextlib.ExitStack() as stp:
                A = A0.sub(stp)
                wq_t, wq_b = A.sb([128, 8 * 2048], BF16, "wq")
                ky_t, ky_b = A.sb([128, 16 * 128], BF16, "keysT")
                hT_t, _ = A.sb([128, 8, 128], BF16, "h2T")
                hTb = [Buf("h2T%d" % c) for c in range(8)]
                sq_t, _ = A.sb([128, 2, 128], BF16, "sq")
                sqb = [Buf("sq0"), Buf("sq1")]
                ln_t, ln_b = A.sb([128, 128], F32, "ln")
                rs_t, rs_b = A.sb([128, 128], F32, "rs")
                htok_t, htok_b = A.sb([128, D], BF16, "h2tok")
                qT_t, _ = A.sb([128, 16, 128], BF16, "qT")
                qTb = [Buf("qT%d" % g) for g in range(4)]
                s_t, s_b = A.sb([128, 16, 128], F32, "s")
                s2_t, s2_b = A.sb([128, 16, 128], F32, "s2")
                sv_t, sv_b = A.sb([128, 16, 16], F32, "sv")
                si_t, si_b = A.sb([128, 16, 16], U32, "si")
                sif_t, sif_b = A.sb([128, 16, 16], F32, "sif")
                cand_t, cand_b = A.sb([128, 8, 256], F32, "cand")
                cand2_t, cand2_b = A.sb([128, 8, 256], F32, "cand2")
                top_t, top_b = A.sb([128, 8, 16], F32, "top")
                pos_t, pos_b = A.sb([128, 8, 16], U32, "pos")
                ai_t, ai_b = A.sb([128, 8, 16], U32, "ai")
                bi_t, bi_b = A.sb([128, 8, 16], U32, "bi")
                af_t, af_b = A.sb([128, 8, 16], F32, "af")
                bf_t, bf_b = A.sb([128, 8, 16], F32, "bf")
                oh_t, oh_b = A.sb([128, 8, 256], F32, "oh")
                tmp_t, tmp_b = A.sb([128, 8, 256], F32, "tmp")
                isel_t, isel_b = A.sb([128, 128], F32, "isel")
                jsel_t, jsel_b = A.sb([128, 128], F32, "jsel")
                eidf_t, eidf_b = A.sb([128, 128], F32, "eidf")
                eid_t, eid_b = A.sb([128, 128], I32, "eid")
                ew_t, ew_b = A.sb([128, 128], F32, "ew")
                ssum_t, ssum_b = A.sb([128, 8], F32, "ssum")
                gw_t, gw_b = A.sb([128, 128], F32, "gw")
                a_t, a_b = A.sb([128, 128], F32, "a")
                g1_t, g1_b = A.sb([128, 128], F32, "g1")
                g2_t, g2_b = A.sb([128, 128], F32, "g2")
                actw_t, actw_b = A.sb([128, 128], F32, "actw")
                junk_t, junk_b = A.sb([128, D], BF16, "junk")
                acc_t, acc_b = A.sb([128, D], F32, "acc")
                gb_t, _ = A.sb([128, NG, D], F32, "gbuf")
                gbb = [Buf("gb%d" % i) for i in range(NG)]

                posf_t, posf_b = A.sb([128, 8, 16], F32, "posf")
                thr_t, thr_b = A.sb([128, 16], F32, "thr")
                S.op("dve", lambda e: e.tensor_scalar(out=thr_t[:], in0=iota16, scalar1=16.0, scalar2=16.0,
                                                      op0=ALU.mult, op1=ALU.add), R=[cst_b], W=[thr_b])
                S.dma("pool", wq_t[:], wq_d[l], W=[wq_b])
                S.dma("pool", ky_t[:], keys_d[l], W=[ky_b])
                gi = [0]
                ptiles = list(range(NKT))
                if l == n_layers - 1:
                    ptiles = ptiles[1:]
                for tt in ptiles:
                    c0, np_ = ktiles[tt]
                    bi = 0 if tt == 0 else 1 + (tt - 1) // 2
                    xs = lambda c: x_t[:, c, c0:c0 + np_]
                    rms_feature(A, xs, xb[bi], 8, 128, np_, float(D), V_FFNG, vt, vb,
                                lambda c: hT_t[:, c, 0:np_], hTb, sq_t, sqb, ln_t, ln_b, rs_t, rs_b)
                    ptk = pst[0][:, :].bitcast(BF16)
                    for c in range(8):
                        S.op("pe", lambda e, c=c: e.transpose(out=ptk[0:np_, c * 128:(c + 1) * 128],
                                                              in_=hT_t[:, c, 0:np_], identity=identb_t[:]),
                             R=[hTb[c], identb_b], W=[psb[0]])
                    S.op("act", lambda e: e.activation(out=htok_t[0:np_, :], in_=ptk[0:np_, :], func=AF.Copy),
                         R=[psb[0]], W=[htok_b])
                    for gq in range(4):
                        pq, pqb = ps_next()
                        for gg in range(4):
                            g = gq * 4 + gg
                            for kc in range(8):
                                S.op("pe", lambda e, g=g, gg=gg, kc=kc: e.matmul(
                                    pq[:, gg * 128:gg * 128 + np_],
                                    lhsT=wq_t[:, kc * 2048 + g * 128:kc * 2048 + (g + 1) * 128],
                                    rhs=hT_t[:, kc, 0:np_], start=(kc == 0), stop=(kc == 7)),
                                    R=[wq_b, hTb[kc]], W=[pqb])
                        S.op("act", lambda e, gq=gq: e.activation(
                            out=qT_t[:, gq * 4:(gq + 1) * 4, 0:np_],
                            in_=pq[:, :].rearrange("p (g t) -> p g t", g=4)[:, :, 0:np_], func=AF.Copy),
                            R=[pqb], W=[qTb[gq]])
                    for gq in range(4):
                        pss, pssb = ps_next()
                        for gg in range(4):
                            g = gq * 4 + gg
                            S.op("pe", lambda e, g=g, gg=gg: e.matmul(
                                pss[0:np_, gg * 128:(gg + 1) * 128], lhsT=qT_t[:, g, 0:np_],
                                rhs=ky_t[:, g * 128:(g + 1) * 128], start=True, stop=True),
                                R=[qTb[gq], ky_b], W=[pssb])
                        S.op("act", lambda e, gq=gq: e.activation(
                            out=s_t[0:np_, gq * 4:(gq + 1) * 4, :],
                            in_=pss[0:np_, :].rearrange("p (g n) -> p g n", g=4), func=AF.Copy),
                            R=[pssb], W=[s_b])
                    P = np_
                    for g in range(16):
                        S.op("dve", lambda e, g=g: e.max(out=sv_t[0:P, g, 0:8], in_=s_t[0:P, g, :]), R=[s_b], W=[sv_b])
                    for g in range(16):
                        S.op("dve", lambda e, g=g: e.max_index(out=si_t[0:P, g, 0:8], in_max=sv_t[0:P, g, 0:8],
                                                               in_values=s_t[0:P, g, :]), R=[s_b, sv_b], W=[si_b])
                    for g in range(16):
                        S.op("dve", lambda e, g=g: e.match_replace(out=s2_t[0:P, g, :], in_to_replace=sv_t[0:P, g, 0:8],
                                                                   in_values=s_t[0:P, g, :], imm_value=-1e30),
                             R=[s_b, sv_b], W=[s2_b])
                    for g in range(16):
                        S.op("dve", lambda e, g=g: e.max(out=sv_t[0:P, g, 8:16], in_=s2_t[0:P, g, :]), R=[s2_b], W=[sv_b])
                    for g in range(16):
                        S.op("dve", lambda e, g=g: e.max_index(out=si_t[0:P, g, 8:16], in_max=sv_t[0:P, g, 8:16],
                                                               in_values=s2_t[0:P, g, :]), R=[s2_b, sv_b], W=[si_b])
                    S.op("dve", lambda e: e.tensor_copy(out=sif_t[0:P], in_=si_t[0:P]), R=[si_b], W=[sif_b])
                    cand4 = cand_t[0:P].rearrange("p h (a b) -> p h a b", a=16)
                    S.op("dve", lambda e: e.tensor_tensor(
                        out=cand4, in0=sv_t[0:P, 0::2, :].unsqueeze(3).broadcast_to([P, 8, 16, 16]),
                        in1=sv_t[0:P, 1::2, :].unsqueeze(2).broadcast_to([P, 8, 16, 16]), op=ALU.add),
                        R=[sv_b], W=[cand_b])
                    for h in range(8):
                        S.op("dve", lambda e, h=h: e.max(out=top_t[0:P, h, 0:8], in_=cand_t[0:P, h, :]), R=[cand_b], W=[top_b])
                    for h in range(8):
                        S.op("dve", lambda e, h=h: e.max_index(out=pos_t[0:P, h, 0:8], in_max=top_t[0:P, h, 0:8],
                                                               in_values=cand_t[0:P, h, :]), R=[cand_b, top_b], W=[pos_b])
                    for h in range(8):
                        S.op("dve", lambda e, h=h: e.match_replace(out=cand2_t[0:P, h, :], in_to_replace=top_t[0:P, h, 0:8],
                                                                   in_values=cand_t[0:P, h, :], imm_value=-1e30),
                             R=[cand_b, top_b], W=[cand2_b])
                    for h in range(8):
                        S.op("dve", lambda e, h=h: e.max(out=top_t[0:P, h, 8:16], in_=cand2_t[0:P, h, :]), R=[cand2_b], W=[top_b])
                    for h in range(8):
                        S.op("dve", lambda e, h=h: e.max_index(out=pos_t[0:P, h, 8:16], in_max=top_t[0:P, h, 8:16],
                                                               in_values=cand2_t[0:P, h, :]), R=[cand2_b, top_b], W=[pos_b])
                    oh4 = oh_t[0:P].rearrange("p h (k a) -> p h k a", k=16)
                    S.op("dve", lambda e: e.tensor_copy(out=posf_t[0:P], in_=pos_t[0:P]), R=[pos_b], W=[posf_b])
                    S.op("dve", lambda e: e.tensor_tensor(
                        out=oh4, in0=posf_t[0:P].unsqueeze(3).broadcast_to([P, 8, 16, 16]),
                        in1=thr_t[0:P, :].unsqueeze(1).unsqueeze(1).broadcast_to([P, 8, 16, 16]), op=ALU.is_ge),
                        R=[posf_b, thr_b], W=[oh_b])
                    S.op("dve", lambda e: e.tensor_reduce(
                        out=af_t[0:P].rearrange("p h k -> p (h k)"),
                        in_=oh_t[0:P].rearrange("p h (k a) -> p (h k) a", k=16), axis=AX.X, op=ALU.add),
                        R=[oh_b], W=[af_b])
                    S.op("dve", lambda e: e.scalar_tensor_tensor(
                        out=bf_t[0:P].rearrange("p h k -> p (h k)"), in0=af_t[0:P].rearrange("p h k -> p (h k)"),
                        scalar=-16.0, in1=posf_t[0:P].rearrange("p h k -> p (h k)"), op0=ALU.mult, op1=ALU.add),
                        R=[af_b, posf_b], W=[bf_b])
                    oh4 = oh_t[0:P].rearrange("p h (k a) -> p h k a", k=16)
                    tmp4 = tmp_t[0:P].rearrange("p h (k a) -> p h k a", k=16)
                    io4 = iota16[0:P, :].unsqueeze(1).unsqueeze(1).broadcast_to([P, 8, 16, 16])
                    for (xf_t, xf_b, par, dst_t, dst_b) in ((af_t, af_b, 0, isel_t, isel_b),
                                                            (bf_t, bf_b, 1, jsel_t, jsel_b)):
                        S.op("dve", lambda e, xf_t=xf_t: e.tensor_tensor(
                            out=oh4, in0=xf_t[0:P].unsqueeze(3).broadcast_to([P, 8, 16, 16]), in1=io4,
                            op=ALU.is_equal), R=[xf_b, cst_b], W=[oh_b])
                        S.op("dve", lambda e, par=par: e.tensor_tensor(
                            out=tmp4, in0=oh4,
                            in1=sif_t[0:P, par::2, :].unsqueeze(2).broadcast_to([P, 8, 16, 16]), op=ALU.mult),
                            R=[oh_b, sif_b], W=[tmp_b])
                        S.op("dve", lambda e, dst_t=dst_t: e.tensor_reduce(
                            out=dst_t[0:P, :], in_=tmp_t[0:P].rearrange("p h (k a) -> p (h k) a", k=16),
                            axis=AX.X, op=ALU.add), R=[tmp_b], W=[dst_b])
                    S.op("dve", lambda e: e.scalar_tensor_tensor(out=eidf_t[0:P, :], in0=isel_t[0:P, :], scalar=128.0,
                                                                 in1=jsel_t[0:P, :], op0=ALU.mult, op1=ALU.add),
                         R=[isel_b, jsel_b], W=[eidf_b])
                    S.op("dve", lambda e: e.tensor_copy(out=eid_t[0:P, :], in_=eidf_t[0:P, :]), R=[eidf_b], W=[eid_b])
                    ew3 = ew_t[0:P, :].rearrange("p (h k) -> p h k", h=8)
                    S.op("dve", lambda e: e.tensor_tensor(out=ew3, in0=top_t[0:P],
                                                          in1=top_t[0:P, :, 0:1].broadcast_to([P, 8, 16]),
                                                          op=ALU.subtract), R=[top_b], W=[ew_b])
                    S.op("act", lambda e: e.activation(out=ew_t[0:P, :], in_=ew_t[0:P, :], func=AF.Exp),
                         R=[ew_b], W=[ew_b])
                    S.op("dve", lambda e: e.tensor_reduce(out=ssum_t[0:P, :], in_=ew3, axis=AX.X, op=ALU.add),
                         R=[ew_b], W=[ssum_b])
                    S.op("dve", lambda e: e.reciprocal(out=ssum_t[0:P, :], in_=ssum_t[0:P, :]), R=[ssum_b], W=[ssum_b])
                    S.op("dve", lambda e: e.tensor_tensor(out=gw_t[0:P, :].rearrange("p (h k) -> p h k", h=8), in0=ew3,
                                                          in1=ssum_t[0:P, :].unsqueeze(2).broadcast_to([P, 8, 16]),
                                                          op=ALU.mult), R=[ew_b, ssum_b], W=[gw_b])
                    for k in range(128):
                        gbi = gi[0] % NG
                        gi[0] += 1
                        S.dma("pool", gb_t[0:P, gbi, :], pu_d[l],
                              R=[eid_b], W=[gbb[gbi]],
                              indirect=bass.IndirectOffsetOnAxis(ap=eid_t[0:P, k:k + 1], axis=0))
                        S.op("dve", lambda e, k=k, gbi=gbi: e.scalar_tensor_tensor(
                            out=junk_t[0:P, :], in0=gb_t[0:P, gbi, :], scalar=1.0, in1=htok_t[0:P, :],
                            op0=ALU.mult, op1=ALU.mult, accum_out=a_t[0:P, k:k + 1]),
                            R=[gbb[gbi], htok_b], W=[junk_b, a_b])
                    S.op("dve", lambda e: e.tensor_tensor(out=g1_t[0:P, :], in0=a_t[0:P, :], in1=a_t[0:P, :], op=ALU.mult),
                         R=[a_b], W=[g1_b])
                    S.op("dve", lambda e: e.tensor_scalar(out=g1_t[0:P, :], in0=g1_t[0:P, :], scalar1=0.044715, scalar2=1.0,
                                                          op0=ALU.mult, op1=ALU.add), R=[g1_b], W=[g1_b])
                    S.op("dve", lambda e: e.tensor_tensor(out=g1_t[0:P, :], in0=g1_t[0:P, :], in1=a_t[0:P, :], op=ALU.mult),
                         R=[g1_b, a_b], W=[g1_b])
                    S.op("act", lambda e: e.activation(out=g2_t[0:P, :], in_=g1_t[0:P, :], func=AF.Sigmoid,
                                                       scale=1.5957691216057308), R=[g1_b], W=[g2_b])
                    S.op("dve", lambda e: e.tensor_tensor(out=g2_t[0:P, :], in0=g2_t[0:P, :], in1=a_t[0:P, :], op=ALU.mult),
                         R=[g2_b, a_b], W=[g2_b])
                    S.op("dve", lambda e: e.tensor_tensor(out=actw_t[0:P, :], in0=g2_t[0:P, :], in1=gw_t[0:P, :], op=ALU.mult),
                         R=[g2_b, gw_b], W=[actw_b])
                    for k in range(128):
                        gbi = gi[0] % NG
                        gi[0] += 1
                        S.dma("pool", gb_t[0:P, gbi, :], pv_d[l],
                              R=[eid_b], W=[gbb[gbi]],
                              indirect=bass.IndirectOffsetOnAxis(ap=eid_t[0:P, k:k + 1], axis=0))
                        if k == 0:
                            S.op("dve", lambda e, gbi=gbi: e.tensor_scalar(
                                out=acc_t[0:P, :], in0=gb_t[0:P, gbi, :], scalar1=actw_t[0:P, 0:1], scalar2=None,
                                op0=ALU.mult), R=[gbb[gbi], actw_b], W=[acc_b])
                        else:
                            S.op("dve", lambda e, k=k, gbi=gbi: e.scalar_tensor_tensor(
                                out=acc_t[0:P, :], in0=gb_t[0:P, gbi, :], scalar=actw_t[0:P, k:k + 1], in1=acc_t[0:P, :],
                                op0=ALU.mult, op1=ALU.add), R=[gbb[gbi], actw_b, acc_b], W=[acc_b])
                    for c in range(8):
                        ptr, ptrb = ps_next()
                        S.op("pe", lambda e, c=c: e.transpose(out=ptr[:, 0:P], in_=acc_t[0:P, c * 128:(c + 1) * 128],
                                                              identity=identf[0:P, 0:P]),
                             R=[acc_b, cst_b], W=[ptrb])
                        S.op("dve", lambda e, c=c: e.tensor_tensor(out=x_t[:, c, c0:c0 + P], in0=x_t[:, c, c0:c0 + P],
                                                                   in1=ptr[:, 0:P], op=ALU.add),
                             R=[ptrb, xb[bi][c]], W=[xb[bi][c]])
                S.barrier()

        outb = Buf("out")
        for c in range(8):
            rl = [xb[1 + b][c] for b in range(NRB)]
            S.dma("sp", out_d[c * 128:(c + 1) * 128, :], x_t[:, c, NMETA:L], R=rl, sembuf=outb)
        S.wait_all("sp", [outb] + [xb[1 + b][c] for b in range(NRB) for c in range(8)])
        nc._stats = (S.nins, S.nwait, S.ndsem, dict(S.cnt))
    return nc


def _wstream(w_in, w_conv_out, w_mla_out, w_out):
    tiles = []

    def ztile(c0, m):
        t = np.zeros((128, 8, 128), np.float32)
        t[:, :, :m] = w_in[:, c0:c0 + m].reshape(8, 128, m).transpose(1, 0, 2)
        tiles.append(t.reshape(128, 1024))

    for c in range(4):
        ztile(128 * c, 128)
        ztile(512 + 128 * c, 128)
    ztile(1024, 128)
    ztile(1152, 128)
    ztile(1280, 128)
    ztile(1344, 96)
    for c in range(8):
        ztile(1440 + 128 * c, 128)
        t = np.zeros((128, 8, 128), np.float32)
        t[:, 0:4, :] = w_conv_out[:, 128 * c:128 * (c + 1)].reshape(4, 128, 128).transpose(1, 0, 2)
        tiles.append(t.reshape(128, 1024))
        ztile(2464 + 128 * c, 128)
        t = np.zeros((128, 8, 128), np.float32)
        t[0:64, :, :] = w_mla_out[:, 128 * c:128 * (c + 1)].reshape(8, 64, 128).transpose(1, 0, 2)
        tiles.append(t.reshape(128, 1024))
    for c in range(8):
        t = w_out[:, 128 * c:128 * (c + 1)].reshape(8, 128, 128).transpose(1, 0, 2)
        tiles.append(np.ascontiguousarray(t).reshape(128, 1024))
    assert len(tiles) == NWT
    return np.stack(tiles)


def _prep_shared(inp):
    f = lambda a: np.ascontiguousarray(np.asarray(a, dtype=np.float32))
    sh = {}
    sh["wstream"] = np.stack([_wstream(f(inp["w_in"][l]), f(inp["w_conv_out"][l]), f(inp["w_mla_out"][l]),
                                       f(inp["w_out"][l])) for l in range(DEPTH)])
    sh["wuq"] = f(np.stack([f(inp["w_uq"][l]).reshape(2, 128, 768).transpose(1, 0, 2).reshape(128, 1536)
                            for l in range(DEPTH)]))
    wukv = f(inp["w_ukv"]).reshape(DEPTH, 128, 8, 128)
    sh["wkn"] = f(wukv[:, :, :, 0:64].reshape(DEPTH, 128, 512))
    sh["wv"] = f(wukv[:, :, :, 64:128].reshape(DEPTH, 128, 512))
    vecs = np.zeros((DEPTH, 128, NVEC), np.float32)
    for l in range(DEPTH):
        vecs[l, :, V_MIXG:V_MIXG + 8] = f(inp["mix_norm_g"][l]).reshape(8, 128).T
        vecs[l, :, V_FFNG:V_FFNG + 8] = f(inp["ffn_norm_g"][l]).reshape(8, 128).T
        cw = f(inp["conv_w"][l]).reshape(31, 4, 128)
        vecs[l, :, V_CONVW:V_CONVW + 124] = cw.transpose(2, 1, 0).reshape(128, 124)
        vecs[l, :, V_CONVB:V_CONVB + 4] = f(inp["conv_b"][l]).reshape(4, 128).T
        vecs[l, :, V_LNG:V_LNG + 4] = f(inp["conv_ln_g"][l]).reshape(4, 128).T
        vecs[l, :, V_LNB:V_LNB + 4] = f(inp["conv_ln_b"][l]).reshape(4, 128).T
        vecs[l, :, V_QAG:V_QAG + 2] = f(inp["q_a_norm_g"][l]).reshape(2, 128).T
        vecs[l, :, V_KVAG] = f(inp["kv_a_norm_g"][l])
        vecs[l, 0:96, V_QNG] = f(inp["q_norm_g"][l])
        vecs[l, 0:96, V_KNG] = f(inp["k_norm_g"][l])
    sh["vecs"] = vecs
    sh["wq"] = f(np.stack([f(inp["peer_wq"][l]).reshape(8, 128, 2048).transpose(1, 0, 2).reshape(128, 8 * 2048)
                           for l in range(DEPTH)]))
    ky = f(inp["peer_keys"]).reshape(DEPTH, 16, 128, 128)
    sh["keysT"] = f(ky.transpose(0, 3, 1, 2).reshape(DEPTH, 128, 16 * 128))
    for l in range(DEPTH):
        sh["peer_u%d" % l] = f(inp["peer_u"][l])
        sh["peer_v%d" % l] = f(inp["peer_v"][l])
    pos = np.arange(L, dtype=np.float32)
    inv = (1.0 / (np.float32(10000.0) ** (np.arange(0, 32, 2, dtype=np.float32) / np.float32(32)))).astype(np.float32)
    ang = pos[:, None] * inv[None, :]
    ang = np.concatenate([ang, ang], axis=-1)
    cosT = np.zeros((96, L), np.float32)
    sinT = np.zeros((96, L), np.float32)
    cosT[64:96] = np.cos(ang).T
    sinT[64:96] = np.sin(ang).T
    sh["cosT"] = cosT
    sh["sinT"] = sinT
    cst = np.zeros((128, 240), np.float32)
    cst[:, 0:128] = np.eye(128, dtype=np.float32)
    prot = np.zeros((128, 96), np.float32)
    for m in range(16):
        prot[64 + m + 16, 64 + m] = -1.0
        prot[64 + m, 64 + m + 16] = 1.0
    cst[:, 128:224] = prot
    cst[:, 224:240] = np.arange(16, dtype=np.float32)[None, :]
    sh["consts"] = cst
    sh["metaT"] = f(f(inp["meta_tokens"]).T)
    return sh


_CACHE = {}


def kernel(**inputs):
    x = np.asarray(inputs["x"], dtype=np.float32)
    nb = x.shape[0]
    sh = _prep_shared(inputs)
    if "nc" not in _CACHE:
        _CACHE["nc"] = build_program()
    nc = _CACHE["nc"]
    in_maps = []
    for b in range(nb):
        m = dict(sh)
        m["xT"] = np.ascontiguousarray(x[b].T)
        in_maps.append(m)
    res = run_bass_kernel_spmd(nc, in_maps, core_ids=list(range(nb)))
    out = np.stack([np.asarray(r["outT"], dtype=np.float32).T for r in res.results], axis=0)
    return np.ascontiguousarray(out)
```

```python
import contextlib
import numpy as np
import concourse.bass as bass
import concourse.mybir as mybir
from concourse.bass_utils import run_bass_kernel_spmd

F32 = mybir.dt.float32
BF16 = mybir.dt.bfloat16
I32 = mybir.dt.int32
U32 = mybir.dt.uint32
ALU = mybir.AluOpType
AF = mybir.ActivationFunctionType
AX = mybir.AxisListType

D = 1024
SEQ = 2048
NMETA = 16
L = SEQ + NMETA
DEPTH = 2
NB = 256
NRB = SEQ // NB
NSLOT = 6
NWT = 52
NKT = 1 + SEQ // 128
NEXP = 16384
NG = 8
NVEC = 160
V_MIXG, V_FFNG, V_CONVW, V_CONVB, V_LNG, V_LNB, V_QAG, V_KVAG, V_QNG, V_KNG = (
    0, 8, 16, 140, 144, 148, 152, 154, 155, 156)
EPS = 1e-6
SCALE = 96.0 ** -0.5


class Buf:
    __slots__ = ("name", "w", "r", "dsem", "dval")

    def __init__(self, name):
        self.name = name
        self.w = None
        self.r = {}
        self.dsem = None
        self.dval = 0


class Sched:
    def __init__(self, nc, stack):
        self.nc = nc
        self.stack = stack
        self.eng = {"pe": nc.tensor, "act": nc.scalar, "dve": nc.vector,
                    "pool": nc.gpsimd, "sp": nc.sync}
        self.sems = {}
        self.cnt = {}
        self.known = {}
        self.snap = {}
        for e in self.eng:
            self.sems[e] = stack.enter_context(nc.semaphore("s_" + e))
            self.cnt[e] = 0
            self.known[e] = {}
            self.snap[e] = {}
        self.ndsem = 0
        self.nwait = 0
        self.nins = 0

    def new_dsem(self):
        k = "d%d" % self.ndsem
        self.ndsem += 1
        self.sems[k] = self.stack.enter_context(self.nc.semaphore("s_" + k))
        return k

    def _need(self, e, deps, key, val):
        if self.known[e].get(key, 0) >= val:
            return
        if deps.get(key, 0) < val:
            deps[key] = val

    def _collect(self, e, R, W):
        deps = {}
        for b in R:
            if b.w is not None:
                k, v = b.w
                if k == e and e == "pe":
                    continue
                self._need(e, deps, k, v)
        for b in W:
            if b.w is not None:
                k, v = b.w
                if k != e:
                    self._need(e, deps, k, v)
            for k, v in b.r.items():
                if k != e:
                    self._need(e, deps, k, v)
        return deps

    def _emit_waits(self, e, deps):
        for k, v in deps.items():
            self.eng[e].wait_ge(self.sems[k], v)
            self.nwait += 1
            kn = self.known[e]
            if kn.get(k, 0) < v:
                kn[k] = v
            sn = self.snap.get(k, {}).get(v)
            if sn:
                for kk, vv in sn.items():
                    if kn.get(kk, 0) < vv:
                        kn[kk] = vv

    def op(self, e, fn, R=(), W=()):
        deps = self._collect(e, R, W)
        self._emit_waits(e, deps)
        ins = fn(self.eng[e])
        self.cnt[e] += 1
        n = self.cnt[e]
        ins.then_inc(self.sems[e], 1)
        self.nins += 1
        self.snap[e][n] = dict(self.known[e])
        for b in R:
            b.r[e] = n
        for b in W:
            b.w = (e, n)
            b.r = {}
        return ins

    def dma(self, q, out, in_, R=(), W=(), sembuf=None, indirect=None):
        deps = self._collect(q, R, W)
        self._emit_waits(q, deps)
        sb = sembuf if sembuf is not None else (W[0] if W else R[0])
        if sb.dsem is None:
            sb.dsem = self.new_dsem()
        if indirect is not None:
            ins = self.eng[q].indirect_dma_start(out=out, out_offset=None, in_=in_,
                                                 in_offset=indirect)
        else:
            ins = self.eng[q].dma_start(out=out, in_=in_)
        sb.dval += 16
        ins.then_inc(self.sems[sb.dsem], 16)
        tok = (sb.dsem, sb.dval)
        for b in R:
            b.r[tok[0]] = tok[1]
        for b in W:
            b.w = tok
            b.r = {}
        return tok

    def wait_all(self, e, bufs):
        deps = {}
        for b in bufs:
            if b.w is not None:
                self._need(e, deps, b.w[0], b.w[1])
            for k, v in b.r.items():
                self._need(e, deps, k, v)
        self._emit_waits(e, deps)

    def barrier(self, bufs=()):
        for e in self.eng:
            deps = {}
            for f in self.eng:
                if f != e and self.cnt[f] > 0:
                    self._need(e, deps, f, self.cnt[f])
            for b in bufs:
                if b.w is not None:
                    self._need(e, deps, b.w[0], b.w[1])
                for k, v in b.r.items():
                    self._need(e, deps, k, v)
            self._emit_waits(e, deps)


class Alloc:
    def __init__(self, nc, stack):
        self.nc = nc
        self.stack = stack
        self.n = [0]

    def sub(self, stack):
        a = Alloc(self.nc, stack)
        a.n = self.n
        return a

    def sb(self, shape, dt, name="t"):
        self.n[0] += 1
        nm = "%s_%d" % (name, self.n[0])
        t = self.stack.enter_context(self.nc.sbuf_tensor(nm, list(shape), dt))
        return t, Buf(nm)

    def ps(self, shape, dt, name="p"):
        self.n[0] += 1
        nm = "%s_%d" % (name, self.n[0])
        t = self.stack.enter_context(self.nc.psum_tensor(nm, list(shape), dt))
        return t, Buf(nm)


def build_program(n_layers=DEPTH, do_mixer=True, do_peer=True):
    nc = bass.Bass("TRN2", target_bir_lowering=False)

    def din(name, shape, dt=F32):
        return nc.dram_tensor(name, list(shape), dt, kind="ExternalInput").ap()

    xT_d = din("xT", [D, SEQ])
    meta_d = din("metaT", [D, NMETA])
    wst_d = din("wstream", [DEPTH, NWT, 128, 1024])
    wuq_d = din("wuq", [DEPTH, 128, 2 * 768])
    wkn_d = din("wkn", [DEPTH, 128, 512])
    wv_d = din("wv", [DEPTH, 128, 512])
    vecs_d = din("vecs", [DEPTH, 128, NVEC])
    wq_d = din("wq", [DEPTH, 128, 8 * 2048])
    keys_d = din("keysT", [DEPTH, 128, 16 * 128])
    pu_d = [din("peer_u%d" % l, [NEXP, D]) for l in range(DEPTH)]
    pv_d = [din("peer_v%d" % l, [NEXP, D]) for l in range(DEPTH)]
    ubf_d = [nc.dram_tensor("ubf%d" % l, [NEXP, D], BF16).ap() for l in range(DEPTH)]
    vbf_d = [nc.dram_tensor("vbf%d" % l, [NEXP, D], BF16).ap() for l in range(DEPTH)]
    cos_d = din("cosT", [96, L])
    sin_d = din("sinT", [96, L])
    cst_d = din("consts", [128, 128 + 96 + 16])
    out_d = nc.dram_tensor("outT", [D, SEQ], F32, kind="ExternalOutput").ap()

    with contextlib.ExitStack() as st0:
        S = Sched(nc, st0)
        A0 = Alloc(nc, st0)

        x_t, _ = A0.sb([128, 8, L], F32, "x")
        xb = [[Buf("x%d_%d" % (b, c)) for c in range(8)] for b in range(1 + NRB)]
        cst_t, cst_b = A0.sb([128, 240], F32, "cst")
        identf = cst_t[:, 0:128]
        iota16 = cst_t[:, 224:240]
        identb_t, identb_b = A0.sb([128, 128], BF16, "identb")
        prot_t, prot_b = A0.sb([128, 96], BF16, "prot")
        ones_t, ones_b = A0.sb([128, 128], BF16, "ones")
        eps_t, eps_b = A0.sb([128, 1], F32, "eps")
        vecs_t = []
        vecs_b = []
        for l in range(DEPTH):
            t, b = A0.sb([128, NVEC], F32, "vecs")
            vecs_t.append(t)
            vecs_b.append(b)
        pst = []
        psb = []
        for i in range(8):
            t, b = A0.ps([128, 512], F32, "bank")
            pst.append(t)
            psb.append(b)
        psrr = [0]

        def ps_next():
            i = 2 + psrr[0] % 6
            psrr[0] += 1
            return pst[i], psb[i]

        S.dma("sp", cst_t[:], cst_d, W=[cst_b])
        for l in range(DEPTH):
            S.dma("sp", vecs_t[l][:], vecs_d[l], W=[vecs_b[l]])
        for c in range(8):
            S.dma("sp", x_t[:, c, 0:NMETA], meta_d[c * 128:(c + 1) * 128, :], W=[xb[0][c]])
        for c in range(8):
            wl = [xb[1 + b][c] for b in range(NRB)]
            S.dma("sp", x_t[:, c, NMETA:L], xT_d[c * 128:(c + 1) * 128, :], W=wl, sembuf=wl[0])
        S.op("pool", lambda e: e.memset(ones_t[:], 1.0), W=[ones_b])
        S.op("pool", lambda e: e.memset(eps_t[:], EPS), W=[eps_b])
        S.op("dve", lambda e: e.tensor_copy(out=identb_t[:], in_=identf), R=[cst_b], W=[identb_b])
        S.op("dve", lambda e: e.tensor_copy(out=prot_t[:], in_=cst_t[:, 128:224]), R=[cst_b], W=[prot_b])

        blocks = [(0, NMETA)] + [(NMETA + NB * b, NB) for b in range(NRB)]
        ktiles = [(0, NMETA)] + [(NMETA + 128 * m, 128) for m in range(SEQ // 128)]

        def rms_feature(A, src_fn, src_bufs, nchunk, npart, n, dim, gcol, vt, vb, dst_fn, dst_bufs,
                        sq_t, sq_b, ln_t, ln_b, rs_t, rs_b):
            pt, pb = ps_next()
            for c in range(nchunk):
                S.op("act", lambda e, c=c: e.activation(out=sq_t[0:npart, c % 2, 0:n], in_=src_fn(c),
                                                        func=AF.Square),
                     R=[src_bufs[c]], W=[sq_b[c % 2]])
                S.op("pe", lambda e, c=c: e.matmul(pt[0:npart, 0:n], lhsT=ones_t[0:npart, 0:npart],
                                                   rhs=sq_t[0:npart, c % 2, 0:n],
                                                   start=(c == 0), stop=(c == nchunk - 1)),
                     R=[ones_b, sq_b[c % 2]], W=[pb])
            S.op("act", lambda e: e.activation(out=ln_t[0:npart, 0:n], in_=pt[0:npart, 0:n], func=AF.Ln,
                                               bias=eps_t[0:npart, :], scale=1.0 / dim),
                 R=[pb, eps_b], W=[ln_b])
            S.op("act", lambda e: e.activation(out=rs_t[0:npart, 0:n], in_=ln_t[0:npart, 0:n], func=AF.Exp,
                                               scale=-0.5),
                 R=[ln_b], W=[rs_b])
            for c in range(nchunk):
                S.op("dve", lambda e, c=c: e.scalar_tensor_tensor(
                    out=dst_fn(c), in0=src_fn(c), scalar=vt[0:npart, gcol + c:gcol + c + 1],
                    in1=rs_t[0:npart, 0:n], op0=ALU.mult, op1=ALU.mult),
                    R=[src_bufs[c], vb, rs_b], W=[dst_bufs[c]])

        for l in range(n_layers):
            vt = vecs_t[l]
            vb = vecs_b[l]
            cstate = {"i": 0, "stg": None}
            NCV = 2 * (NEXP // 256)

            def conv_step():
                i = cstate["i"]
                if i >= NCV:
                    return
                cstate["i"] += 1
                stg_t, stgb = cstate["stg"]
                src = (pu_d[l], pv_d[l])[i % 2]
                dst = (ubf_d[l], vbf_d[l])[i % 2]
                r0 = (i // 2) * 256
                sb_ = i % 2
                S.dma("pool", stg_t[:, sb_, :, :], src[r0:r0 + 256, :].rearrange("(p j) d -> p j d", j=2),
                      W=[stgb[sb_]])
                S.dma("sp", dst[r0:r0 + 256, :].rearrange("(p j) d -> p j d", j=2), stg_t[:, sb_, :, :],
                      R=[stgb[sb_]], sembuf=stgb[sb_])
            if do_mixer:
              with contextlib.ExitStack() as stm:
                A = A0.sub(stm)
                kst_t, _ = A.sb([128, 8, L], BF16, "kst")
                kstb = [[Buf("k%d_%d" % (b, h)) for h in range(8)] for b in range(1 + NRB)]
                vst_t, _ = A.sb([128, NKT, 512], BF16, "vst")
                vstb = [Buf("v%d" % i) for i in range(NKT)]
                ring_t, _ = A.sb([128, NSLOT, 1024], BF16, "ring")
                ringb = [Buf("ring%d" % i) for i in range(NSLOT)]
                wuq_t, wuq_b = A.sb([128, 2 * 768], BF16, "wuq")
                wkn_t, wkn_b = A.sb([128, 512], BF16, "wkn")
                wv_t, wv_b = A.sb([128, 512], BF16, "wv")
                cos_t, cos_b = A.sb([128, NB], F32, "cos")
                sin_t, sin_b = A.sb([128, NB], F32, "sin")
                hT_t, _ = A.sb([128, 8, NB], BF16, "hT")
                hTb = [Buf("hT%d" % c) for c in range(8)]
                sq_t, _ = A.sb([128, 2, NB], BF16, "sq")
                sqb = [Buf("sq0"), Buf("sq1")]
                ln_t, ln_b = A.sb([128, NB], F32, "ln")
                rs_t, rs_b = A.sb([128, NB], F32, "rs")
                sig_t, _ = A.sb([128, 2, NB], F32, "sig")
                sigb = [Buf("sig0"), Buf("sig1")]
                ub_t, _ = A.sb([128, 4, 30 + NB], F32, "ubuf")
                ubb = [Buf("ub%d" % c) for c in range(4)]
                halo_t, _ = A.sb([128, 4, 30], F32, "halo")
                halob = [Buf("halo%d" % c) for c in range(4)]
                y_t, _ = A.sb([128, 4, NB], F32, "y")
                yb = [Buf("y%d" % c) for c in range(4)]
                yh_t, _ = A.sb([128, 4, NB], BF16, "yh")
                yhb = [Buf("yh%d" % c) for c in range(4)]
                ysq_t, _ = A.sb([128, 4, NB], BF16, "ysq")
                ysqb = [Buf("ysq%d" % c) for c in range(4)]
                mu_t, mu_b = A.sb([128, NB], F32, "mu")
                var_t, var_b = A.sb([128, NB], F32, "var")
                actc_t, _ = A.sb([128, 4, NB], BF16, "actc")
                actcb = [Buf("actc%d" % c) for c in range(4)]
                cq_t, _ = A.sb([128, 2, NB], F32, "cq")
                cqb = [Buf("cq0"), Buf("cq1")]
                cqn_t, _ = A.sb([128, 2, NB], BF16, "cqn")
                cqnb = [Buf("cqn0"), Buf("cqn1")]
                ckv_t, ckv_b = A.sb([128, NB], F32, "ckv")
                ckvn_t, ckvn_b = A.sb([128, NB], BF16, "ckvn")
                hc_t, _ = A.sb([128, 2, NB], F32, "hc")
                hcb = [Buf("hc0"), Buf("hc1")]
                hn_t, _ = A.sb([128, 2, NB], F32, "hn")
                hnb = [Buf("hn0"), Buf("hn1")]
                hnh_t, _ = A.sb([128, 2, NB], BF16, "hnh")
                hnhb = [Buf("hnh0"), Buf("hnh1")]
                t1_t, _ = A.sb([128, 2, NB], F32, "t1")
                t1b = [Buf("t1_0"), Buf("t1_1")]
                t2_t, _ = A.sb([128, 2, NB], F32, "t2")
                t2b = [Buf("t2_0"), Buf("t2_1")]
                q_t, _ = A.sb([128, 8, NB], BF16, "qblk")
                qb = [Buf("q%d" % h) for h in range(8)]
                pT_t, _ = A.sb([128, 4, NB], BF16, "pT")
                pTb = [Buf("pT%d" % i) for i in range(4)]
                rden_t, _ = A.sb([64, 2, NB], F32, "rden")
                rdenb = [Buf("rden0"), Buf("rden1")]
                oT_t, _ = A.sb([64, 8, NB], BF16, "oT")
                oTb = [Buf("oT%d" % h) for h in range(8)]
                m1_t, _ = A.sb([128, 2, NB], F32, "m1")
                m1b = [Buf("m1_0"), Buf("m1_1")]
                mg_t, _ = A.sb([128, 8, NB], BF16, "merged")
                mgb = [Buf("mg%d" % c) for c in range(8)]

                stg_t, _ = A.sb([128, 2, 2, D], BF16, "stg")
                stgb_m = [Buf("stg0"), Buf("stg1")]
                cstate["stg"] = (stg_t, stgb_m)
                S.dma("pool", wuq_t[:], wuq_d[l], W=[wuq_b])
                S.dma("pool", wkn_t[:], wkn_d[l], W=[wkn_b])
                S.dma("pool", wv_t[:], wv_d[l], W=[wv_b])
                for c in range(4):
                    S.op("pool", lambda e, c=c: e.memset(halo_t[:, c, :], 0.0), W=[halob[c]])

                wstate = {"issued": 0, "used": 0}
                total_tiles = NWT * len(blocks)

                def w_issue():
                    i = wstate["issued"]
                    if i >= total_tiles:
                        return
                    s = i % NSLOT
                    S.dma("pool", ring_t[:, s, :], wst_d[l, i % NWT], W=[ringb[s]])
                    wstate["issued"] += 1

                def w_take():
                    i = wstate["used"]
                    wstate["used"] += 1
                    s = i % NSLOT
                    return ring_t[:, s, :], ringb[s]

                for _ in range(NSLOT):
                    w_issue()

                for bi, (c0, n) in enumerate(blocks):
                    xs = lambda c: x_t[:, c, c0:c0 + n]
                    S.dma("sp", cos_t[64:96, 0:n], cos_d[64:96, c0:c0 + n], W=[cos_b])
                    S.dma("sp", sin_t[64:96, 0:n], sin_d[64:96, c0:c0 + n], W=[sin_b])
                    rms_feature(A, xs, xb[bi], 8, 128, n, float(D), V_MIXG, vt, vb,
                                lambda c: hT_t[:, c, 0:n], hTb, sq_t, sqb, ln_t, ln_b, rs_t, rs_b)

                    def zproj(M):
                        wt, wb = w_take()
                        pt, pb = ps_next()
                        for kc in range(8):
                            S.op("pe", lambda e, kc=kc: e.matmul(pt[0:M, 0:n], lhsT=wt[:, kc * 128:kc * 128 + M],
                                                                 rhs=hT_t[:, kc, 0:n],
                                                                 start=(kc == 0), stop=(kc == 7)),
                                 R=[wb, hTb[kc]], W=[pb])
                        w_issue()
                        if wstate["used"] % NWT < 16:
                            conv_step()
                        return pt, pb

                    for c in range(4):
                        pa, pab = zproj(128)
                        pg, pgb = zproj(128)
                        sg = c % 2
                        S.op("act", lambda e: e.activation(out=sig_t[:, sg, 0:n], in_=pg[:, 0:n], func=AF.Sigmoid),
                             R=[pgb], W=[sigb[sg]])
                        S.op("pool", lambda e, c=c: e.tensor_copy(out=ub_t[:, c, 0:30], in_=halo_t[:, c, :]),
                             R=[halob[c]], W=[ubb[c]])
                        S.op("dve", lambda e, c=c: e.tensor_tensor(out=ub_t[:, c, 30:30 + n], in0=pa[:, 0:n],
                                                                   in1=sig_t[:, sg, 0:n], op=ALU.mult),
                             R=[pab, sigb[sg]], W=[ubb[c]])
                        S.op("pool", lambda e, c=c: e.tensor_copy(out=halo_t[:, c, :], in_=ub_t[:, c, n:n + 30]),
                             R=[ubb[c]], W=[halob[c]])
                    for c in range(2):
                        pq, pqb = zproj(128)
                        S.op("act", lambda e, c=c: e.activation(out=cq_t[:, c, 0:n], in_=pq[:, 0:n], func=AF.Copy),
                             R=[pqb], W=[cqb[c]])
                    pk, pkb = zproj(128)
                    S.op("act", lambda e: e.activation(out=ckv_t[:, 0:n], in_=pk[:, 0:n], func=AF.Copy),
                         R=[pkb], W=[ckv_b])
                    pr, prb = zproj(96)
                    kr_t, kr_b = ln_t, ln_b
                    if "krope" not in wstate:
                        wstate["krope"] = A.sb([128, NB], F32, "krope")
                    kr_t, kr_b = wstate["krope"]
                    S.op("act", lambda e: e.activation(out=kr_t[64:96, 0:n], in_=pr[64:96, 0:n], func=AF.Copy),
                         R=[prb], W=[kr_b])
                    rms_feature(A, lambda c: cq_t[:, c, 0:n], cqb, 2, 128, n, 256.0, V_QAG, vt, vb,
                                lambda c: cqn_t[:, c, 0:n], cqnb, sq_t, sqb, ln_t, ln_b, rs_t, rs_b)
                    rms_feature(A, lambda c: ckv_t[:, 0:n], [ckv_b], 1, 128, n, 128.0, V_KVAG, vt, vb,
                                lambda c: ckvn_t[:, 0:n], [ckvn_b], sq_t, sqb, ln_t, ln_b, rs_t, rs_b)

                    for c in range(4):
                        S.op("dve", lambda e, c=c: e.tensor_scalar(
                            out=y_t[:, c, 0:n], in0=ub_t[:, c, 0:n],
                            scalar1=vt[:, V_CONVW + c * 31:V_CONVW + c * 31 + 1],
                            scalar2=vt[:, V_CONVB + c:V_CONVB + c + 1], op0=ALU.mult, op1=ALU.add),
                            R=[ubb[c], vb], W=[yb[c]])
                        for k in range(1, 31):
                            S.op("dve", lambda e, c=c, k=k: e.scalar_tensor_tensor(
                                out=y_t[:, c, 0:n], in0=ub_t[:, c, k:k + n],
                                scalar=vt[:, V_CONVW + c * 31 + k:V_CONVW + c * 31 + k + 1],
                                in1=y_t[:, c, 0:n], op0=ALU.mult, op1=ALU.add),
                                R=[ubb[c], vb, yb[c]], W=[yb[c]])
                    p1, p1b = ps_next()
                    p2, p2b = ps_next()
                    for c in range(4):
                        S.op("act", lambda e, c=c: e.activation(out=yh_t[:, c, 0:n], in_=y_t[:, c, 0:n], func=AF.Copy),
                             R=[yb[c]], W=[yhb[c]])
                        S.op("act", lambda e, c=c: e.activation(out=ysq_t[:, c, 0:n], in_=y_t[:, c, 0:n], func=AF.Square),
                             R=[yb[c]], W=[ysqb[c]])
                    for c in range(4):
                        S.op("pe", lambda e, c=c: e.matmul(p1[:, 0:n], lhsT=ones_t[:], rhs=yh_t[:, c, 0:n],
                                                           start=(c == 0), stop=(c == 3)),
                             R=[ones_b, yhb[c]], W=[p1b])
                    for c in range(4):
                        S.op("pe", lambda e, c=c: e.matmul(p2[:, 0:n], lhsT=ones_t[:], rhs=ysq_t[:, c, 0:n],
                                                           start=(c == 0), stop=(c == 3)),
                             R=[ones_b, ysqb[c]], W=[p2b])
                    S.op("dve", lambda e: e.tensor_scalar(out=mu_t[:, 0:n], in0=p1[:, 0:n], scalar1=1.0 / 512,
                                                          scalar2=None, op0=ALU.mult),
                         R=[p1b], W=[mu_b])
                    S.op("dve", lambda e: e.tensor_tensor(out=var_t[:, 0:n], in0=mu_t[:, 0:n], in1=mu_t[:, 0:n],
                                                          op=ALU.mult),
                         R=[mu_b], W=[var_b])
                    S.op("dve", lambda e: e.scalar_tensor_tensor(out=var_t[:, 0:n], in0=p2[:, 0:n], scalar=1.0 / 512,
                                                                 in1=var_t[:, 0:n], op0=ALU.mult, op1=ALU.subtract),
                         R=[p2b, var_b], W=[var_b])
                    S.op("act", lambda e: e.activation(out=ln_t[:, 0:n], in_=var_t[:, 0:n], func=AF.Ln,
                                                       bias=eps_t[:, :], scale=1.0),
                         R=[var_b, eps_b], W=[ln_b])
                    S.op("act", lambda e: e.activation(out=rs_t[:, 0:n], in_=ln_t[:, 0:n], func=AF.Exp, scale=-0.5),
                         R=[ln_b], W=[rs_b])
                    for c in range(4):
                        S.op("dve", lambda e, c=c: e.tensor_tensor(out=y_t[:, c, 0:n], in0=y_t[:, c, 0:n],
                                                                   in1=mu_t[:, 0:n], op=ALU.subtract),
                             R=[yb[c], mu_b], W=[yb[c]])
                        S.op("dve", lambda e, c=c: e.tensor_tensor(out=y_t[:, c, 0:n], in0=y_t[:, c, 0:n],
                                                                   in1=rs_t[:, 0:n], op=ALU.mult),
                             R=[yb[c], rs_b], W=[yb[c]])
                        S.op("act", lambda e, c=c: e.activation(
                            out=actc_t[:, c, 0:n], in_=y_t[:, c, 0:n], func=AF.Silu,
                            bias=vt[:, V_LNB + c:V_LNB + c + 1], scale=vt[:, V_LNG + c:V_LNG + c + 1]),
                            R=[yb[c], vb], W=[actcb[c]])

                    bkt = [0] if bi == 0 else [1 + 2 * (bi - 1), 2 + 2 * (bi - 1)]
                    for j, kt in enumerate(bkt):
                        kc0, nk = ktiles[kt]
                        off = kc0 - c0
                        pv_, pvb_ = ps_next()
                        S.op("pe", lambda e: e.matmul(pv_[0:nk, 0:512], lhsT=ckvn_t[:, off:off + nk], rhs=wv_t[:],
                                                      start=True, stop=True),
                             R=[ckvn_b, wv_b], W=[pvb_])
                        S.op("act", lambda e: e.activation(out=vst_t[0:nk, kt, :], in_=pv_[0:nk, 0:512], func=AF.Copy),
                             R=[pvb_], W=[vstb[kt]])

                    def head_norm_rope(src_ps, src_pb, gcol, dst_fn, dst_buf, hb, extra_fn=None):
                        if extra_fn is None:
                            S.op("act", lambda e: e.activation(out=hc_t[0:96, hb, 0:n], in_=src_ps[0:96, 0:n],
                                                               func=AF.Copy),
                                 R=[src_pb], W=[hcb[hb]])
                        else:
                            S.op("act", lambda e: e.activation(out=hc_t[0:64, hb, 0:n], in_=src_ps[0:64, 0:n],
                                                               func=AF.Copy),
                                 R=[src_pb], W=[hcb[hb]])
                            extra_fn(hb)
                        pt, pb = ps_next()
                        S.op("act", lambda e: e.activation(out=sq_t[0:96, hb, 0:n], in_=hc_t[0:96, hb, 0:n],
                                                           func=AF.Square),
                             R=[hcb[hb]], W=[sqb[hb]])
                        S.op("pe", lambda e: e.matmul(pt[0:96, 0:n], lhsT=ones_t[0:96, 0:96], rhs=sq_t[0:96, hb, 0:n],
                                                      start=True, stop=True),
                             R=[ones_b, sqb[hb]], W=[pb])
                        S.op("act", lambda e: e.activation(out=t1_t[0:96, hb, 0:n], in_=pt[0:96, 0:n], func=AF.Ln,
                                                           bias=eps_t[0:96, :], scale=1.0 / 96),
                             R=[pb, eps_b], W=[t1b[hb]])
                        S.op("act", lambda e: e.activation(out=t2_t[0:96, hb, 0:n], in_=t1_t[0:96, hb, 0:n],
                                                           func=AF.Exp, scale=-0.5),
                             R=[t1b[hb]], W=[t2b[hb]])
                        S.op("dve", lambda e: e.scalar_tensor_tensor(
                            out=hn_t[0:96, hb, 0:n], in0=hc_t[0:96, hb, 0:n], scalar=vt[0:96, gcol:gcol + 1],
                            in1=t2_t[0:96, hb, 0:n], op0=ALU.mult, op1=ALU.mult),
                            R=[hcb[hb], vb, t2b[hb]], W=[hnb[hb]])
                        S.op("act", lambda e: e.activation(out=dst_fn(0, 64), in_=hn_t[0:64, hb, 0:n], func=AF.Copy),
                             R=[hnb[hb]], W=[dst_buf])
                        S.op("dve", lambda e: e.tensor_copy(out=hnh_t[64:96, hb, 0:n], in_=hn_t[64:96, hb, 0:n]),
                             R=[hnb[hb]], W=[hnhb[hb]])
                        pr2, pr2b = ps_next()
                        S.op("pe", lambda e: e.matmul(pr2[0:96, 0:n], lhsT=prot_t[64:96, 0:96],
                                                      rhs=hnh_t[64:96, hb, 0:n], start=True, stop=True),
                             R=[prot_b, hnhb[hb]], W=[pr2b])
                        S.op("dve", lambda e: e.tensor_tensor(out=t1_t[64:96, hb, 0:n], in0=hn_t[64:96, hb, 0:n],
                                                              in1=cos_t[64:96, 0:n], op=ALU.mult),
                             R=[hnb[hb], cos_b, t1b[hb]], W=[t1b[hb]])
                        S.op("dve", lambda e: e.tensor_tensor(out=t2_t[64:96, hb, 0:n], in0=pr2[64:96, 0:n],
                                                              in1=sin_t[64:96, 0:n], op=ALU.mult),
                             R=[pr2b, sin_b, t2b[hb]], W=[t2b[hb]])
                        S.op("dve", lambda e: e.tensor_tensor(out=dst_fn(64, 96), in0=t1_t[64:96, hb, 0:n],
                                                              in1=t2_t[64:96, hb, 0:n], op=ALU.add),
                             R=[t1b[hb], t2b[hb]], W=[dst_buf])

                    for h in range(8):
                        hb = h % 2
                        pkh, pkhb = ps_next()
                        S.op("pe", lambda e, h=h: e.matmul(pkh[0:64, 0:n], lhsT=wkn_t[:, h * 64:(h + 1) * 64],
                                                           rhs=ckvn_t[:, 0:n], start=True, stop=True),
                             R=[wkn_b, ckvn_b], W=[pkhb])

                        def add_rope(hb_):
                            S.op("pool", lambda e: e.tensor_copy(out=hc_t[64:96, hb_, 0:n], in_=kr_t[64:96, 0:n]),
                                 R=[kr_b], W=[hcb[hb_]])
                        head_norm_rope(pkh, pkhb, V_KNG,
                                       lambda a, b_, h=h: kst_t[a:b_, h, c0:c0 + n], kstb[bi][h], hb, add_rope)
                        pqh, pqhb = ps_next()
                        for kc in range(2):
                            S.op("pe", lambda e, h=h, kc=kc: e.matmul(
                                pqh[0:96, 0:n], lhsT=wuq_t[:, kc * 768 + h * 96:kc * 768 + (h + 1) * 96],
                                rhs=cqn_t[:, kc, 0:n], start=(kc == 0), stop=(kc == 1)),
                                R=[wuq_b, cqnb[kc]], W=[pqhb])
                        head_norm_rope(pqh, pqhb, V_QNG,
                                       lambda a, b_, h=h: q_t[a:b_, h, 0:n], qb[h], hb)

                    if bi == 0:
                        vis = [(0, 0, False)]
                    else:
                        B = bi - 1
                        vis = [(0, 0, False)] + [(1 + m, 0, False) for m in range(2 * B)]
                        vis += [(1 + 2 * B + j, 128 * j, True) for j in range(2)]
                    pti = [0]
                    for h in range(8):
                        pnum, pnumb = pst[0], psb[0]
                        pden, pdenb = pst[1], psb[1]
                        pend = None
                        for vi in range(len(vis) + 1):
                            cur = None
                            if vi < len(vis):
                                kt, q0, diag = vis[vi]
                                kc0, nk = ktiles[kt]
                                kbi = 0 if kt == 0 else 1 + (kt - 1) // 2
                                sps, spsb = ps_next()
                                S.op("pe", lambda e: e.matmul(sps[0:nk, q0:n], lhsT=kst_t[0:96, h, kc0:kc0 + nk],
                                                              rhs=q_t[0:96, h, q0:n], start=True, stop=True),
                                     R=[kstb[kbi][h], qb[h]], W=[spsb])
                                pi = pti[0] % 4
                                pti[0] += 1
                                S.op("act", lambda e: e.activation(out=pT_t[0:nk, pi, q0:n], in_=sps[0:nk, q0:n],
                                                                   func=AF.Exp, scale=SCALE),
                                     R=[spsb], W=[pTb[pi]])
                                if diag:
                                    S.op("pool", lambda e: e.memset(pT_t[64:128, pi, q0:q0 + 64], 0.0), W=[pTb[pi]])
                                cur = (kt, q0, nk, pi, vi)
                            if pend is not None:
                                kt_, q0_, nk_, pi_, vi_ = pend
                                first = (vi_ == 0)
                                last = (vi_ == len(vis) - 1)
                                S.op("pe", lambda e: e.matmul(pnum[0:64, q0_:n], lhsT=vst_t[0:nk_, kt_, h * 64:(h + 1) * 64],
                                                              rhs=pT_t[0:nk_, pi_, q0_:n], start=first, stop=last),
                                     R=[vstb[kt_], pTb[pi_]], W=[pnumb])
                                S.op("pe", lambda e: e.matmul(pden[0:64, q0_:n], lhsT=ones_t[0:nk_, 0:64],
                                                              rhs=pT_t[0:nk_, pi_, q0_:n], start=first, stop=last),
                                     R=[ones_b, pTb[pi_]], W=[pdenb])
                            pend = cur
                        rb = h % 2
                        S.op("dve", lambda e: e.reciprocal(out=rden_t[0:64, rb, 0:n], in_=pden[0:64, 0:n]),
                             R=[pdenb], W=[rdenb[rb]])
                        S.op("dve", lambda e, h=h: e.tensor_tensor(out=oT_t[0:64, h, 0:n], in0=pnum[0:64, 0:n],
                                                                   in1=rden_t[0:64, rb, 0:n], op=ALU.mult),
                             R=[pnumb, rdenb[rb]], W=[oTb[h]])

                    for c in range(8):
                        pg1, pg1b = zproj(128)
                        S.op("act", lambda e: e.activation(out=sig_t[:, 0, 0:n], in_=pg1[:, 0:n], func=AF.Sigmoid),
                             R=[pg1b], W=[sigb[0]])
                        wt, wb = w_take()
                        pc, pcb = ps_next()
                        for kc in range(4):
                            S.op("pe", lambda e, kc=kc: e.matmul(pc[:, 0:n], lhsT=wt[:, kc * 128:(kc + 1) * 128],
                                                                 rhs=actc_t[:, kc, 0:n], start=(kc == 0), stop=(kc == 3)),
                                 R=[wb, actcb[kc]], W=[pcb])
                        w_issue()
                        S.op("dve", lambda e: e.tensor_tensor(out=m1_t[:, 0, 0:n], in0=pc[:, 0:n], in1=sig_t[:, 0, 0:n],
                                                              op=ALU.mult),
                             R=[pcb, sigb[0]], W=[m1b[0]])
                        pg2, pg2b = zproj(128)
                        S.op("act", lambda e: e.activation(out=sig_t[:, 1, 0:n], in_=pg2[:, 0:n], func=AF.Sigmoid),
                             R=[pg2b], W=[sigb[1]])
                        wt2, wb2 = w_take()
                        pm, pmb = ps_next()
                        for h in range(8):
                            S.op("pe", lambda e, h=h: e.matmul(pm[:, 0:n], lhsT=wt2[0:64, h * 128:(h + 1) * 128],
                                                               rhs=oT_t[0:64, h, 0:n], start=(h == 0), stop=(h == 7)),
                                 R=[wb2, oTb[h]], W=[pmb])
                        w_issue()
                        S.op("dve", lambda e: e.tensor_tensor(out=m1_t[:, 1, 0:n], in0=pm[:, 0:n], in1=sig_t[:, 1, 0:n],
                                                              op=ALU.mult),
                             R=[pmb, sigb[1]], W=[m1b[1]])
                        S.op("dve", lambda e, c=c: e.tensor_tensor(out=mg_t[:, c, 0:n], in0=m1_t[:, 0, 0:n],
                                                                   in1=m1_t[:, 1, 0:n], op=ALU.add),
                             R=[m1b[0], m1b[1]], W=[mgb[c]])
                    for c in range(8):
                        wt, wb = w_take()
                        po, pob = ps_next()
                        for kc in range(8):
                            S.op("pe", lambda e, kc=kc: e.matmul(po[:, 0:n], lhsT=wt[:, kc * 128:(kc + 1) * 128],
                                                                 rhs=mg_t[:, kc, 0:n], start=(kc == 0), stop=(kc == 7)),
                                 R=[wb, mgb[kc]], W=[pob])
                        w_issue()
                        S.op("dve", lambda e, c=c: e.tensor_tensor(out=x_t[:, c, c0:c0 + n], in0=x_t[:, c, c0:c0 + n],
                                                                   in1=po[:, 0:n], op=ALU.add),
                             R=[pob, xb[bi][c]], W=[xb[bi][c]])
                while cstate["i"] < NCV:
                    conv_step()
                S.barrier(stgb_m)

            if do_peer:
              with contextlib.ExitStack() as stp:
                A = A0.sub(stp)
                wq_t, wq_b = A.sb([128, 8 * 2048], BF16, "wq")
                ky_t, ky_b = A.sb([128, 16 * 128], BF16, "keysT")
                hT_t, _ = A.sb([128, 8, 128], BF16, "h2T")
                hTb = [Buf("h2T%d" % c) for c in range(8)]
                sq_t, _ = A.sb([128, 2, 128], BF16, "sq")
                sqb = [Buf("sq0"), Buf("sq1")]
                ln_t, ln_b = A.sb([128, 128], F32, "ln")
                rs_t, rs_b = A.sb([128, 128], F32, "rs")
                htok_t, htok_b = A.sb([128, D], BF16, "h2tok")
                qT_t, _ = A.sb([128, 16, 128], BF16, "qT")
                qTb = [Buf("qT%d" % g) for g in range(4)]
                s_t, s_b = A.sb([128, 16, 128], F32, "s")
                s2_t, s2_b = A.sb([128, 16, 128], F32, "s2")
                sv_t, sv_b = A.sb([128, 16, 16], F32, "sv")
                si_t, si_b = A.sb([128, 16, 16], U32, "si")
                sif_t, sif_b = A.sb([128, 16, 16], F32, "sif")
                cand_t, cand_b = A.sb([128, 8, 256], F32, "cand")
                cand2_t, cand2_b = A.sb([128, 8, 256], F32, "cand2")
                top_t, top_b = A.sb([128, 8, 16], F32, "top")
                pos_t, pos_b = A.sb([128, 8, 16], U32, "pos")
                ai_t, ai_b = A.sb([128, 8, 16], U32, "ai")
                bi_t, bi_b = A.sb([128, 8, 16], U32, "bi")
                af_t, af_b = A.sb([128, 8, 16], F32, "af")
                bf_t, bf_b = A.sb([128, 8, 16], F32, "bf")
                oh_t, oh_b = A.sb([128, 8, 256], F32, "oh")
                tmp_t, tmp_b = A.sb([128, 8, 256], F32, "tmp")
                isel_t, isel_b = A.sb([128, 128], F32, "isel")
                jsel_t, jsel_b = A.sb([128, 128], F32, "jsel")
                eidf_t, eidf_b = A.sb([128, 128], F32, "eidf")
                eid_t, eid_b = A.sb([128, 128], I32, "eid")
                ew_t, ew_b = A.sb([128, 128], F32, "ew")
                ssum_t, ssum_b = A.sb([128, 8], F32, "ssum")
                gw_t, gw_b = A.sb([128, 128], F32, "gw")
                a_t, a_b = A.sb([128, 128], F32, "a")
                g1_t, g1_b = A.sb([128, 128], F32, "g1")
                g2_t, g2_b = A.sb([128, 128], F32, "g2")
                actw_t, actw_b = A.sb([128, 128], F32, "actw")
                junk_t, junk_b = A.sb([128, D], BF16, "junk")
                acc_t, acc_b = A.sb([128, D], F32, "acc")
                gb_t, _ = A.sb([128, NG, D], BF16, "gbuf")
                gbb = [Buf("gb%d" % i) for i in range(NG)]
                if cstate["i"] < NCV:
                    stg_t, _ = A.sb([128, 2, 2, D], BF16, "stg")
                    stgb_p = [Buf("stg0"), Buf("stg1")]
                    cstate["stg"] = (stg_t, stgb_p)
                    while cstate["i"] < NCV:
                        conv_step()
                    S.barrier(stgb_p)

                posf_t, posf_b = A.sb([128, 8, 16], F32, "posf")
                thr_t, thr_b = A.sb([128, 16], F32, "thr")
                S.op("dve", lambda e: e.tensor_scalar(out=thr_t[:], in0=iota16, scalar1=16.0, scalar2=16.0,
                                                      op0=ALU.mult, op1=ALU.add), R=[cst_b], W=[thr_b])
                S.dma("pool", wq_t[:], wq_d[l], W=[wq_b])
                S.dma("pool", ky_t[:], keys_d[l], W=[ky_b])
                gi = [0]
                ptiles = list(range(NKT))
                if l == n_layers - 1:
                    ptiles = ptiles[1:]
                for tt in ptiles:
                    c0, np_ = ktiles[tt]
                    bi = 0 if tt == 0 else 1 + (tt - 1) // 2
                    xs = lambda c: x_t[:, c, c0:c0 + np_]
                    rms_feature(A, xs, xb[bi], 8, 128, np_, float(D), V_FFNG, vt, vb,
                                lambda c: hT_t[:, c, 0:np_], hTb, sq_t, sqb, ln_t, ln_b, rs_t, rs_b)
                    ptk = pst[0][:, :].bitcast(BF16)
                    for c in range(8):
                        S.op("pe", lambda e, c=c: e.transpose(out=ptk[0:np_, c * 128:(c + 1) * 128],
                                                              in_=hT_t[:, c, 0:np_], identity=identb_t[:]),
                             R=[hTb[c], identb_b], W=[psb[0]])
                    S.op("act", lambda e: e.activation(out=htok_t[0:np_, :], in_=ptk[0:np_, :], func=AF.Copy),
                         R=[psb[0]], W=[htok_b])
                    for gq in range(4):
                        pq, pqb = ps_next()
                        for gg in range(4):
                            g = gq * 4 + gg
                            for kc in range(8):
                                S.op("pe", lambda e, g=g, gg=gg, kc=kc: e.matmul(
                                    pq[:, gg * 128:gg * 128 + np_],
                                    lhsT=wq_t[:, kc * 2048 + g * 128:kc * 2048 + (g + 1) * 128],
                                    rhs=hT_t[:, kc, 0:np_], start=(kc == 0), stop=(kc == 7)),
                                    R=[wq_b, hTb[kc]], W=[pqb])
                        S.op("act", lambda e, gq=gq: e.activation(
                            out=qT_t[:, gq * 4:(gq + 1) * 4, 0:np_],
                            in_=pq[:, :].rearrange("p (g t) -> p g t", g=4)[:, :, 0:np_], func=AF.Copy),
                            R=[pqb], W=[qTb[gq]])
                    for gq in range(4):
                        pss, pssb = ps_next()
                        for gg in range(4):
                            g = gq * 4 + gg
                            S.op("pe", lambda e, g=g, gg=gg: e.matmul(
                                pss[0:np_, gg * 128:(gg + 1) * 128], lhsT=qT_t[:, g, 0:np_],
                                rhs=ky_t[:, g * 128:(g + 1) * 128], start=True, stop=True),
                                R=[qTb[gq], ky_b], W=[pssb])
                        S.op("act", lambda e, gq=gq: e.activation(
                            out=s_t[0:np_, gq * 4:(gq + 1) * 4, :],
                            in_=pss[0:np_, :].rearrange("p (g n) -> p g n", g=4), func=AF.Copy),
                            R=[pssb], W=[s_b])
                    P = np_
                    for g in range(16):
                        S.op("dve", lambda e, g=g: e.max(out=sv_t[0:P, g, 0:8], in_=s_t[0:P, g, :]), R=[s_b], W=[sv_b])
                    for g in range(16):
                        S.op("dve", lambda e, g=g: e.max_index(out=si_t[0:P, g, 0:8], in_max=sv_t[0:P, g, 0:8],
                                                               in_values=s_t[0:P, g, :]), R=[s_b, sv_b], W=[si_b])
                    for g in range(16):
                        S.op("dve", lambda e, g=g: e.match_replace(out=s2_t[0:P, g, :], in_to_replace=sv_t[0:P, g, 0:8],
                                                                   in_values=s_t[0:P, g, :], imm_value=-1e30),
                             R=[s_b, sv_b], W=[s2_b])
                    for g in range(16):
                        S.op("dve", lambda e, g=g: e.max(out=sv_t[0:P, g, 8:16], in_=s2_t[0:P, g, :]), R=[s2_b], W=[sv_b])
                    for g in range(16):
                        S.op("dve", lambda e, g=g: e.max_index(out=si_t[0:P, g, 8:16], in_max=sv_t[0:P, g, 8:16],
                                                               in_values=s2_t[0:P, g, :]), R=[s2_b, sv_b], W=[si_b])
                    S.op("dve", lambda e: e.tensor_copy(out=sif_t[0:P], in_=si_t[0:P]), R=[si_b], W=[sif_b])
                    cand4 = cand_t[0:P].rearrange("p h (a b) -> p h a b", a=16)
                    S.op("dve", lambda e: e.tensor_tensor(
                        out=cand4, in0=sv_t[0:P, 0::2, :].unsqueeze(3).broadcast_to([P, 8, 16, 16]),
                        in1=sv_t[0:P, 1::2, :].unsqueeze(2).broadcast_to([P, 8, 16, 16]), op=ALU.add),
                        R=[sv_b], W=[cand_b])
                    for h in range(8):
                        S.op("dve", lambda e, h=h: e.max(out=top_t[0:P, h, 0:8], in_=cand_t[0:P, h, :]), R=[cand_b], W=[top_b])
                    for h in range(8):
                        S.op("dve", lambda e, h=h: e.max_index(out=pos_t[0:P, h, 0:8], in_max=top_t[0:P, h, 0:8],
                                                               in_values=cand_t[0:P, h, :]), R=[cand_b, top_b], W=[pos_b])
                    for h in range(8):
                        S.op("dve", lambda e, h=h: e.match_replace(out=cand2_t[0:P, h, :], in_to_replace=top_t[0:P, h, 0:8],
                                                                   in_values=cand_t[0:P, h, :], imm_value=-1e30),
                             R=[cand_b, top_b], W=[cand2_b])
                    for h in range(8):
                        S.op("dve", lambda e, h=h: e.max(out=top_t[0:P, h, 8:16], in_=cand2_t[0:P, h, :]), R=[cand2_b], W=[top_b])
                    for h in range(8):
                        S.op("dve", lambda e, h=h: e.max_index(out=pos_t[0:P, h, 8:16], in_max=top_t[0:P, h, 8:16],
                                                               in_values=cand2_t[0:P, h, :]), R=[cand2_b, top_b], W=[pos_b])
                    oh4 = oh_t[0:P].rearrange("p h (k a) -> p h k a", k=16)
                    S.op("dve", lambda e: e.tensor_copy(out=posf_t[0:P], in_=pos_t[0:P]), R=[pos_b], W=[posf_b])
                    S.op("dve", lambda e: e.tensor_tensor(
                        out=oh4, in0=posf_t[0:P].unsqueeze(3).broadcast_to([P, 8, 16, 16]),
                        in1=thr_t[0:P, :].unsqueeze(1).unsqueeze(1).broadcast_to([P, 8, 16, 16]), op=ALU.is_ge),
                        R=[posf_b, thr_b], W=[oh_b])
                    S.op("dve", lambda e: e.tensor_reduce(
                        out=af_t[0:P].rearrange("p h k -> p (h k)"),
                        in_=oh_t[0:P].rearrange("p h (k a) -> p (h k) a", k=16), axis=AX.X, op=ALU.add),
                        R=[oh_b], W=[af_b])
                    S.op("dve", lambda e: e.scalar_tensor_tensor(
                        out=bf_t[0:P].rearrange("p h k -> p (h k)"), in0=af_t[0:P].rearrange("p h k -> p (h k)"),
                        scalar=-16.0, in1=posf_t[0:P].rearrange("p h k -> p (h k)"), op0=ALU.mult, op1=ALU.add),
                        R=[af_b, posf_b], W=[bf_b])
                    oh4 = oh_t[0:P].rearrange("p h (k a) -> p h k a", k=16)
                    tmp4 = tmp_t[0:P].rearrange("p h (k a) -> p h k a", k=16)
                    io4 = iota16[0:P, :].unsqueeze(1).unsqueeze(1).broadcast_to([P, 8, 16, 16])
                    for (xf_t, xf_b, par, dst_t, dst_b) in ((af_t, af_b, 0, isel_t, isel_b),
                                                            (bf_t, bf_b, 1, jsel_t, jsel_b)):
                        S.op("dve", lambda e, xf_t=xf_t: e.tensor_tensor(
                            out=oh4, in0=xf_t[0:P].unsqueeze(3).broadcast_to([P, 8, 16, 16]), in1=io4,
                            op=ALU.is_equal), R=[xf_b, cst_b], W=[oh_b])
                        S.op("dve", lambda e, par=par: e.tensor_tensor(
                            out=tmp4, in0=oh4,
                            in1=sif_t[0:P, par::2, :].unsqueeze(2).broadcast_to([P, 8, 16, 16]), op=ALU.mult),
                            R=[oh_b, sif_b], W=[tmp_b])
                        S.op("dve", lambda e, dst_t=dst_t: e.tensor_reduce(
                            out=dst_t[0:P, :], in_=tmp_t[0:P].rearrange("p h (k a) -> p (h k) a", k=16),
                            axis=AX.X, op=ALU.add), R=[tmp_b], W=[dst_b])
                    S.op("dve", lambda e: e.scalar_tensor_tensor(out=eidf_t[0:P, :], in0=isel_t[0:P, :], scalar=128.0,
                                                                 in1=jsel_t[0:P, :], op0=ALU.mult, op1=ALU.add),
                         R=[isel_b, jsel_b], W=[eidf_b])
                    S.op("dve", lambda e: e.tensor_copy(out=eid_t[0:P, :], in_=eidf_t[0:P, :]), R=[eidf_b], W=[eid_b])
                    ew3 = ew_t[0:P, :].rearrange("p (h k) -> p h k", h=8)
                    S.op("dve", lambda e: e.tensor_tensor(out=ew3, in0=top_t[0:P],
                                                          in1=top_t[0:P, :, 0:1].broadcast_to([P, 8, 16]),
                                                          op=ALU.subtract), R=[top_b], W=[ew_b])
                    S.op("act", lambda e: e.activation(out=ew_t[0:P, :], in_=ew_t[0:P, :], func=AF.Exp),
                         R=[ew_b], W=[ew_b])
                    S.op("dve", lambda e: e.tensor_reduce(out=ssum_t[0:P, :], in_=ew3, axis=AX.X, op=ALU.add),
                         R=[ew_b], W=[ssum_b])
                    S.op("dve", lambda e: e.reciprocal(out=ssum_t[0:P, :], in_=ssum_t[0:P, :]), R=[ssum_b], W=[ssum_b])
                    S.op("dve", lambda e: e.tensor_tensor(out=gw_t[0:P, :].rearrange("p (h k) -> p h k", h=8), in0=ew3,
                                                          in1=ssum_t[0:P, :].unsqueeze(2).broadcast_to([P, 8, 16]),
                                                          op=ALU.mult), R=[ew_b, ssum_b], W=[gw_b])
                    for k in range(128):
                        gbi = gi[0] % NG
                        gi[0] += 1
                        S.dma("pool", gb_t[0:P, gbi, :], ubf_d[l],
                              R=[eid_b], W=[gbb[gbi]],
                              indirect=bass.IndirectOffsetOnAxis(ap=eid_t[0:P, k:k + 1], axis=0))
                        S.op("dve", lambda e, k=k, gbi=gbi: e.scalar_tensor_tensor(
                            out=junk_t[0:P, :], in0=gb_t[0:P, gbi, :], scalar=1.0, in1=htok_t[0:P, :],
                            op0=ALU.mult, op1=ALU.mult, accum_out=a_t[0:P, k:k + 1]),
                            R=[gbb[gbi], htok_b], W=[junk_b, a_b])
                    S.op("dve", lambda e: e.tensor_tensor(out=g1_t[0:P, :], in0=a_t[0:P, :], in1=a_t[0:P, :], op=ALU.mult),
                         R=[a_b], W=[g1_b])
                    S.op("dve", lambda e: e.tensor_scalar(out=g1_t[0:P, :], in0=g1_t[0:P, :], scalar1=0.044715, scalar2=1.0,
                                                          op0=ALU.mult, op1=ALU.add), R=[g1_b], W=[g1_b])
                    S.op("dve", lambda e: e.tensor_tensor(out=g1_t[0:P, :], in0=g1_t[0:P, :], in1=a_t[0:P, :], op=ALU.mult),
                         R=[g1_b, a_b], W=[g1_b])
                    S.op("act", lambda e: e.activation(out=g2_t[0:P, :], in_=g1_t[0:P, :], func=AF.Sigmoid,
                                                       scale=1.5957691216057308), R=[g1_b], W=[g2_b])
                    S.op("dve", lambda e: e.tensor_tensor(out=g2_t[0:P, :], in0=g2_t[0:P, :], in1=a_t[0:P, :], op=ALU.mult),
                         R=[g2_b, a_b], W=[g2_b])
                    S.op("dve", lambda e: e.tensor_tensor(out=actw_t[0:P, :], in0=g2_t[0:P, :], in1=gw_t[0:P, :], op=ALU.mult),
                         R=[g2_b, gw_b], W=[actw_b])
                    for k in range(128):
                        gbi = gi[0] % NG
                        gi[0] += 1
                        S.dma("pool", gb_t[0:P, gbi, :], vbf_d[l],
                              R=[eid_b], W=[gbb[gbi]],
                              indirect=bass.IndirectOffsetOnAxis(ap=eid_t[0:P, k:k + 1], axis=0))
                        if k == 0:
                            S.op("dve", lambda e, gbi=gbi: e.tensor_scalar(
                                out=acc_t[0:P, :], in0=gb_t[0:P, gbi, :], scalar1=actw_t[0:P, 0:1], scalar2=None,
                                op0=ALU.mult), R=[gbb[gbi], actw_b], W=[acc_b])
                        else:
                            S.op("dve", lambda e, k=k, gbi=gbi: e.scalar_tensor_tensor(
                                out=acc_t[0:P, :], in0=gb_t[0:P, gbi, :], scalar=actw_t[0:P, k:k + 1], in1=acc_t[0:P, :],
                                op0=ALU.mult, op1=ALU.add), R=[gbb[gbi], actw_b, acc_b], W=[acc_b])
                    for c in range(8):
                        ptr, ptrb = ps_next()
                        S.op("pe", lambda e, c=c: e.transpose(out=ptr[:, 0:P], in_=acc_t[0:P, c * 128:(c + 1) * 128],
                                                              identity=identf[0:P, 0:P]),
                             R=[acc_b, cst_b], W=[ptrb])
                        S.op("dve", lambda e, c=c: e.tensor_tensor(out=x_t[:, c, c0:c0 + P], in0=x_t[:, c, c0:c0 + P],
                                                                   in1=ptr[:, 0:P], op=ALU.add),
                             R=[ptrb, xb[bi][c]], W=[xb[bi][c]])
                S.barrier()

        outb = Buf("out")
        for c in range(8):
            rl = [xb[1 + b][c] for b in range(NRB)]
            S.dma("sp", out_d[c * 128:(c + 1) * 128, :], x_t[:, c, NMETA:L], R=rl, sembuf=outb)
        S.wait_all("sp", [outb] + [xb[1 + b][c] for b in range(NRB) for c in range(8)])
        nc._stats = (S.nins, S.nwait, S.ndsem, dict(S.cnt))
    return nc


def _wstream(w_in, w_conv_out, w_mla_out, w_out):
    tiles = []

    def ztile(c0, m):
        t = np.zeros((128, 8, 128), np.float32)
        t[:, :, :m] = w_in[:, c0:c0 + m].reshape(8, 128, m).transpose(1, 0, 2)
        tiles.append(t.reshape(128, 1024))

    for c in range(4):
        ztile(128 * c, 128)
        ztile(512 + 128 * c, 128)
    ztile(1024, 128)
    ztile(1152, 128)
    ztile(1280, 128)
    ztile(1344, 96)
    for c in range(8):
        ztile(1440 + 128 * c, 128)
        t = np.zeros((128, 8, 128), np.float32)
        t[:, 0:4, :] = w_conv_out[:, 128 * c:128 * (c + 1)].reshape(4, 128, 128).transpose(1, 0, 2)
        tiles.append(t.reshape(128, 1024))
        ztile(2464 + 128 * c, 128)
        t = np.zeros((128, 8, 128), np.float32)
        t[0:64, :, :] = w_mla_out[:, 128 * c:128 * (c + 1)].reshape(8, 64, 128).transpose(1, 0, 2)
        tiles.append(t.reshape(128, 1024))
    for c in range(8):
        t = w_out[:, 128 * c:128 * (c + 1)].reshape(8, 128, 128).transpose(1, 0, 2)
        tiles.append(np.ascontiguousarray(t).reshape(128, 1024))
    assert len(tiles) == NWT
    return np.stack(tiles)


def _prep_shared(inp):
    f = lambda a: np.ascontiguousarray(np.asarray(a, dtype=np.float32))
    sh = {}
    sh["wstream"] = np.stack([_wstream(f(inp["w_in"][l]), f(inp["w_conv_out"][l]), f(inp["w_mla_out"][l]),
                                       f(inp["w_out"][l])) for l in range(DEPTH)])
    sh["wuq"] = f(np.stack([f(inp["w_uq"][l]).reshape(2, 128, 768).transpose(1, 0, 2).reshape(128, 1536)
                            for l in range(DEPTH)]))
    wukv = f(inp["w_ukv"]).reshape(DEPTH, 128, 8, 128)
    sh["wkn"] = f(wukv[:, :, :, 0:64].reshape(DEPTH, 128, 512))
    sh["wv"] = f(wukv[:, :, :, 64:128].reshape(DEPTH, 128, 512))
    vecs = np.zeros((DEPTH, 128, NVEC), np.float32)
    for l in range(DEPTH):
        vecs[l, :, V_MIXG:V_MIXG + 8] = f(inp["mix_norm_g"][l]).reshape(8, 128).T
        vecs[l, :, V_FFNG:V_FFNG + 8] = f(inp["ffn_norm_g"][l]).reshape(8, 128).T
        cw = f(inp["conv_w"][l]).reshape(31, 4, 128)
        vecs[l, :, V_CONVW:V_CONVW + 124] = cw.transpose(2, 1, 0).reshape(128, 124)
        vecs[l, :, V_CONVB:V_CONVB + 4] = f(inp["conv_b"][l]).reshape(4, 128).T
        vecs[l, :, V_LNG:V_LNG + 4] = f(inp["conv_ln_g"][l]).reshape(4, 128).T
        vecs[l, :, V_LNB:V_LNB + 4] = f(inp["conv_ln_b"][l]).reshape(4, 128).T
        vecs[l, :, V_QAG:V_QAG + 2] = f(inp["q_a_norm_g"][l]).reshape(2, 128).T
        vecs[l, :, V_KVAG] = f(inp["kv_a_norm_g"][l])
        vecs[l, 0:96, V_QNG] = f(inp["q_norm_g"][l])
        vecs[l, 0:96, V_KNG] = f(inp["k_norm_g"][l])
    sh["vecs"] = vecs
    sh["wq"] = f(np.stack([f(inp["peer_wq"][l]).reshape(8, 128, 2048).transpose(1, 0, 2).reshape(128, 8 * 2048)
                           for l in range(DEPTH)]))
    ky = f(inp["peer_keys"]).reshape(DEPTH, 16, 128, 128)
    sh["keysT"] = f(ky.transpose(0, 3, 1, 2).reshape(DEPTH, 128, 16 * 128))
    for l in range(DEPTH):
        sh["peer_u%d" % l] = f(inp["peer_u"][l])
        sh["peer_v%d" % l] = f(inp["peer_v"][l])
    pos = np.arange(L, dtype=np.float32)
    inv = (1.0 / (np.float32(10000.0) ** (np.arange(0, 32, 2, dtype=np.float32) / np.float32(32)))).astype(np.float32)
    ang = pos[:, None] * inv[None, :]
    ang = np.concatenate([ang, ang], axis=-1)
    cosT = np.zeros((96, L), np.float32)
    sinT = np.zeros((96, L), np.float32)
    cosT[64:96] = np.cos(ang).T
    sinT[64:96] = np.sin(ang).T
    sh["cosT"] = cosT
    sh["sinT"] = sinT
    cst = np.zeros((128, 240), np.float32)
    cst[:, 0:128] = np.eye(128, dtype=np.float32)
    prot = np.zeros((128, 96), np.float32)
    for m in range(16):
        prot[64 + m + 16, 64 + m] = -1.0
        prot[64 + m, 64 + m + 16] = 1.0
    cst[:, 128:224] = prot
    cst[:, 224:240] = np.arange(16, dtype=np.float32)[None, :]
    sh["consts"] = cst
    sh["metaT"] = f(f(inp["meta_tokens"]).T)
    return sh


_CACHE = {}


def kernel(**inputs):
    x = np.asarray(inputs["x"], dtype=np.float32)
    nb = x.shape[0]
    sh = _prep_shared(inputs)
    if "nc" not in _CACHE:
        _CACHE["nc"] = build_program()
    nc = _CACHE["nc"]
    in_maps = []
    for b in range(nb):
        m = dict(sh)
        m["xT"] = np.ascontiguousarray(x[b].T)
        in_maps.append(m)
    res = run_bass_kernel_spmd(nc, in_maps, core_ids=list(range(nb)))
    out = np.stack([np.asarray(r["outT"], dtype=np.float32).T for r in res.results], axis=0)
    return np.ascontiguousarray(out)
```

```python
import contextlib
import numpy as np
import concourse.bass as bass
import concourse.mybir as mybir
from concourse.bass_utils import run_bass_kernel_spmd

F32 = mybir.dt.float32
BF16 = mybir.dt.bfloat16
I32 = mybir.dt.int32
U32 = mybir.dt.uint32
ALU = mybir.AluOpType
AF = mybir.ActivationFunctionType
AX = mybir.AxisListType

D = 1024
SEQ = 2048
NMETA = 16
L = SEQ + NMETA
DEPTH = 2
NB = 256
NRB = SEQ // NB
NSLOT = 6
NWT = 52
NKT = 1 + SEQ // 128
NEXP = 16384
NG = 12
NVEC = 160
V_MIXG, V_FFNG, V_CONVW, V_CONVB, V_LNG, V_LNB, V_QAG, V_KVAG, V_QNG, V_KNG = (
    0, 8, 16, 140, 144, 148, 152, 154, 155, 156)
EPS = 1e-6
SCALE = 96.0 ** -0.5


class Buf:
    __slots__ = ("name", "w", "r", "dsem", "dval")

    def __init__(self, name):
        self.name = name
        self.w = None
        self.r = {}
        self.dsem = None
        self.dval = 0


class Sched:
    def __init__(self, nc, stack):
        self.nc = nc
        self.stack = stack
        self.eng = {"pe": nc.tensor, "act": nc.scalar, "dve": nc.vector,
                    "pool": nc.gpsimd, "sp": nc.sync}
        self.sems = {}
        self.cnt = {}
        self.known = {}
        self.snap = {}
        for e in self.eng:
            self.sems[e] = stack.enter_context(nc.semaphore("s_" + e))
            self.cnt[e] = 0
            self.known[e] = {}
            self.snap[e] = {}
        self.ndsem = 0
        self.nwait = 0
        self.nins = 0

    def new_dsem(self):
        k = "d%d" % self.ndsem
        self.ndsem += 1
        self.sems[k] = self.stack.enter_context(self.nc.semaphore("s_" + k))
        return k

    def _need(self, e, deps, key, val):
        if self.known[e].get(key, 0) >= val:
            return
        if deps.get(key, 0) < val:
            deps[key] = val

    def _collect(self, e, R, W):
        deps = {}
        for b in R:
            if b.w is not None:
                k, v = b.w
                if k == e and e == "pe":
                    continue
                self._need(e, deps, k, v)
        for b in W:
            if b.w is not None:
                k, v = b.w
                if k != e:
                    self._need(e, deps, k, v)
            for k, v in b.r.items():
                if k != e:
                    self._need(e, deps, k, v)
        return deps

    def _emit_waits(self, e, deps):
        for k, v in deps.items():
            self.eng[e].wait_ge(self.sems[k], v)
            self.nwait += 1
            kn = self.known[e]
            if kn.get(k, 0) < v:
                kn[k] = v
            sn = self.snap.get(k, {}).get(v)
            if sn:
                for kk, vv in sn.items():
                    if kn.get(kk, 0) < vv:
                        kn[kk] = vv

    def op(self, e, fn, R=(), W=()):
        deps = self._collect(e, R, W)
        self._emit_waits(e, deps)
        ins = fn(self.eng[e])
        self.cnt[e] += 1
        n = self.cnt[e]
        ins.then_inc(self.sems[e], 1)
        self.nins += 1
        self.snap[e][n] = dict(self.known[e])
        for b in R:
            b.r[e] = n
        for b in W:
            b.w = (e, n)
            b.r = {}
        return ins

    def dma(self, q, out, in_, R=(), W=(), sembuf=None, indirect=None):
        deps = self._collect(q, R, W)
        self._emit_waits(q, deps)
        sb = sembuf if sembuf is not None else (W[0] if W else R[0])
        if sb.dsem is None:
            sb.dsem = self.new_dsem()
        if indirect is not None:
            ins = self.eng[q].indirect_dma_start(out=out, out_offset=None, in_=in_,
                                                 in_offset=indirect)
        else:
            ins = self.eng[q].dma_start(out=out, in_=in_)
        sb.dval += 16
        ins.then_inc(self.sems[sb.dsem], 16)
        tok = (sb.dsem, sb.dval)
        for b in R:
            b.r[tok[0]] = tok[1]
        for b in W:
            b.w = tok
            b.r = {}
        return tok

    def wait_all(self, e, bufs):
        deps = {}
        for b in bufs:
            if b.w is not None:
                self._need(e, deps, b.w[0], b.w[1])
            for k, v in b.r.items():
                self._need(e, deps, k, v)
        self._emit_waits(e, deps)

    def barrier(self, bufs=()):
        for e in self.eng:
            deps = {}
            for f in self.eng:
                if f != e and self.cnt[f] > 0:
                    self._need(e, deps, f, self.cnt[f])
            for b in bufs:
                if b.w is not None:
                    self._need(e, deps, b.w[0], b.w[1])
                for k, v in b.r.items():
                    self._need(e, deps, k, v)
            self._emit_waits(e, deps)


class Alloc:
    def __init__(self, nc, stack):
        self.nc = nc
        self.stack = stack
        self.n = [0]

    def sub(self, stack):
        a = Alloc(self.nc, stack)
        a.n = self.n
        return a

    def sb(self, shape, dt, name="t"):
        self.n[0] += 1
        nm = "%s_%d" % (name, self.n[0])
        t = self.stack.enter_context(self.nc.sbuf_tensor(nm, list(shape), dt))
        return t, Buf(nm)

    def ps(self, shape, dt, name="p"):
        self.n[0] += 1
        nm = "%s_%d" % (name, self.n[0])
        t = self.stack.enter_context(self.nc.psum_tensor(nm, list(shape), dt))
        return t, Buf(nm)


def build_program(n_layers=DEPTH, do_mixer=True, do_peer=True):
    nc = bass.Bass("TRN2", target_bir_lowering=False)

    def din(name, shape, dt=F32):
        return nc.dram_tensor(name, list(shape), dt, kind="ExternalInput").ap()

    xT_d = din("xT", [D, SEQ])
    meta_d = din("metaT", [D, NMETA])
    wst_d = din("wstream", [DEPTH, NWT, 128, 1024])
    wuq_d = din("wuq", [DEPTH, 128, 2 * 768])
    wkn_d = din("wkn", [DEPTH, 128, 512])
    wv_d = din("wv", [DEPTH, 128, 512])
    vecs_d = din("vecs", [DEPTH, 128, NVEC])
    wq_d = din("wq", [DEPTH, 128, 8 * 2048])
    keys_d = din("keysT", [DEPTH, 128, 16 * 128])
    pu_d = [din("peer_u%d" % l, [NEXP, D]) for l in range(DEPTH)]
    pv_d = [din("peer_v%d" % l, [NEXP, D]) for l in range(DEPTH)]
    uv_d = [nc.dram_tensor("uvbf%d" % l, [NEXP, 2 * D], BF16).ap() for l in range(DEPTH)]
    cos_d = din("cosT", [96, L])
    sin_d = din("sinT", [96, L])
    cst_d = din("consts", [128, 128 + 96 + 16])
    out_d = nc.dram_tensor("outT", [D, SEQ], F32, kind="ExternalOutput").ap()

    with contextlib.ExitStack() as st0:
        S = Sched(nc, st0)
        A0 = Alloc(nc, st0)

        x_t, _ = A0.sb([128, 8, L], F32, "x")
        xb = [[Buf("x%d_%d" % (b, c)) for c in range(8)] for b in range(1 + NRB)]
        cst_t, cst_b = A0.sb([128, 240], F32, "cst")
        identf = cst_t[:, 0:128]
        iota16 = cst_t[:, 224:240]
        identb_t, identb_b = A0.sb([128, 128], BF16, "identb")
        prot_t, prot_b = A0.sb([128, 96], BF16, "prot")
        ones_t, ones_b = A0.sb([128, 128], BF16, "ones")
        eps_t, eps_b = A0.sb([128, 1], F32, "eps")
        vecs_t = []
        vecs_b = []
        for l in range(DEPTH):
            t, b = A0.sb([128, NVEC], F32, "vecs")
            vecs_t.append(t)
            vecs_b.append(b)
        pst = []
        psb = []
        for i in range(8):
            t, b = A0.ps([128, 512], F32, "bank")
            pst.append(t)
            psb.append(b)
        psrr = [0]

        def ps_next():
            i = 2 + psrr[0] % 6
            psrr[0] += 1
            return pst[i], psb[i]

        S.dma("sp", cst_t[:], cst_d, W=[cst_b])
        for l in range(DEPTH):
            S.dma("sp", vecs_t[l][:], vecs_d[l], W=[vecs_b[l]])
        for c in range(8):
            S.dma("sp", x_t[:, c, 0:NMETA], meta_d[c * 128:(c + 1) * 128, :], W=[xb[0][c]])
        for c in range(8):
            wl = [xb[1 + b][c] for b in range(NRB)]
            S.dma("sp", x_t[:, c, NMETA:L], xT_d[c * 128:(c + 1) * 128, :], W=wl, sembuf=wl[0])
        S.op("pool", lambda e: e.memset(ones_t[:], 1.0), W=[ones_b])
        S.op("pool", lambda e: e.memset(eps_t[:], EPS), W=[eps_b])
        S.op("dve", lambda e: e.tensor_copy(out=identb_t[:], in_=identf), R=[cst_b], W=[identb_b])
        S.op("dve", lambda e: e.tensor_copy(out=prot_t[:], in_=cst_t[:, 128:224]), R=[cst_b], W=[prot_b])

        blocks = [(0, NMETA)] + [(NMETA + NB * b, NB) for b in range(NRB)]
        ktiles = [(0, NMETA)] + [(NMETA + 128 * m, 128) for m in range(SEQ // 128)]

        def rms_feature(A, src_fn, src_bufs, nchunk, npart, n, dim, gcol, vt, vb, dst_fn, dst_bufs,
                        sq_t, sq_b, ln_t, ln_b, rs_t, rs_b):
            pt, pb = ps_next()
            for c in range(nchunk):
                S.op("act", lambda e, c=c: e.activation(out=sq_t[0:npart, c % 2, 0:n], in_=src_fn(c),
                                                        func=AF.Square),
                     R=[src_bufs[c]], W=[sq_b[c % 2]])
                S.op("pe", lambda e, c=c: e.matmul(pt[0:npart, 0:n], lhsT=ones_t[0:npart, 0:npart],
                                                   rhs=sq_t[0:npart, c % 2, 0:n],
                                                   start=(c == 0), stop=(c == nchunk - 1)),
                     R=[ones_b, sq_b[c % 2]], W=[pb])
            S.op("act", lambda e: e.activation(out=ln_t[0:npart, 0:n], in_=pt[0:npart, 0:n], func=AF.Ln,
                                               bias=eps_t[0:npart, :], scale=1.0 / dim),
                 R=[pb, eps_b], W=[ln_b])
            S.op("act", lambda e: e.activation(out=rs_t[0:npart, 0:n], in_=ln_t[0:npart, 0:n], func=AF.Exp,
                                               scale=-0.5),
                 R=[ln_b], W=[rs_b])
            for c in range(nchunk):
                S.op("dve", lambda e, c=c: e.scalar_tensor_tensor(
                    out=dst_fn(c), in0=src_fn(c), scalar=vt[0:npart, gcol + c:gcol + c + 1],
                    in1=rs_t[0:npart, 0:n], op0=ALU.mult, op1=ALU.mult),
                    R=[src_bufs[c], vb, rs_b], W=[dst_bufs[c]])

        for l in range(n_layers):
            vt = vecs_t[l]
            vb = vecs_b[l]
            cstate = {"i": 0, "stg": None}
            NCV = 2 * (NEXP // 256)

            def conv_step():
                i = cstate["i"]
                if i >= NCV:
                    return
                cstate["i"] += 1
                stg_t, stgb = cstate["stg"]
                src = (pu_d[l], pv_d[l])[i % 2]
                dst = uv_d[l][:, (i % 2) * D:(i % 2 + 1) * D]
                r0 = (i // 2) * 256
                sb_ = i % 2
                S.dma("pool", stg_t[:, sb_, :, :], src[r0:r0 + 256, :].rearrange("(p j) d -> p j d", j=2),
                      W=[stgb[sb_]])
                S.dma("sp", dst[r0:r0 + 256, :].rearrange("(p j) d -> p j d", j=2), stg_t[:, sb_, :, :],
                      R=[stgb[sb_]], sembuf=stgb[sb_])
            if do_mixer:
              with contextlib.ExitStack() as stm:
                A = A0.sub(stm)
                kst_t, _ = A.sb([128, 8, L], BF16, "kst")
                kstb = [[Buf("k%d_%d" % (b, h)) for h in range(8)] for b in range(1 + NRB)]
                vst_t, _ = A.sb([128, NKT, 512], BF16, "vst")
                vstb = [Buf("v%d" % i) for i in range(NKT)]
                ring_t, _ = A.sb([128, NSLOT, 1024], BF16, "ring")
                ringb = [Buf("ring%d" % i) for i in range(NSLOT)]
                wuq_t, wuq_b = A.sb([128, 2 * 768], BF16, "wuq")
                wkn_t, wkn_b = A.sb([128, 512], BF16, "wkn")
                wv_t, wv_b = A.sb([128, 512], BF16, "wv")
                cos_t, cos_b = A.sb([128, NB], F32, "cos")
                sin_t, sin_b = A.sb([128, NB], F32, "sin")
                hT_t, _ = A.sb([128, 8, NB], BF16, "hT")
                hTb = [Buf("hT%d" % c) for c in range(8)]
                sq_t, _ = A.sb([128, 2, NB], BF16, "sq")
                sqb = [Buf("sq0"), Buf("sq1")]
                ln_t, ln_b = A.sb([128, NB], F32, "ln")
                rs_t, rs_b = A.sb([128, NB], F32, "rs")
                sig_t, _ = A.sb([128, 2, NB], F32, "sig")
                sigb = [Buf("sig0"), Buf("sig1")]
                ub_t, _ = A.sb([128, 4, 30 + NB], F32, "ubuf")
                ubb = [Buf("ub%d" % c) for c in range(4)]
                halo_t, _ = A.sb([128, 4, 30], F32, "halo")
                halob = [Buf("halo%d" % c) for c in range(4)]
                y_t, _ = A.sb([128, 4, NB], F32, "y")
                yb = [Buf("y%d" % c) for c in range(4)]
                yh_t, _ = A.sb([128, 4, NB], BF16, "yh")
                yhb = [Buf("yh%d" % c) for c in range(4)]
                ysq_t, _ = A.sb([128, 4, NB], BF16, "ysq")
                ysqb = [Buf("ysq%d" % c) for c in range(4)]
                mu_t, mu_b = A.sb([128, NB], F32, "mu")
                var_t, var_b = A.sb([128, NB], F32, "var")
                actc_t, _ = A.sb([128, 4, NB], BF16, "actc")
                actcb = [Buf("actc%d" % c) for c in range(4)]
                cq_t, _ = A.sb([128, 2, NB], F32, "cq")
                cqb = [Buf("cq0"), Buf("cq1")]
                cqn_t, _ = A.sb([128, 2, NB], BF16, "cqn")
                cqnb = [Buf("cqn0"), Buf("cqn1")]
                ckv_t, ckv_b = A.sb([128, NB], F32, "ckv")
                ckvn_t, ckvn_b = A.sb([128, NB], BF16, "ckvn")
                hc_t, _ = A.sb([128, 2, NB], F32, "hc")
                hcb = [Buf("hc0"), Buf("hc1")]
                hn_t, _ = A.sb([128, 2, NB], F32, "hn")
                hnb = [Buf("hn0"), Buf("hn1")]
                hnh_t, _ = A.sb([128, 2, NB], BF16, "hnh")
                hnhb = [Buf("hnh0"), Buf("hnh1")]
                t1_t, _ = A.sb([128, 2, NB], F32, "t1")
                t1b = [Buf("t1_0"), Buf("t1_1")]
                t2_t, _ = A.sb([128, 2, NB], F32, "t2")
                t2b = [Buf("t2_0"), Buf("t2_1")]
                q_t, _ = A.sb([128, 8, NB], BF16, "qblk")
                qb = [Buf("q%d" % h) for h in range(8)]
                pT_t, _ = A.sb([128, 4, NB], BF16, "pT")
                pTb = [Buf("pT%d" % i) for i in range(4)]
                rden_t, _ = A.sb([64, 2, NB], F32, "rden")
                rdenb = [Buf("rden0"), Buf("rden1")]
                oT_t, _ = A.sb([64, 8, NB], BF16, "oT")
                oTb = [Buf("oT%d" % h) for h in range(8)]
                m1_t, _ = A.sb([128, 2, NB], F32, "m1")
                m1b = [Buf("m1_0"), Buf("m1_1")]
                mg_t, _ = A.sb([128, 8, NB], BF16, "merged")
                mgb = [Buf("mg%d" % c) for c in range(8)]

                stg_t, _ = A.sb([128, 2, 2, D], BF16, "stg")
                stgb_m = [Buf("stg0"), Buf("stg1")]
                cstate["stg"] = (stg_t, stgb_m)
                S.dma("pool", wuq_t[:], wuq_d[l], W=[wuq_b])
                S.dma("pool", wkn_t[:], wkn_d[l], W=[wkn_b])
                S.dma("pool", wv_t[:], wv_d[l], W=[wv_b])
                for c in range(4):
                    S.op("pool", lambda e, c=c: e.memset(halo_t[:, c, :], 0.0), W=[halob[c]])

                wstate = {"issued": 0, "used": 0}
                total_tiles = NWT * len(blocks)

                def w_issue():
                    i = wstate["issued"]
                    if i >= total_tiles:
                        return
                    s = i % NSLOT
                    S.dma("pool", ring_t[:, s, :], wst_d[l, i % NWT], W=[ringb[s]])
                    wstate["issued"] += 1

                def w_take():
                    i = wstate["used"]
                    wstate["used"] += 1
                    s = i % NSLOT
                    return ring_t[:, s, :], ringb[s]

                for _ in range(NSLOT):
                    w_issue()

                for bi, (c0, n) in enumerate(blocks):
                    xs = lambda c: x_t[:, c, c0:c0 + n]
                    S.dma("sp", cos_t[64:96, 0:n], cos_d[64:96, c0:c0 + n], W=[cos_b])
                    S.dma("sp", sin_t[64:96, 0:n], sin_d[64:96, c0:c0 + n], W=[sin_b])
                    rms_feature(A, xs, xb[bi], 8, 128, n, float(D), V_MIXG, vt, vb,
                                lambda c: hT_t[:, c, 0:n], hTb, sq_t, sqb, ln_t, ln_b, rs_t, rs_b)

                    def zproj(M):
                        wt, wb = w_take()
                        pt, pb = ps_next()
                        for kc in range(8):
                            S.op("pe", lambda e, kc=kc: e.matmul(pt[0:M, 0:n], lhsT=wt[:, kc * 128:kc * 128 + M],
                                                                 rhs=hT_t[:, kc, 0:n],
                                                                 start=(kc == 0), stop=(kc == 7)),
                                 R=[wb, hTb[kc]], W=[pb])
                        w_issue()
                        if wstate["used"] % NWT < 16:
                            conv_step()
                        return pt, pb

                    for c in range(4):
                        pa, pab = zproj(128)
                        pg, pgb = zproj(128)
                        sg = c % 2
                        S.op("act", lambda e: e.activation(out=sig_t[:, sg, 0:n], in_=pg[:, 0:n], func=AF.Sigmoid),
                             R=[pgb], W=[sigb[sg]])
                        S.op("pool", lambda e, c=c: e.tensor_copy(out=ub_t[:, c, 0:30], in_=halo_t[:, c, :]),
                             R=[halob[c]], W=[ubb[c]])
                        S.op("dve", lambda e, c=c: e.tensor_tensor(out=ub_t[:, c, 30:30 + n], in0=pa[:, 0:n],
                                                                   in1=sig_t[:, sg, 0:n], op=ALU.mult),
                             R=[pab, sigb[sg]], W=[ubb[c]])
                        S.op("pool", lambda e, c=c: e.tensor_copy(out=halo_t[:, c, :], in_=ub_t[:, c, n:n + 30]),
                             R=[ubb[c]], W=[halob[c]])
                    for c in range(2):
                        pq, pqb = zproj(128)
                        S.op("act", lambda e, c=c: e.activation(out=cq_t[:, c, 0:n], in_=pq[:, 0:n], func=AF.Copy),
                             R=[pqb], W=[cqb[c]])
                    pk, pkb = zproj(128)
                    S.op("act", lambda e: e.activation(out=ckv_t[:, 0:n], in_=pk[:, 0:n], func=AF.Copy),
                         R=[pkb], W=[ckv_b])
                    pr, prb = zproj(96)
                    kr_t, kr_b = ln_t, ln_b
                    if "krope" not in wstate:
                        wstate["krope"] = A.sb([128, NB], F32, "krope")
                    kr_t, kr_b = wstate["krope"]
                    S.op("act", lambda e: e.activation(out=kr_t[64:96, 0:n], in_=pr[64:96, 0:n], func=AF.Copy),
                         R=[prb], W=[kr_b])
                    rms_feature(A, lambda c: cq_t[:, c, 0:n], cqb, 2, 128, n, 256.0, V_QAG, vt, vb,
                                lambda c: cqn_t[:, c, 0:n], cqnb, sq_t, sqb, ln_t, ln_b, rs_t, rs_b)
                    rms_feature(A, lambda c: ckv_t[:, 0:n], [ckv_b], 1, 128, n, 128.0, V_KVAG, vt, vb,
                                lambda c: ckvn_t[:, 0:n], [ckvn_b], sq_t, sqb, ln_t, ln_b, rs_t, rs_b)

                    for c in range(4):
                        S.op("dve", lambda e, c=c: e.tensor_scalar(
                            out=y_t[:, c, 0:n], in0=ub_t[:, c, 0:n],
                            scalar1=vt[:, V_CONVW + c * 31:V_CONVW + c * 31 + 1],
                            scalar2=vt[:, V_CONVB + c:V_CONVB + c + 1], op0=ALU.mult, op1=ALU.add),
                            R=[ubb[c], vb], W=[yb[c]])
                        for k in range(1, 31):
                            S.op("dve", lambda e, c=c, k=k: e.scalar_tensor_tensor(
                                out=y_t[:, c, 0:n], in0=ub_t[:, c, k:k + n],
                                scalar=vt[:, V_CONVW + c * 31 + k:V_CONVW + c * 31 + k + 1],
                                in1=y_t[:, c, 0:n], op0=ALU.mult, op1=ALU.add),
                                R=[ubb[c], vb, yb[c]], W=[yb[c]])
                    p1, p1b = ps_next()
                    p2, p2b = ps_next()
                    for c in range(4):
                        S.op("act", lambda e, c=c: e.activation(out=yh_t[:, c, 0:n], in_=y_t[:, c, 0:n], func=AF.Copy),
                             R=[yb[c]], W=[yhb[c]])
                        S.op("act", lambda e, c=c: e.activation(out=ysq_t[:, c, 0:n], in_=y_t[:, c, 0:n], func=AF.Square),
                             R=[yb[c]], W=[ysqb[c]])
                    for c in range(4):
                        S.op("pe", lambda e, c=c: e.matmul(p1[:, 0:n], lhsT=ones_t[:], rhs=yh_t[:, c, 0:n],
                                                           start=(c == 0), stop=(c == 3)),
                             R=[ones_b, yhb[c]], W=[p1b])
                    for c in range(4):
                        S.op("pe", lambda e, c=c: e.matmul(p2[:, 0:n], lhsT=ones_t[:], rhs=ysq_t[:, c, 0:n],
                                                           start=(c == 0), stop=(c == 3)),
                             R=[ones_b, ysqb[c]], W=[p2b])
                    S.op("dve", lambda e: e.tensor_scalar(out=mu_t[:, 0:n], in0=p1[:, 0:n], scalar1=1.0 / 512,
                                                          scalar2=None, op0=ALU.mult),
                         R=[p1b], W=[mu_b])
                    S.op("dve", lambda e: e.tensor_tensor(out=var_t[:, 0:n], in0=mu_t[:, 0:n], in1=mu_t[:, 0:n],
                                                          op=ALU.mult),
                         R=[mu_b], W=[var_b])
                    S.op("dve", lambda e: e.scalar_tensor_tensor(out=var_t[:, 0:n], in0=p2[:, 0:n], scalar=1.0 / 512,
                                                                 in1=var_t[:, 0:n], op0=ALU.mult, op1=ALU.subtract),
                         R=[p2b, var_b], W=[var_b])
                    S.op("act", lambda e: e.activation(out=ln_t[:, 0:n], in_=var_t[:, 0:n], func=AF.Ln,
                                                       bias=eps_t[:, :], scale=1.0),
                         R=[var_b, eps_b], W=[ln_b])
                    S.op("act", lambda e: e.activation(out=rs_t[:, 0:n], in_=ln_t[:, 0:n], func=AF.Exp, scale=-0.5),
                         R=[ln_b], W=[rs_b])
                    for c in range(4):
                        S.op("dve", lambda e, c=c: e.tensor_tensor(out=y_t[:, c, 0:n], in0=y_t[:, c, 0:n],
                                                                   in1=mu_t[:, 0:n], op=ALU.subtract),
                             R=[yb[c], mu_b], W=[yb[c]])
                        S.op("dve", lambda e, c=c: e.tensor_tensor(out=y_t[:, c, 0:n], in0=y_t[:, c, 0:n],
                                                                   in1=rs_t[:, 0:n], op=ALU.mult),
                             R=[yb[c], rs_b], W=[yb[c]])
                        S.op("act", lambda e, c=c: e.activation(
                            out=actc_t[:, c, 0:n], in_=y_t[:, c, 0:n], func=AF.Silu,
                            bias=vt[:, V_LNB + c:V_LNB + c + 1], scale=vt[:, V_LNG + c:V_LNG + c + 1]),
                            R=[yb[c], vb], W=[actcb[c]])

                    bkt = [0] if bi == 0 else [1 + 2 * (bi - 1), 2 + 2 * (bi - 1)]
                    for j, kt in enumerate(bkt):
                        kc0, nk = ktiles[kt]
                        off = kc0 - c0
                        pv_, pvb_ = ps_next()
                        S.op("pe", lambda e: e.matmul(pv_[0:nk, 0:512], lhsT=ckvn_t[:, off:off + nk], rhs=wv_t[:],
                                                      start=True, stop=True),
                             R=[ckvn_b, wv_b], W=[pvb_])
                        S.op("act", lambda e: e.activation(out=vst_t[0:nk, kt, :], in_=pv_[0:nk, 0:512], func=AF.Copy),
                             R=[pvb_], W=[vstb[kt]])

                    def head_norm_rope(src_ps, src_pb, gcol, dst_fn, dst_buf, hb, extra_fn=None):
                        if extra_fn is None:
                            S.op("act", lambda e: e.activation(out=hc_t[0:96, hb, 0:n], in_=src_ps[0:96, 0:n],
                                                               func=AF.Copy),
                                 R=[src_pb], W=[hcb[hb]])
                        else:
                            S.op("act", lambda e: e.activation(out=hc_t[0:64, hb, 0:n], in_=src_ps[0:64, 0:n],
                                                               func=AF.Copy),
                                 R=[src_pb], W=[hcb[hb]])
                            extra_fn(hb)
                        pt, pb = ps_next()
                        S.op("act", lambda e: e.activation(out=sq_t[0:96, hb, 0:n], in_=hc_t[0:96, hb, 0:n],
                                                           func=AF.Square),
                             R=[hcb[hb]], W=[sqb[hb]])
                        S.op("pe", lambda e: e.matmul(pt[0:96, 0:n], lhsT=ones_t[0:96, 0:96], rhs=sq_t[0:96, hb, 0:n],
                                                      start=True, stop=True),
                             R=[ones_b, sqb[hb]], W=[pb])
                        S.op("act", lambda e: e.activation(out=t1_t[0:96, hb, 0:n], in_=pt[0:96, 0:n], func=AF.Ln,
                                                           bias=eps_t[0:96, :], scale=1.0 / 96),
                             R=[pb, eps_b], W=[t1b[hb]])
                        S.op("act", lambda e: e.activation(out=t2_t[0:96, hb, 0:n], in_=t1_t[0:96, hb, 0:n],
                                                           func=AF.Exp, scale=-0.5),
                             R=[t1b[hb]], W=[t2b[hb]])
                        S.op("dve", lambda e: e.scalar_tensor_tensor(
                            out=hn_t[0:96, hb, 0:n], in0=hc_t[0:96, hb, 0:n], scalar=vt[0:96, gcol:gcol + 1],
                            in1=t2_t[0:96, hb, 0:n], op0=ALU.mult, op1=ALU.mult),
                            R=[hcb[hb], vb, t2b[hb]], W=[hnb[hb]])
                        S.op("act", lambda e: e.activation(out=dst_fn(0, 64), in_=hn_t[0:64, hb, 0:n], func=AF.Copy),
                             R=[hnb[hb]], W=[dst_buf])
                        S.op("dve", lambda e: e.tensor_copy(out=hnh_t[64:96, hb, 0:n], in_=hn_t[64:96, hb, 0:n]),
                             R=[hnb[hb]], W=[hnhb[hb]])
                        pr2, pr2b = ps_next()
                        S.op("pe", lambda e: e.matmul(pr2[0:96, 0:n], lhsT=prot_t[64:96, 0:96],
                                                      rhs=hnh_t[64:96, hb, 0:n], start=True, stop=True),
                             R=[prot_b, hnhb[hb]], W=[pr2b])
                        S.op("dve", lambda e: e.tensor_tensor(out=t1_t[64:96, hb, 0:n], in0=hn_t[64:96, hb, 0:n],
                                                              in1=cos_t[64:96, 0:n], op=ALU.mult),
                             R=[hnb[hb], cos_b, t1b[hb]], W=[t1b[hb]])
                        S.op("dve", lambda e: e.tensor_tensor(out=t2_t[64:96, hb, 0:n], in0=pr2[64:96, 0:n],
                                                              in1=sin_t[64:96, 0:n], op=ALU.mult),
                             R=[pr2b, sin_b, t2b[hb]], W=[t2b[hb]])
                        S.op("dve", lambda e: e.tensor_tensor(out=dst_fn(64, 96), in0=t1_t[64:96, hb, 0:n],
                                                              in1=t2_t[64:96, hb, 0:n], op=ALU.add),
                             R=[t1b[hb], t2b[hb]], W=[dst_buf])

                    for h in range(8):
                        hb = h % 2
                        pkh, pkhb = ps_next()
                        S.op("pe", lambda e, h=h: e.matmul(pkh[0:64, 0:n], lhsT=wkn_t[:, h * 64:(h + 1) * 64],
                                                           rhs=ckvn_t[:, 0:n], start=True, stop=True),
                             R=[wkn_b, ckvn_b], W=[pkhb])

                        def add_rope(hb_):
                            S.op("pool", lambda e: e.tensor_copy(out=hc_t[64:96, hb_, 0:n], in_=kr_t[64:96, 0:n]),
                                 R=[kr_b], W=[hcb[hb_]])
                        head_norm_rope(pkh, pkhb, V_KNG,
                                       lambda a, b_, h=h: kst_t[a:b_, h, c0:c0 + n], kstb[bi][h], hb, add_rope)
                        pqh, pqhb = ps_next()
                        for kc in range(2):
                            S.op("pe", lambda e, h=h, kc=kc: e.matmul(
                                pqh[0:96, 0:n], lhsT=wuq_t[:, kc * 768 + h * 96:kc * 768 + (h + 1) * 96],
                                rhs=cqn_t[:, kc, 0:n], start=(kc == 0), stop=(kc == 1)),
                                R=[wuq_b, cqnb[kc]], W=[pqhb])
                        head_norm_rope(pqh, pqhb, V_QNG,
                                       lambda a, b_, h=h: q_t[a:b_, h, 0:n], qb[h], hb)

                    if bi == 0:
                        vis = [(0, 0, False)]
                    else:
                        B = bi - 1
                        vis = [(0, 0, False)] + [(1 + m, 0, False) for m in range(2 * B)]
                        vis += [(1 + 2 * B + j, 128 * j, True) for j in range(2)]
                    pti = [0]
                    for h in range(8):
                        pnum, pnumb = pst[0], psb[0]
                        pden, pdenb = pst[1], psb[1]
                        pend = None
                        for vi in range(len(vis) + 1):
                            cur = None
                            if vi < len(vis):
                                kt, q0, diag = vis[vi]
                                kc0, nk = ktiles[kt]
                                kbi = 0 if kt == 0 else 1 + (kt - 1) // 2
                                sps, spsb = ps_next()
                                S.op("pe", lambda e: e.matmul(sps[0:nk, q0:n], lhsT=kst_t[0:96, h, kc0:kc0 + nk],
                                                              rhs=q_t[0:96, h, q0:n], start=True, stop=True),
                                     R=[kstb[kbi][h], qb[h]], W=[spsb])
                                pi = pti[0] % 4
                                pti[0] += 1
                                S.op("act", lambda e: e.activation(out=pT_t[0:nk, pi, q0:n], in_=sps[0:nk, q0:n],
                                                                   func=AF.Exp, scale=SCALE),
                                     R=[spsb], W=[pTb[pi]])
                                if diag:
                                    S.op("pool", lambda e: e.memset(pT_t[64:128, pi, q0:q0 + 64], 0.0), W=[pTb[pi]])
                                cur = (kt, q0, nk, pi, vi)
                            if pend is not None:
                                kt_, q0_, nk_, pi_, vi_ = pend
                                first = (vi_ == 0)
                                last = (vi_ == len(vis) - 1)
                                S.op("pe", lambda e: e.matmul(pnum[0:64, q0_:n], lhsT=vst_t[0:nk_, kt_, h * 64:(h + 1) * 64],
                                                              rhs=pT_t[0:nk_, pi_, q0_:n], start=first, stop=last),
                                     R=[vstb[kt_], pTb[pi_]], W=[pnumb])
                                S.op("pe", lambda e: e.matmul(pden[0:64, q0_:n], lhsT=ones_t[0:nk_, 0:64],
                                                              rhs=pT_t[0:nk_, pi_, q0_:n], start=first, stop=last),
                                     R=[ones_b, pTb[pi_]], W=[pdenb])
                            pend = cur
                        rb = h % 2
                        S.op("dve", lambda e: e.reciprocal(out=rden_t[0:64, rb, 0:n], in_=pden[0:64, 0:n]),
                             R=[pdenb], W=[rdenb[rb]])
                        S.op("dve", lambda e, h=h: e.tensor_tensor(out=oT_t[0:64, h, 0:n], in0=pnum[0:64, 0:n],
                                                                   in1=rden_t[0:64, rb, 0:n], op=ALU.mult),
                             R=[pnumb, rdenb[rb]], W=[oTb[h]])

                    for c in range(8):
                        pg1, pg1b = zproj(128)
                        S.op("act", lambda e: e.activation(out=sig_t[:, 0, 0:n], in_=pg1[:, 0:n], func=AF.Sigmoid),
                             R=[pg1b], W=[sigb[0]])
                        wt, wb = w_take()
                        pc, pcb = ps_next()
                        for kc in range(4):
                            S.op("pe", lambda e, kc=kc: e.matmul(pc[:, 0:n], lhsT=wt[:, kc * 128:(kc + 1) * 128],
                                                                 rhs=actc_t[:, kc, 0:n], start=(kc == 0), stop=(kc == 3)),
                                 R=[wb, actcb[kc]], W=[pcb])
                        w_issue()
                        S.op("dve", lambda e: e.tensor_tensor(out=m1_t[:, 0, 0:n], in0=pc[:, 0:n], in1=sig_t[:, 0, 0:n],
                                                              op=ALU.mult),
                             R=[pcb, sigb[0]], W=[m1b[0]])
                        pg2, pg2b = zproj(128)
                        S.op("act", lambda e: e.activation(out=sig_t[:, 1, 0:n], in_=pg2[:, 0:n], func=AF.Sigmoid),
                             R=[pg2b], W=[sigb[1]])
                        wt2, wb2 = w_take()
                        pm, pmb = ps_next()
                        for h in range(8):
                            S.op("pe", lambda e, h=h: e.matmul(pm[:, 0:n], lhsT=wt2[0:64, h * 128:(h + 1) * 128],
                                                               rhs=oT_t[0:64, h, 0:n], start=(h == 0), stop=(h == 7)),
                                 R=[wb2, oTb[h]], W=[pmb])
                        w_issue()
                        S.op("dve", lambda e: e.tensor_tensor(out=m1_t[:, 1, 0:n], in0=pm[:, 0:n], in1=sig_t[:, 1, 0:n],
                                                              op=ALU.mult),
                             R=[pmb, sigb[1]], W=[m1b[1]])
                        S.op("dve", lambda e, c=c: e.tensor_tensor(out=mg_t[:, c, 0:n], in0=m1_t[:, 0, 0:n],
                                                                   in1=m1_t[:, 1, 0:n], op=ALU.add),
                             R=[m1b[0], m1b[1]], W=[mgb[c]])
                    for c in range(8):
                        wt, wb = w_take()
                        po, pob = ps_next()
                        for kc in range(8):
                            S.op("pe", lambda e, kc=kc: e.matmul(po[:, 0:n], lhsT=wt[:, kc * 128:(kc + 1) * 128],
                                                                 rhs=mg_t[:, kc, 0:n], start=(kc == 0), stop=(kc == 7)),
                                 R=[wb, mgb[kc]], W=[pob])
                        w_issue()
                        S.op("dve", lambda e, c=c: e.tensor_tensor(out=x_t[:, c, c0:c0 + n], in0=x_t[:, c, c0:c0 + n],
                                                                   in1=po[:, 0:n], op=ALU.add),
                             R=[pob, xb[bi][c]], W=[xb[bi][c]])
                while cstate["i"] < NCV:
                    conv_step()
                S.barrier(stgb_m)

            if do_peer:
              with contextlib.ExitStack() as stp:
                A = A0.sub(stp)
                wq_t, wq_b = A.sb([128, 8 * 2048], BF16, "wq")
                ky_t, ky_b = A.sb([128, 16 * 128], BF16, "keysT")
                hT_t, _ = A.sb([128, 8, 128], BF16, "h2T")
                hTb = [Buf("h2T%d" % c) for c in range(8)]
                sq_t, _ = A.sb([128, 2, 128], BF16, "sq")
                sqb = [Buf("sq0"), Buf("sq1")]
                ln_t, ln_b = A.sb([128, 128], F32, "ln")
                rs_t, rs_b = A.sb([128, 128], F32, "rs")
                htok_t, htok_b = A.sb([128, D], BF16, "h2tok")
                qT_t, _ = A.sb([128, 16, 128], BF16, "qT")
                qTb = [Buf("qT%d" % g) for g in range(4)]
                s_t, s_b = A.sb([128, 16, 128], F32, "s")
                s2_t, s2_b = A.sb([128, 16, 128], F32, "s2")
                sv_t, sv_b = A.sb([128, 16, 16], F32, "sv")
                si_t, si_b = A.sb([128, 16, 16], U32, "si")
                sif_t, sif_b = A.sb([128, 16, 16], F32, "sif")
                cand_t, cand_b = A.sb([128, 8, 256], F32, "cand")
                cand2_t, cand2_b = s2_t[:, :, :].rearrange("p g n -> p (g n)").rearrange("p (h c) -> p h c", h=8), s2_b
                top_t, top_b = A.sb([128, 8, 16], F32, "top")
                pos_t, pos_b = A.sb([128, 8, 16], U32, "pos")
                ai_t, ai_b = A.sb([128, 8, 16], U32, "ai")
                bi_t, bi_b = A.sb([128, 8, 16], U32, "bi")
                af_t, af_b = A.sb([128, 8, 16], F32, "af")
                bf_t, bf_b = A.sb([128, 8, 16], F32, "bf")
                oh_t, oh_b = s_t[:, :, :].rearrange("p g n -> p (g n)").rearrange("p (h c) -> p h c", h=8), s_b
                tmp_t, tmp_b = cand_t, cand_b
                isel_t, isel_b = A.sb([128, 128], F32, "isel")
                jsel_t, jsel_b = A.sb([128, 128], F32, "jsel")
                eidf_t, eidf_b = A.sb([128, 128], F32, "eidf")
                eid_t, eid_b = A.sb([128, 128], I32, "eid")
                ew_t, ew_b = A.sb([128, 128], F32, "ew")
                ssum_t, ssum_b = A.sb([128, 8], F32, "ssum")
                gw_t, gw_b = A.sb([128, 128], F32, "gw")
                a_t, a_b = A.sb([128, 128], F32, "a")
                g1_t, g1_b = A.sb([128, 128], F32, "g1")
                g2_t, g2_b = A.sb([128, 128], F32, "g2")
                actw_t, actw_b = A.sb([128, 128], F32, "actw")
                junk_t, junk_b = A.sb([128, D], BF16, "junk")
                gb_t, _ = A.sb([128, NG, 2 * D], BF16, "gbuf")
                gbb = [Buf("gb%d" % i) for i in range(NG)]
                acc_t, acc_b = gb_t[:, 0, :].bitcast(F32), gbb[0]
                dg_t, _ = A.sb([128, 4, 128], BF16, "diag")
                dgb = [Buf("dg%d" % i) for i in range(4)]
                g1b = [Buf("g1_%d" % i) for i in range(32)]
                g2b = [Buf("g2_%d" % i) for i in range(32)]
                awb = [Buf("aw_%d" % i) for i in range(32)]
                if cstate["i"] < NCV:
                    stg_t, _ = A.sb([128, 2, 2, D], BF16, "stg")
                    stgb_p = [Buf("stg0"), Buf("stg1")]
                    cstate["stg"] = (stg_t, stgb_p)
                    while cstate["i"] < NCV:
                        conv_step()
                    S.barrier(stgb_p)

                posf_t, posf_b = A.sb([128, 8, 16], F32, "posf")
                thr_t, thr_b = A.sb([128, 16], F32, "thr")
                S.op("dve", lambda e: e.tensor_scalar(out=thr_t[:], in0=iota16, scalar1=16.0, scalar2=16.0,
                                                      op0=ALU.mult, op1=ALU.add), R=[cst_b], W=[thr_b])
                S.dma("pool", wq_t[:], wq_d[l], W=[wq_b])
                S.dma("pool", ky_t[:], keys_d[l], W=[ky_b])
                gi = [0]
                ptiles = list(range(NKT))
                if l == n_layers - 1:
                    ptiles = ptiles[1:]
                for tt in ptiles:
                    c0, np_ = ktiles[tt]
                    bi = 0 if tt == 0 else 1 + (tt - 1) // 2
                    xs = lambda c: x_t[:, c, c0:c0 + np_]
                    rms_feature(A, xs, xb[bi], 8, 128, np_, float(D), V_FFNG, vt, vb,
                                lambda c: hT_t[:, c, 0:np_], hTb, sq_t, sqb, ln_t, ln_b, rs_t, rs_b)
                    ptk = pst[0][:, :].bitcast(BF16)
                    for c in range(8):
                        S.op("pe", lambda e, c=c: e.transpose(out=ptk[0:np_, c * 128:(c + 1) * 128],
                                                              in_=hT_t[:, c, 0:np_], identity=identb_t[:]),
                             R=[hTb[c], identb_b], W=[psb[0]])
                    S.op("act", lambda e: e.activation(out=htok_t[0:np_, :], in_=ptk[0:np_, :], func=AF.Copy),
                         R=[psb[0]], W=[htok_b])
                    for gq in range(4):
                        pq, pqb = ps_next()
                        for gg in range(4):
                            g = gq * 4 + gg
                            for kc in range(8):
                                S.op("pe", lambda e, g=g, gg=gg, kc=kc: e.matmul(
                                    pq[:, gg * 128:gg * 128 + np_],
                                    lhsT=wq_t[:, kc * 2048 + g * 128:kc * 2048 + (g + 1) * 128],
                                    rhs=hT_t[:, kc, 0:np_], start=(kc == 0), stop=(kc == 7)),
                                    R=[wq_b, hTb[kc]], W=[pqb])
                        S.op("act", lambda e, gq=gq: e.activation(
                            out=qT_t[:, gq * 4:(gq + 1) * 4, 0:np_],
                            in_=pq[:, :].rearrange("p (g t) -> p g t", g=4)[:, :, 0:np_], func=AF.Copy),
                            R=[pqb], W=[qTb[gq]])
                    for gq in range(4):
                        pss, pssb = ps_next()
                        for gg in range(4):
                            g = gq * 4 + gg
                            S.op("pe", lambda e, g=g, gg=gg: e.matmul(
                                pss[0:np_, gg * 128:(gg + 1) * 128], lhsT=qT_t[:, g, 0:np_],
                                rhs=ky_t[:, g * 128:(g + 1) * 128], start=True, stop=True),
                                R=[qTb[gq], ky_b], W=[pssb])
                        S.op("act", lambda e, gq=gq: e.activation(
                            out=s_t[0:np_, gq * 4:(gq + 1) * 4, :],
                            in_=pss[0:np_, :].rearrange("p (g n) -> p g n", g=4), func=AF.Copy),
                            R=[pssb], W=[s_b])
                    P = np_
                    for g in range(16):
                        S.op("dve", lambda e, g=g: e.max(out=sv_t[0:P, g, 0:8], in_=s_t[0:P, g, :]), R=[s_b], W=[sv_b])
                    for g in range(16):
                        S.op("dve", lambda e, g=g: e.max_index(out=si_t[0:P, g, 0:8], in_max=sv_t[0:P, g, 0:8],
                                                               in_values=s_t[0:P, g, :]), R=[s_b, sv_b], W=[si_b])
                    for g in range(16):
                        S.op("dve", lambda e, g=g: e.match_replace(out=s2_t[0:P, g, :], in_to_replace=sv_t[0:P, g, 0:8],
                                                                   in_values=s_t[0:P, g, :], imm_value=-1e30),
                             R=[s_b, sv_b], W=[s2_b])
                    for g in range(16):
                        S.op("dve", lambda e, g=g: e.max(out=sv_t[0:P, g, 8:16], in_=s2_t[0:P, g, :]), R=[s2_b], W=[sv_b])
                    for g in range(16):
                        S.op("dve", lambda e, g=g: e.max_index(out=si_t[0:P, g, 8:16], in_max=sv_t[0:P, g, 8:16],
                                                               in_values=s2_t[0:P, g, :]), R=[s2_b, sv_b], W=[si_b])
                    S.op("dve", lambda e: e.tensor_copy(out=sif_t[0:P], in_=si_t[0:P]), R=[si_b], W=[sif_b])
                    cand4 = cand_t[0:P].rearrange("p h (a b) -> p h a b", a=16)
                    S.op("dve", lambda e: e.tensor_tensor(
                        out=cand4, in0=sv_t[0:P, 0::2, :].unsqueeze(3).broadcast_to([P, 8, 16, 16]),
                        in1=sv_t[0:P, 1::2, :].unsqueeze(2).broadcast_to([P, 8, 16, 16]), op=ALU.add),
                        R=[sv_b], W=[cand_b])
                    for h in range(8):
                        S.op("dve", lambda e, h=h: e.max(out=top_t[0:P, h, 0:8], in_=cand_t[0:P, h, :]), R=[cand_b], W=[top_b])
                    for h in range(8):
                        S.op("dve", lambda e, h=h: e.max_index(out=pos_t[0:P, h, 0:8], in_max=top_t[0:P, h, 0:8],
                                                               in_values=cand_t[0:P, h, :]), R=[cand_b, top_b], W=[pos_b])
                    for h in range(8):
                        S.op("dve", lambda e, h=h: e.match_replace(out=cand2_t[0:P, h, :], in_to_replace=top_t[0:P, h, 0:8],
                                                                   in_values=cand_t[0:P, h, :], imm_value=-1e30),
                             R=[cand_b, top_b], W=[cand2_b])
                    for h in range(8):
                        S.op("dve", lambda e, h=h: e.max(out=top_t[0:P, h, 8:16], in_=cand2_t[0:P, h, :]), R=[cand2_b], W=[top_b])
                    for h in range(8):
                        S.op("dve", lambda e, h=h: e.max_index(out=pos_t[0:P, h, 8:16], in_max=top_t[0:P, h, 8:16],
                                                               in_values=cand2_t[0:P, h, :]), R=[cand2_b, top_b], W=[pos_b])
                    oh4 = oh_t[0:P].rearrange("p h (k a) -> p h k a", k=16)
                    S.op("dve", lambda e: e.tensor_copy(out=posf_t[0:P], in_=pos_t[0:P]), R=[pos_b], W=[posf_b])
                    S.op("dve", lambda e: e.tensor_tensor(
                        out=oh4, in0=posf_t[0:P].unsqueeze(3).broadcast_to([P, 8, 16, 16]),
                        in1=thr_t[0:P, :].unsqueeze(1).unsqueeze(1).broadcast_to([P, 8, 16, 16]), op=ALU.is_ge),
                        R=[posf_b, thr_b], W=[oh_b])
                    S.op("dve", lambda e: e.tensor_reduce(
                        out=af_t[0:P].rearrange("p h k -> p (h k)"),
                        in_=oh_t[0:P].rearrange("p h (k a) -> p (h k) a", k=16), axis=AX.X, op=ALU.add),
                        R=[oh_b], W=[af_b])
                    S.op("dve", lambda e: e.scalar_tensor_tensor(
                        out=bf_t[0:P].rearrange("p h k -> p (h k)"), in0=af_t[0:P].rearrange("p h k -> p (h k)"),
                        scalar=-16.0, in1=posf_t[0:P].rearrange("p h k -> p (h k)"), op0=ALU.mult, op1=ALU.add),
                        R=[af_b, posf_b], W=[bf_b])
                    oh4 = oh_t[0:P].rearrange("p h (k a) -> p h k a", k=16)
                    tmp4 = tmp_t[0:P].rearrange("p h (k a) -> p h k a", k=16)
                    io4 = iota16[0:P, :].unsqueeze(1).unsqueeze(1).broadcast_to([P, 8, 16, 16])
                    for (xf_t, xf_b, par, dst_t, dst_b) in ((af_t, af_b, 0, isel_t, isel_b),
                                                            (bf_t, bf_b, 1, jsel_t, jsel_b)):
                        S.op("dve", lambda e, xf_t=xf_t: e.tensor_tensor(
                            out=oh4, in0=xf_t[0:P].unsqueeze(3).broadcast_to([P, 8, 16, 16]), in1=io4,
                            op=ALU.is_equal), R=[xf_b, cst_b], W=[oh_b])
                        S.op("dve", lambda e, par=par: e.tensor_tensor(
                            out=tmp4, in0=oh4,
                            in1=sif_t[0:P, par::2, :].unsqueeze(2).broadcast_to([P, 8, 16, 16]), op=ALU.mult),
                            R=[oh_b, sif_b], W=[tmp_b])
                        S.op("dve", lambda e, dst_t=dst_t: e.tensor_reduce(
                            out=dst_t[0:P, :], in_=tmp_t[0:P].rearrange("p h (k a) -> p (h k) a", k=16),
                            axis=AX.X, op=ALU.add), R=[tmp_b], W=[dst_b])
                    S.op("dve", lambda e: e.scalar_tensor_tensor(out=eidf_t[0:P, :], in0=isel_t[0:P, :], scalar=128.0,
                                                                 in1=jsel_t[0:P, :], op0=ALU.mult, op1=ALU.add),
                         R=[isel_b, jsel_b], W=[eidf_b])
                    S.op("dve", lambda e: e.tensor_copy(out=eid_t[0:P, :], in_=eidf_t[0:P, :]), R=[eidf_b], W=[eid_b])
                    ew3 = ew_t[0:P, :].rearrange("p (h k) -> p h k", h=8)
                    S.op("dve", lambda e: e.tensor_tensor(out=ew3, in0=top_t[0:P],
                                                          in1=top_t[0:P, :, 0:1].broadcast_to([P, 8, 16]),
                                                          op=ALU.subtract), R=[top_b], W=[ew_b])
                    S.op("act", lambda e: e.activation(out=ew_t[0:P, :], in_=ew_t[0:P, :], func=AF.Exp),
                         R=[ew_b], W=[ew_b])
                    S.op("dve", lambda e: e.tensor_reduce(out=ssum_t[0:P, :], in_=ew3, axis=AX.X, op=ALU.add),
                         R=[ew_b], W=[ssum_b])
                    S.op("dve", lambda e: e.reciprocal(out=ssum_t[0:P, :], in_=ssum_t[0:P, :]), R=[ssum_b], W=[ssum_b])
                    S.op("dve", lambda e: e.tensor_tensor(out=gw_t[0:P, :].rearrange("p (h k) -> p h k", h=8), in0=ew3,
                                                          in1=ssum_t[0:P, :].unsqueeze(2).broadcast_to([P, 8, 16]),
                                                          op=ALU.mult), R=[ew_b, ssum_b], W=[gw_b])
                    acc0, acc0b = pst[0], psb[0]
                    acc1, acc1b = pst[1], psb[1]
                    di = [0]
                    for k in range(128):
                        gbi = gi[0] % NG
                        gi[0] += 1
                        S.dma("pool", gb_t[0:P, gbi, :], uv_d[l],
                              R=[eid_b], W=[gbb[gbi]],
                              indirect=bass.IndirectOffsetOnAxis(ap=eid_t[0:P, k:k + 1], axis=0))
                        S.op("dve", lambda e: e.scalar_tensor_tensor(
                            out=junk_t[0:P, :], in0=gb_t[0:P, gbi, 0:D], scalar=1.0, in1=htok_t[0:P, :],
                            op0=ALU.mult, op1=ALU.mult, accum_out=a_t[0:P, k:k + 1]),
                            R=[gbb[gbi], htok_b], W=[junk_b, a_b])
                        if k % 4 != 3:
                            continue
                        g = k // 4
                        sl = slice(4 * g, 4 * g + 4)
                        S.op("dve", lambda e: e.tensor_tensor(out=g1_t[0:P, sl], in0=a_t[0:P, sl], in1=a_t[0:P, sl], op=ALU.mult),
                             R=[a_b], W=[g1b[g]])
                        S.op("dve", lambda e: e.tensor_scalar(out=g1_t[0:P, sl], in0=g1_t[0:P, sl], scalar1=0.044715, scalar2=1.0,
                                                              op0=ALU.mult, op1=ALU.add), R=[g1b[g]], W=[g1b[g]])
                        S.op("dve", lambda e: e.tensor_tensor(out=g1_t[0:P, sl], in0=g1_t[0:P, sl], in1=a_t[0:P, sl], op=ALU.mult),
                             R=[g1b[g], a_b], W=[g1b[g]])
                        S.op("act", lambda e: e.activation(out=g2_t[0:P, sl], in_=g1_t[0:P, sl], func=AF.Sigmoid,
                                                           scale=1.5957691216057308), R=[g1b[g]], W=[g2b[g]])
                        S.op("dve", lambda e: e.tensor_tensor(out=g2_t[0:P, sl], in0=g2_t[0:P, sl], in1=a_t[0:P, sl], op=ALU.mult),
                             R=[g2b[g], a_b], W=[g2b[g]])
                        S.op("dve", lambda e: e.tensor_tensor(out=actw_t[0:P, sl], in0=g2_t[0:P, sl], in1=gw_t[0:P, sl], op=ALU.mult),
                             R=[g2b[g], gw_b], W=[awb[g]])
                        for kk in range(4 * g, 4 * g + 4):
                            gbk = (gi[0] - 128 + 0) % NG
                            gbk = (gi[0] - (k + 1) + kk) % NG
                            dj = di[0] % 4
                            di[0] += 1
                            S.op("act", lambda e: e.activation(out=dg_t[0:P, dj, 0:P], in_=identb_t[0:P, 0:P], func=AF.Copy,
                                                               scale=actw_t[0:P, kk:kk + 1]),
                                 R=[identb_b, awb[g]], W=[dgb[dj]])
                            S.op("pe", lambda e: e.matmul(acc0[0:P, 0:512], lhsT=dg_t[0:P, dj, 0:P], rhs=gb_t[0:P, gbk, D:D + 512],
                                                          start=(kk == 0), stop=(kk == 127)),
                                 R=[dgb[dj], gbb[gbk]], W=[acc0b])
                            S.op("pe", lambda e: e.matmul(acc1[0:P, 0:512], lhsT=dg_t[0:P, dj, 0:P], rhs=gb_t[0:P, gbk, D + 512:2 * D],
                                                          start=(kk == 0), stop=(kk == 127)),
                                 R=[dgb[dj], gbb[gbk]], W=[acc1b])
                    S.op("act", lambda e: e.activation(out=acc_t[0:P, 0:512], in_=acc0[0:P, 0:512], func=AF.Copy),
                         R=[acc0b], W=[acc_b])
                    S.op("act", lambda e: e.activation(out=acc_t[0:P, 512:1024], in_=acc1[0:P, 0:512], func=AF.Copy),
                         R=[acc1b], W=[acc_b])
                    for c in range(8):
                        ptr, ptrb = ps_next()
                        S.op("pe", lambda e, c=c: e.transpose(out=ptr[:, 0:P], in_=acc_t[0:P, c * 128:(c + 1) * 128],
                                                              identity=identf[0:P, 0:P]),
                             R=[acc_b, cst_b], W=[ptrb])
                        S.op("dve", lambda e, c=c: e.tensor_tensor(out=x_t[:, c, c0:c0 + P], in0=x_t[:, c, c0:c0 + P],
                                                                   in1=ptr[:, 0:P], op=ALU.add),
                             R=[ptrb, xb[bi][c]], W=[xb[bi][c]])
                S.barrier()

        outb = Buf("out")
        for c in range(8):
            rl = [xb[1 + b][c] for b in range(NRB)]
            S.dma("sp", out_d[c * 128:(c + 1) * 128, :], x_t[:, c, NMETA:L], R=rl, sembuf=outb)
        S.wait_all("sp", [outb] + [xb[1 + b][c] for b in range(NRB) for c in range(8)])
        nc._stats = (S.nins, S.nwait, S.ndsem, dict(S.cnt))
    return nc


def _wstream(w_in, w_conv_out, w_mla_out, w_out):
    tiles = []

    def ztile(c0, m):
        t = np.zeros((128, 8, 128), np.float32)
        t[:, :, :m] = w_in[:, c0:c0 + m].reshape(8, 128, m).transpose(1, 0, 2)
        tiles.append(t.reshape(128, 1024))

    for c in range(4):
        ztile(128 * c, 128)
        ztile(512 + 128 * c, 128)
    ztile(1024, 128)
    ztile(1152, 128)
    ztile(1280, 128)
    ztile(1344, 96)
    for c in range(8):
        ztile(1440 + 128 * c, 128)
        t = np.zeros((128, 8, 128), np.float32)
        t[:, 0:4, :] = w_conv_out[:, 128 * c:128 * (c + 1)].reshape(4, 128, 128).transpose(1, 0, 2)
        tiles.append(t.reshape(128, 1024))
        ztile(2464 + 128 * c, 128)
        t = np.zeros((128, 8, 128), np.float32)
        t[0:64, :, :] = w_mla_out[:, 128 * c:128 * (c + 1)].reshape(8, 64, 128).transpose(1, 0, 2)
        tiles.append(t.reshape(128, 1024))
    for c in range(8):
        t = w_out[:, 128 * c:128 * (c + 1)].reshape(8, 128, 128).transpose(1, 0, 2)
        tiles.append(np.ascontiguousarray(t).reshape(128, 1024))
    assert len(tiles) == NWT
    return np.stack(tiles)


def _prep_shared(inp):
    f = lambda a: np.ascontiguousarray(np.asarray(a, dtype=np.float32))
    sh = {}
    sh["wstream"] = np.stack([_wstream(f(inp["w_in"][l]), f(inp["w_conv_out"][l]), f(inp["w_mla_out"][l]),
                                       f(inp["w_out"][l])) for l in range(DEPTH)])
    sh["wuq"] = f(np.stack([f(inp["w_uq"][l]).reshape(2, 128, 768).transpose(1, 0, 2).reshape(128, 1536)
                            for l in range(DEPTH)]))
    wukv = f(inp["w_ukv"]).reshape(DEPTH, 128, 8, 128)
    sh["wkn"] = f(wukv[:, :, :, 0:64].reshape(DEPTH, 128, 512))
    sh["wv"] = f(wukv[:, :, :, 64:128].reshape(DEPTH, 128, 512))
    vecs = np.zeros((DEPTH, 128, NVEC), np.float32)
    for l in range(DEPTH):
        vecs[l, :, V_MIXG:V_MIXG + 8] = f(inp["mix_norm_g"][l]).reshape(8, 128).T
        vecs[l, :, V_FFNG:V_FFNG + 8] = f(inp["ffn_norm_g"][l]).reshape(8, 128).T
        cw = f(inp["conv_w"][l]).reshape(31, 4, 128)
        vecs[l, :, V_CONVW:V_CONVW + 124] = cw.transpose(2, 1, 0).reshape(128, 124)
        vecs[l, :, V_CONVB:V_CONVB + 4] = f(inp["conv_b"][l]).reshape(4, 128).T
        vecs[l, :, V_LNG:V_LNG + 4] = f(inp["conv_ln_g"][l]).reshape(4, 128).T
        vecs[l, :, V_LNB:V_LNB + 4] = f(inp["conv_ln_b"][l]).reshape(4, 128).T
        vecs[l, :, V_QAG:V_QAG + 2] = f(inp["q_a_norm_g"][l]).reshape(2, 128).T
        vecs[l, :, V_KVAG] = f(inp["kv_a_norm_g"][l])
        vecs[l, 0:96, V_QNG] = f(inp["q_norm_g"][l])
        vecs[l, 0:96, V_KNG] = f(inp["k_norm_g"][l])
    sh["vecs"] = vecs
    sh["wq"] = f(np.stack([f(inp["peer_wq"][l]).reshape(8, 128, 2048).transpose(1, 0, 2).reshape(128, 8 * 2048)
                           for l in range(DEPTH)]))
    ky = f(inp["peer_keys"]).reshape(DEPTH, 16, 128, 128)
    sh["keysT"] = f(ky.transpose(0, 3, 1, 2).reshape(DEPTH, 128, 16 * 128))
    for l in range(DEPTH):
        sh["peer_u%d" % l] = f(inp["peer_u"][l])
        sh["peer_v%d" % l] = f(inp["peer_v"][l])
    pos = np.arange(L, dtype=np.float32)
    inv = (1.0 / (np.float32(10000.0) ** (np.arange(0, 32, 2, dtype=np.float32) / np.float32(32)))).astype(np.float32)
    ang = pos[:, None] * inv[None, :]
    ang = np.concatenate([ang, ang], axis=-1)
    cosT = np.zeros((96, L), np.float32)
    sinT = np.zeros((96, L), np.float32)
    cosT[64:96] = np.cos(ang).T
    sinT[64:96] = np.sin(ang).T
    sh["cosT"] = cosT
    sh["sinT"] = sinT
    cst = np.zeros((128, 240), np.float32)
    cst[:, 0:128] = np.eye(128, dtype=np.float32)
    prot = np.zeros((128, 96), np.float32)
    for m in range(16):
        prot[64 + m + 16, 64 + m] = -1.0
        prot[64 + m, 64 + m + 16] = 1.0
    cst[:, 128:224] = prot
    cst[:, 224:240] = np.arange(16, dtype=np.float32)[None, :]
    sh["consts"] = cst
    sh["metaT"] = f(f(inp["meta_tokens"]).T)
    return sh


_CACHE = {}


def kernel(**inputs):
    x = np.asarray(inputs["x"], dtype=np.float32)
    nb = x.shape[0]
    sh = _prep_shared(inputs)
    if "nc" not in _CACHE:
        _CACHE["nc"] = build_program()
    nc = _CACHE["nc"]
    in_maps = []
    for b in range(nb):
        m = dict(sh)
        m["xT"] = np.ascontiguousarray(x[b].T)
        in_maps.append(m)
    res = run_bass_kernel_spmd(nc, in_maps, core_ids=list(range(nb)))
    out = np.stack([np.asarray(r["outT"], dtype=np.float32).T for r in res.results], axis=0)
    return np.ascontiguousarray(out)
```

```python
import contextlib
import numpy as np
import concourse.bass as bass
import concourse.mybir as mybir
from concourse.bass_utils import run_bass_kernel_spmd

F32 = mybir.dt.float32
BF16 = mybir.dt.bfloat16
I32 = mybir.dt.int32
U32 = mybir.dt.uint32
ALU = mybir.AluOpType
AF = mybir.ActivationFunctionType
AX = mybir.AxisListType

D = 1024
SEQ = 2048
NMETA = 16
L = SEQ + NMETA
DEPTH = 2
NB = 256
NRB = SEQ // NB
NSLOT = 6
NWT = 52
NKT = 1 + SEQ // 128
NEXP = 16384
NG = 12
NVEC = 160
V_MIXG, V_FFNG, V_CONVW, V_CONVB, V_LNG, V_LNB, V_QAG, V_KVAG, V_QNG, V_KNG = (
    0, 8, 16, 140, 144, 148, 152, 154, 155, 156)
EPS = 1e-6
SCALE = 96.0 ** -0.5


class Buf:
    __slots__ = ("name", "w", "r", "dsem", "dval")

    def __init__(self, name):
        self.name = name
        self.w = None
        self.r = {}
        self.dsem = None
        self.dval = 0


class Sched:
    def __init__(self, nc, stack):
        self.nc = nc
        self.stack = stack
        self.eng = {"pe": nc.tensor, "act": nc.scalar, "dve": nc.vector,
                    "pool": nc.gpsimd, "sp": nc.sync}
        self.sems = {}
        self.cnt = {}
        self.known = {}
        self.snap = {}
        for e in self.eng:
            self.sems[e] = stack.enter_context(nc.semaphore("s_" + e))
            self.cnt[e] = 0
            self.known[e] = {}
            self.snap[e] = {}
        self.ndsem = 0
        self.nwait = 0
        self.nins = 0

    def new_dsem(self):
        k = "d%d" % self.ndsem
        self.ndsem += 1
        self.sems[k] = self.stack.enter_context(self.nc.semaphore("s_" + k))
        return k

    def _need(self, e, deps, key, val):
        if self.known[e].get(key, 0) >= val:
            return
        if deps.get(key, 0) < val:
            deps[key] = val

    def _collect(self, e, R, W):
        deps = {}
        for b in R:
            if b.w is not None:
                k, v = b.w
                if k == e and e == "pe":
                    continue
                self._need(e, deps, k, v)
        for b in W:
            if b.w is not None:
                k, v = b.w
                if k != e:
                    self._need(e, deps, k, v)
            for k, v in b.r.items():
                if k != e:
                    self._need(e, deps, k, v)
        return deps

    def _emit_waits(self, e, deps):
        for k, v in deps.items():
            self.eng[e].wait_ge(self.sems[k], v)
            self.nwait += 1
            kn = self.known[e]
            if kn.get(k, 0) < v:
                kn[k] = v
            sn = self.snap.get(k, {}).get(v)
            if sn:
                for kk, vv in sn.items():
                    if kn.get(kk, 0) < vv:
                        kn[kk] = vv

    def op(self, e, fn, R=(), W=()):
        deps = self._collect(e, R, W)
        self._emit_waits(e, deps)
        ins = fn(self.eng[e])
        self.cnt[e] += 1
        n = self.cnt[e]
        ins.then_inc(self.sems[e], 1)
        self.nins += 1
        self.snap[e][n] = dict(self.known[e])
        for b in R:
            b.r[e] = n
        for b in W:
            b.w = (e, n)
            b.r = {}
        return ins

    def dma(self, q, out, in_, R=(), W=(), sembuf=None, indirect=None):
        deps = self._collect(q, R, W)
        self._emit_waits(q, deps)
        sb = sembuf if sembuf is not None else (W[0] if W else R[0])
        if sb.dsem is None:
            sb.dsem = self.new_dsem()
        if indirect is not None:
            ins = self.eng[q].indirect_dma_start(out=out, out_offset=None, in_=in_,
                                                 in_offset=indirect)
        else:
            ins = self.eng[q].dma_start(out=out, in_=in_)
        sb.dval += 16
        ins.then_inc(self.sems[sb.dsem], 16)
        tok = (sb.dsem, sb.dval)
        for b in R:
            b.r[tok[0]] = tok[1]
        for b in W:
            b.w = tok
            b.r = {}
        return tok

    def wait_all(self, e, bufs):
        deps = {}
        for b in bufs:
            if b.w is not None:
                self._need(e, deps, b.w[0], b.w[1])
            for k, v in b.r.items():
                self._need(e, deps, k, v)
        self._emit_waits(e, deps)

    def barrier(self, bufs=()):
        for e in self.eng:
            deps = {}
            for f in self.eng:
                if f != e and self.cnt[f] > 0:
                    self._need(e, deps, f, self.cnt[f])
            for b in bufs:
                if b.w is not None:
                    self._need(e, deps, b.w[0], b.w[1])
                for k, v in b.r.items():
                    self._need(e, deps, k, v)
            self._emit_waits(e, deps)


class Alloc:
    def __init__(self, nc, stack):
        self.nc = nc
        self.stack = stack
        self.n = [0]

    def sub(self, stack):
        a = Alloc(self.nc, stack)
        a.n = self.n
        return a

    def sb(self, shape, dt, name="t"):
        self.n[0] += 1
        nm = "%s_%d" % (name, self.n[0])
        t = self.stack.enter_context(self.nc.sbuf_tensor(nm, list(shape), dt))
        return t, Buf(nm)

    def ps(self, shape, dt, name="p"):
        self.n[0] += 1
        nm = "%s_%d" % (name, self.n[0])
        t = self.stack.enter_context(self.nc.psum_tensor(nm, list(shape), dt))
        return t, Buf(nm)


def build_program(n_layers=DEPTH, do_mixer=True, do_peer=True):
    nc = bass.Bass("TRN2", target_bir_lowering=False)

    def din(name, shape, dt=F32):
        return nc.dram_tensor(name, list(shape), dt, kind="ExternalInput").ap()

    xT_d = din("xT", [D, SEQ])
    meta_d = din("metaT", [D, NMETA])
    wst_d = din("wstream", [DEPTH, NWT, 128, 1024])
    wuq_d = din("wuq", [DEPTH, 128, 2 * 768])
    wkn_d = din("wkn", [DEPTH, 128, 512])
    wv_d = din("wv", [DEPTH, 128, 512])
    vecs_d = din("vecs", [DEPTH, 128, NVEC])
    wq_d = din("wq", [DEPTH, 128, 8 * 2048])
    keys_d = din("keysT", [DEPTH, 128, 16 * 128])
    pu_d = [din("peer_u%d" % l, [NEXP, D]) for l in range(DEPTH)]
    pv_d = [din("peer_v%d" % l, [NEXP, D]) for l in range(DEPTH)]
    uv_d = [nc.dram_tensor("uvbf%d" % l, [NEXP, 2 * D], BF16).ap() for l in range(DEPTH)]
    wbf_d = nc.dram_tensor("wbf", [DEPTH, NWT, 128, 1024], BF16).ap()
    cos_d = din("cosT", [96, L])
    sin_d = din("sinT", [96, L])
    cst_d = din("consts", [128, 128 + 96 + 16])
    out_d = nc.dram_tensor("outT", [D, SEQ], F32, kind="ExternalOutput").ap()

    with contextlib.ExitStack() as st0:
        S = Sched(nc, st0)
        A0 = Alloc(nc, st0)

        x_t, _ = A0.sb([128, 8, L], F32, "x")
        xb = [[Buf("x%d_%d" % (b, c)) for c in range(8)] for b in range(1 + NRB)]
        cst_t, cst_b = A0.sb([128, 240], F32, "cst")
        identf = cst_t[:, 0:128]
        iota16 = cst_t[:, 224:240]
        identb_t, identb_b = A0.sb([128, 128], BF16, "identb")
        prot_t, prot_b = A0.sb([128, 96], BF16, "prot")
        ones_t, ones_b = A0.sb([128, 128], BF16, "ones")
        eps_t, eps_b = A0.sb([128, 1], F32, "eps")
        vecs_t = []
        vecs_b = []
        for l in range(DEPTH):
            t, b = A0.sb([128, NVEC], F32, "vecs")
            vecs_t.append(t)
            vecs_b.append(b)
        ps_all, _ = A0.ps([128, 8, 512], F32, "psum")
        pst = [ps_all[:, i, :] for i in range(8)]
        psb = [Buf("bank%d" % i) for i in range(8)]
        psrr = [0]
        pprr = [0]

        def ps_next():
            i = 4 + psrr[0] % 4
            psrr[0] += 1
            return pst[i], psb[i]

        def ps_pair():
            j = 4 + 2 * (pprr[0] % 2)
            pprr[0] += 1
            return ps_all[:, j:j + 2, :].rearrange("p b n -> p (b n)"), [psb[j], psb[j + 1]]

        S.dma("sp", cst_t[:], cst_d, W=[cst_b])
        for l in range(DEPTH):
            S.dma("sp", vecs_t[l][:], vecs_d[l], W=[vecs_b[l]])
        for c in range(8):
            S.dma("sp", x_t[:, c, 0:NMETA], meta_d[c * 128:(c + 1) * 128, :], W=[xb[0][c]])
        for c in range(8):
            wl = [xb[1 + b][c] for b in range(NRB)]
            S.dma("sp", x_t[:, c, NMETA:L], xT_d[c * 128:(c + 1) * 128, :], W=wl, sembuf=wl[0])
        S.op("pool", lambda e: e.memset(ones_t[:], 1.0), W=[ones_b])
        S.op("pool", lambda e: e.memset(eps_t[:], EPS), W=[eps_b])
        S.op("dve", lambda e: e.tensor_copy(out=identb_t[:], in_=identf), R=[cst_b], W=[identb_b])
        S.op("dve", lambda e: e.tensor_copy(out=prot_t[:], in_=cst_t[:, 128:224]), R=[cst_b], W=[prot_b])

        blocks = [(0, NMETA)] + [(NMETA + NB * b, NB) for b in range(NRB)]
        ktiles = [(0, NMETA)] + [(NMETA + 128 * m, 128) for m in range(SEQ // 128)]

        def rms_feature(A, src_fn, src_bufs, nchunk, npart, n, dim, gcol, vt, vb, dst_fn, dst_bufs,
                        sq_t, sq_b, ln_t, ln_b, rs_t, rs_b):
            pt, pb = ps_next()
            for c in range(nchunk):
                S.op("act", lambda e, c=c: e.activation(out=sq_t[0:npart, c % 2, 0:n], in_=src_fn(c),
                                                        func=AF.Square),
                     R=[src_bufs[c]], W=[sq_b[c % 2]])
                S.op("pe", lambda e, c=c: e.matmul(pt[0:npart, 0:n], lhsT=ones_t[0:npart, 0:npart],
                                                   rhs=sq_t[0:npart, c % 2, 0:n],
                                                   start=(c == 0), stop=(c == nchunk - 1)),
                     R=[ones_b, sq_b[c % 2]], W=[pb])
            S.op("act", lambda e: e.activation(out=ln_t[0:npart, 0:n], in_=pt[0:npart, 0:n], func=AF.Ln,
                                               bias=eps_t[0:npart, :], scale=1.0 / dim),
                 R=[pb, eps_b], W=[ln_b])
            S.op("act", lambda e: e.activation(out=rs_t[0:npart, 0:n], in_=ln_t[0:npart, 0:n], func=AF.Exp,
                                               scale=-0.5),
                 R=[ln_b], W=[rs_b])
            for c in range(nchunk):
                S.op("dve", lambda e, c=c: e.scalar_tensor_tensor(
                    out=dst_fn(c), in0=src_fn(c), scalar=vt[0:npart, gcol + c:gcol + c + 1],
                    in1=rs_t[0:npart, 0:n], op0=ALU.mult, op1=ALU.mult),
                    R=[src_bufs[c], vb, rs_b], W=[dst_bufs[c]])

        with contextlib.ExitStack() as stw:
            Aw = A0.sub(stw)
            ws_t, _ = Aw.sb([128, 8, 1024], BF16, "wstg")
            wsb = [Buf("wstg%d" % i) for i in range(8)]
            for i in range(n_layers * NWT):
                l_, ti = divmod(i, NWT)
                sbi = i % 8
                S.dma("pool", ws_t[:, sbi, :], wst_d[l_, ti], W=[wsb[sbi]])
                S.dma("sp", wbf_d[l_, ti], ws_t[:, sbi, :], R=[wsb[sbi]], sembuf=wsb[sbi])
            S.barrier(wsb)

        for l in range(n_layers):
            vt = vecs_t[l]
            vb = vecs_b[l]
            cstate = {"i": 0, "stg": None}
            NCV = 2 * (NEXP // 256)

            def conv_step():
                i = cstate["i"]
                if i >= NCV:
                    return
                cstate["i"] += 1
                stg_t, stgb = cstate["stg"]
                src = (pu_d[l], pv_d[l])[i % 2]
                dst = uv_d[l][:, (i % 2) * D:(i % 2 + 1) * D]
                r0 = (i // 2) * 256
                sb_ = i % 2
                S.dma("pool", stg_t[:, sb_, :, :], src[r0:r0 + 256, :].rearrange("(p j) d -> p j d", j=2),
                      W=[stgb[sb_]])
                S.dma("sp", dst[r0:r0 + 256, :].rearrange("(p j) d -> p j d", j=2), stg_t[:, sb_, :, :],
                      R=[stgb[sb_]], sembuf=stgb[sb_])
            if do_mixer:
              with contextlib.ExitStack() as stm:
                A = A0.sub(stm)
                kst_t, _ = A.sb([128, 8, L], BF16, "kst")
                kstb = [[Buf("k%d_%d" % (b, h)) for h in range(8)] for b in range(1 + NRB)]
                vst_t, _ = A.sb([128, NKT, 512], BF16, "vst")
                vstb = [Buf("v%d" % i) for i in range(NKT)]
                ring_t, _ = A.sb([128, NSLOT, 1024], BF16, "ring")
                ringb = [Buf("ring%d" % i) for i in range(NSLOT)]
                wuq_t, wuq_b = A.sb([128, 2 * 768], BF16, "wuq")
                wkn_t, wkn_b = A.sb([128, 512], BF16, "wkn")
                wv_t, wv_b = A.sb([128, 512], BF16, "wv")
                cos_t, cos_b = A.sb([128, NB], F32, "cos")
                sin_t, sin_b = A.sb([128, NB], F32, "sin")
                hT_t, _ = A.sb([128, 8, NB], BF16, "hT")
                hTb = [Buf("hT%d" % c) for c in range(8)]
                sq_t, _ = A.sb([128, 2, NB], BF16, "sq")
                sqb = [Buf("sq0"), Buf("sq1")]
                ln_t, ln_b = A.sb([128, NB], F32, "ln")
                rs_t, rs_b = A.sb([128, NB], F32, "rs")
                sig_t, _ = A.sb([128, 2, NB], F32, "sig")
                sigb = [Buf("sig0"), Buf("sig1")]
                ub_t, _ = A.sb([128, 4, 30 + NB], F32, "ubuf")
                ubb = [Buf("ub%d" % c) for c in range(4)]
                halo_t, _ = A.sb([128, 4, 30], F32, "halo")
                halob = [Buf("halo%d" % c) for c in range(4)]
                y_t, _ = A.sb([128, 4, NB], F32, "y")
                yb = [Buf("y%d" % c) for c in range(4)]
                yh_t, _ = A.sb([128, 4, NB], BF16, "yh")
                yhb = [Buf("yh%d" % c) for c in range(4)]
                ysq_t, _ = A.sb([128, 4, NB], BF16, "ysq")
                ysqb = [Buf("ysq%d" % c) for c in range(4)]
                mu_t, mu_b = A.sb([128, NB], F32, "mu")
                var_t, var_b = A.sb([128, NB], F32, "var")
                actc_t, _ = A.sb([128, 4, NB], BF16, "actc")
                actcb = [Buf("actc%d" % c) for c in range(4)]
                cq_t, _ = A.sb([128, 2, NB], F32, "cq")
                cqb = [Buf("cq0"), Buf("cq1")]
                cqn_t, _ = A.sb([128, 2, NB], BF16, "cqn")
                cqnb = [Buf("cqn0"), Buf("cqn1")]
                ckv_t, ckv_b = A.sb([128, NB], F32, "ckv")
                ckvn_t, ckvn_b = A.sb([128, NB], BF16, "ckvn")
                hc4_t, hc4_b = A.sb([128, 4 * NB], F32, "hc4")
                sq4_t, sq4_b = A.sb([128, 4 * NB], BF16, "sq4")
                r4_t, r4_b = A.sb([128, 4 * NB], F32, "r4")
                t24_t, t24_b = A.sb([128, 4 * NB], BF16, "t24")
                hnh4_t, hnh4_b = A.sb([128, 4 * NB], BF16, "hnh4")
                q_t, _ = A.sb([128, 8, NB], BF16, "qblk")
                qb = [Buf("q%d" % h) for h in range(8)]
                pT_t, _ = A.sb([128, 4, NB], BF16, "pT")
                pTb = [Buf("pT%d" % i) for i in range(4)]
                rden_t, _ = A.sb([64, 2, NB], F32, "rden")
                rdenb = [Buf("rden0"), Buf("rden1")]
                oT_t, _ = A.sb([64, 8, NB], BF16, "oT")
                oTb = [Buf("oT%d" % h) for h in range(8)]
                m1_t, _ = A.sb([128, 2, NB], F32, "m1")
                m1b = [Buf("m1_0"), Buf("m1_1")]
                mg_t, _ = A.sb([128, 8, NB], BF16, "merged")
                mgb = [Buf("mg%d" % c) for c in range(8)]

                stg_t, _ = A.sb([128, 2, 2, D], BF16, "stg")
                stgb_m = [Buf("stg0"), Buf("stg1")]
                cstate["stg"] = (stg_t, stgb_m)
                S.dma("pool", wuq_t[:], wuq_d[l], W=[wuq_b])
                S.dma("pool", wkn_t[:], wkn_d[l], W=[wkn_b])
                S.dma("pool", wv_t[:], wv_d[l], W=[wv_b])
                for c in range(4):
                    S.op("pool", lambda e, c=c: e.memset(halo_t[:, c, :], 0.0), W=[halob[c]])

                wstate = {"issued": 0, "used": 0}
                total_tiles = NWT * len(blocks)

                def w_issue():
                    i = wstate["issued"]
                    if i >= total_tiles:
                        return
                    s = i % NSLOT
                    S.dma("sp", ring_t[:, s, :], wbf_d[l, i % NWT], W=[ringb[s]])
                    wstate["issued"] += 1

                def w_take():
                    i = wstate["used"]
                    wstate["used"] += 1
                    s = i % NSLOT
                    return ring_t[:, s, :], ringb[s]

                for _ in range(NSLOT):
                    w_issue()

                for bi, (c0, n) in enumerate(blocks):
                    xs = lambda c: x_t[:, c, c0:c0 + n]
                    S.dma("sp", cos_t[64:96, 0:n], cos_d[64:96, c0:c0 + n], W=[cos_b])
                    S.dma("sp", sin_t[64:96, 0:n], sin_d[64:96, c0:c0 + n], W=[sin_b])
                    rms_feature(A, xs, xb[bi], 8, 128, n, float(D), V_MIXG, vt, vb,
                                lambda c: hT_t[:, c, 0:n], hTb, sq_t, sqb, ln_t, ln_b, rs_t, rs_b)

                    def zproj(M):
                        wt, wb = w_take()
                        pt, pb = ps_next()
                        for kc in range(8):
                            S.op("pe", lambda e, kc=kc: e.matmul(pt[0:M, 0:n], lhsT=wt[:, kc * 128:kc * 128 + M],
                                                                 rhs=hT_t[:, kc, 0:n],
                                                                 start=(kc == 0), stop=(kc == 7)),
                                 R=[wb, hTb[kc]], W=[pb])
                        w_issue()
                        return pt, pb

                    for c in range(4):
                        pa, pab = zproj(128)
                        pg, pgb = zproj(128)
                        sg = c % 2
                        S.op("act", lambda e: e.activation(out=sig_t[:, sg, 0:n], in_=pg[:, 0:n], func=AF.Sigmoid),
                             R=[pgb], W=[sigb[sg]])
                        S.op("pool", lambda e, c=c: e.tensor_copy(out=ub_t[:, c, 0:30], in_=halo_t[:, c, :]),
                             R=[halob[c]], W=[ubb[c]])
                        S.op("dve", lambda e, c=c: e.tensor_tensor(out=ub_t[:, c, 30:30 + n], in0=pa[:, 0:n],
                                                                   in1=sig_t[:, sg, 0:n], op=ALU.mult),
                             R=[pab, sigb[sg]], W=[ubb[c]])
                        S.op("pool", lambda e, c=c: e.tensor_copy(out=halo_t[:, c, :], in_=ub_t[:, c, n:n + 30]),
                             R=[ubb[c]], W=[halob[c]])
                    for c in range(2):
                        pq, pqb = zproj(128)
                        S.op("act", lambda e, c=c: e.activation(out=cq_t[:, c, 0:n], in_=pq[:, 0:n], func=AF.Copy),
                             R=[pqb], W=[cqb[c]])
                    pk, pkb = zproj(128)
                    S.op("act", lambda e: e.activation(out=ckv_t[:, 0:n], in_=pk[:, 0:n], func=AF.Copy),
                         R=[pkb], W=[ckv_b])
                    pr, prb = zproj(96)
                    kr_t, kr_b = ln_t, ln_b
                    if "krope" not in wstate:
                        wstate["krope"] = A.sb([128, NB], F32, "krope")
                    kr_t, kr_b = wstate["krope"]
                    S.op("act", lambda e: e.activation(out=kr_t[64:96, 0:n], in_=pr[64:96, 0:n], func=AF.Copy),
                         R=[prb], W=[kr_b])
                    rms_feature(A, lambda c: cq_t[:, c, 0:n], cqb, 2, 128, n, 256.0, V_QAG, vt, vb,
                                lambda c: cqn_t[:, c, 0:n], cqnb, sq_t, sqb, ln_t, ln_b, rs_t, rs_b)
                    rms_feature(A, lambda c: ckv_t[:, 0:n], [ckv_b], 1, 128, n, 128.0, V_KVAG, vt, vb,
                                lambda c: ckvn_t[:, 0:n], [ckvn_b], sq_t, sqb, ln_t, ln_b, rs_t, rs_b)

                    conv_thunks = []
                    for c in range(4):
                        conv_thunks.append(lambda c=c: S.op("dve", lambda e: e.tensor_scalar(
                            out=y_t[:, c, 0:n], in0=ub_t[:, c, 0:n],
                            scalar1=vt[:, V_CONVW + c * 31:V_CONVW + c * 31 + 1],
                            scalar2=vt[:, V_CONVB + c:V_CONVB + c + 1], op0=ALU.mult, op1=ALU.add),
                            R=[ubb[c], vb], W=[yb[c]]))
                        for k in range(1, 31):
                            conv_thunks.append(lambda c=c, k=k: S.op("dve", lambda e: e.scalar_tensor_tensor(
                                out=y_t[:, c, 0:n], in0=ub_t[:, c, k:k + n],
                                scalar=vt[:, V_CONVW + c * 31 + k:V_CONVW + c * 31 + k + 1],
                                in1=y_t[:, c, 0:n], op0=ALU.mult, op1=ALU.add),
                                R=[ubb[c], vb, yb[c]], W=[yb[c]]))
                    bkt = [0] if bi == 0 else [1 + 2 * (bi - 1), 2 + 2 * (bi - 1)]
                    for j, kt in enumerate(bkt):
                        kc0, nk = ktiles[kt]
                        off = kc0 - c0
                        pv_, pvb_ = ps_next()
                        S.op("pe", lambda e: e.matmul(pv_[0:nk, 0:512], lhsT=ckvn_t[:, off:off + nk], rhs=wv_t[:],
                                                      start=True, stop=True),
                             R=[ckvn_b, wv_b], W=[pvb_])
                        S.op("act", lambda e: e.activation(out=vst_t[0:nk, kt, :], in_=pv_[0:nk, 0:512], func=AF.Copy),
                             R=[pvb_], W=[vstb[kt]])

                    W4 = 4 * n

                    def v3(ap):
                        return ap.rearrange("p (h n) -> p h n", h=4)

                    def chunks512():
                        return [(o, min(512, W4 - o)) for o in range(0, W4, 512)]

                    def head_batch(kind, h0):
                        PP, PPb = ps_pair()
                        if kind == "k":
                            for hh in range(4):
                                h = h0 + hh
                                S.op("pe", lambda e: e.matmul(PP[0:64, hh * n:(hh + 1) * n], lhsT=wkn_t[:, h * 64:(h + 1) * 64],
                                                              rhs=ckvn_t[:, 0:n], start=True, stop=True),
                                     R=[wkn_b, ckvn_b], W=PPb)
                            S.op("act", lambda e: e.activation(out=hc4_t[0:64, 0:W4], in_=PP[0:64, 0:W4], func=AF.Copy),
                                 R=PPb, W=[hc4_b])
                            S.op("pool", lambda e: e.tensor_copy(
                                out=v3(hc4_t[64:96, 0:W4]), in_=kr_t[64:96, 0:n].unsqueeze(1).broadcast_to([32, 4, n])),
                                R=[kr_b], W=[hc4_b])
                            gcol = V_KNG
                            dst = lambda r0, r1: kst_t[r0:r1, h0:h0 + 4, c0:c0 + n]
                            dstb = [kstb[bi][h0 + i] for i in range(4)]
                        else:
                            for hh in range(4):
                                h = h0 + hh
                                for kc in range(2):
                                    S.op("pe", lambda e: e.matmul(
                                        PP[0:96, hh * n:(hh + 1) * n],
                                        lhsT=wuq_t[:, kc * 768 + h * 96:kc * 768 + (h + 1) * 96],
                                        rhs=cqn_t[:, kc, 0:n], start=(kc == 0), stop=(kc == 1)),
                                        R=[wuq_b, cqnb[kc]], W=PPb)
                            S.op("act", lambda e: e.activation(out=hc4_t[0:96, 0:W4], in_=PP[0:96, 0:W4], func=AF.Copy),
                                 R=PPb, W=[hc4_b])
                            gcol = V_QNG
                            dst = lambda r0, r1: q_t[r0:r1, h0:h0 + 4, 0:n]
                            dstb = [qb[h0 + i] for i in range(4)]
                        S.op("act", lambda e: e.activation(out=sq4_t[0:96, 0:W4], in_=hc4_t[0:96, 0:W4], func=AF.Square),
                             R=[hc4_b], W=[sq4_b])
                        PQ, PQb = ps_pair()
                        for (o, w) in chunks512():
                            S.op("pe", lambda e: e.matmul(PQ[0:96, o:o + w], lhsT=ones_t[0:96, 0:96], rhs=sq4_t[0:96, o:o + w],
                                                          start=True, stop=True),
                                 R=[ones_b, sq4_b], W=PQb)
                        S.op("act", lambda e: e.activation(out=r4_t[0:96, 0:W4], in_=PQ[0:96, 0:W4], func=AF.Ln,
                                                           bias=eps_t[0:96, :], scale=1.0 / 96),
                             R=PQb + [eps_b], W=[r4_b])
                        S.op("act", lambda e: e.activation(out=r4_t[0:96, 0:W4], in_=r4_t[0:96, 0:W4], func=AF.Exp, scale=-0.5),
                             R=[r4_b], W=[r4_b])
                        S.op("dve", lambda e: e.scalar_tensor_tensor(
                            out=hc4_t[0:96, 0:W4], in0=hc4_t[0:96, 0:W4], scalar=vt[0:96, gcol:gcol + 1],
                            in1=r4_t[0:96, 0:W4], op0=ALU.mult, op1=ALU.mult),
                            R=[hc4_b, vb, r4_b], W=[hc4_b])
                        S.op("act", lambda e: e.activation(out=dst(0, 64), in_=v3(hc4_t[0:64, 0:W4]), func=AF.Copy),
                             R=[hc4_b], W=dstb)
                        S.op("dve", lambda e: e.tensor_copy(out=hnh4_t[64:96, 0:W4], in_=hc4_t[64:96, 0:W4]),
                             R=[hc4_b], W=[hnh4_b])
                        PR, PRb = ps_pair()
                        for (o, w) in chunks512():
                            S.op("pe", lambda e: e.matmul(PR[0:96, o:o + w], lhsT=prot_t[64:96, 0:96], rhs=hnh4_t[64:96, o:o + w],
                                                          start=True, stop=True),
                                 R=[prot_b, hnh4_b], W=PRb)
                        S.op("dve", lambda e: e.tensor_tensor(
                            out=v3(t24_t[64:96, 0:W4]), in0=v3(PR[64:96, 0:W4]),
                            in1=sin_t[64:96, 0:n].unsqueeze(1).broadcast_to([32, 4, n]), op=ALU.mult),
                            R=PRb + [sin_b], W=[t24_b])
                        S.op("dve", lambda e: e.tensor_tensor(
                            out=v3(r4_t[64:96, 0:W4]), in0=v3(hc4_t[64:96, 0:W4]),
                            in1=cos_t[64:96, 0:n].unsqueeze(1).broadcast_to([32, 4, n]), op=ALU.mult),
                            R=[hc4_b, cos_b, r4_b], W=[r4_b])
                        S.op("dve", lambda e: e.tensor_tensor(out=dst(64, 96), in0=v3(r4_t[64:96, 0:W4]),
                                                              in1=v3(t24_t[64:96, 0:W4]), op=ALU.add),
                             R=[r4_b, t24_b], W=dstb)

                    for h0 in (0, 4):
                        head_batch("k", h0)
                        head_batch("q", h0)

                    if bi == 0:
                        vis = [(0, 0, False)]
                    else:
                        B = bi - 1
                        vis = [(0, 0, False)] + [(1 + m, 0, False) for m in range(2 * B)]
                        vis += [(1 + 2 * B + j, 128 * j, True) for j in range(2)]
                    pti = [0]
                    for h in range(8):
                        for _ in range(16):
                            if conv_thunks:
                                conv_thunks.pop(0)()
                        conv_step()
                        conv_step()
                        pnum, pnumb = pst[2 * (h % 2)], psb[2 * (h % 2)]
                        pden, pdenb = pst[2 * (h % 2) + 1], psb[2 * (h % 2) + 1]
                        pend = None
                        for vi in range(len(vis) + 1):
                            cur = None
                            if vi < len(vis):
                                kt, q0, diag = vis[vi]
                                kc0, nk = ktiles[kt]
                                kbi = 0 if kt == 0 else 1 + (kt - 1) // 2
                                sps, spsb = ps_next()
                                S.op("pe", lambda e: e.matmul(sps[0:nk, q0:n], lhsT=kst_t[0:96, h, kc0:kc0 + nk],
                                                              rhs=q_t[0:96, h, q0:n], start=True, stop=True),
                                     R=[kstb[kbi][h], qb[h]], W=[spsb])
                                pi = pti[0] % 4
                                pti[0] += 1
                                S.op("act", lambda e: e.activation(out=pT_t[0:nk, pi, q0:n], in_=sps[0:nk, q0:n],
                                                                   func=AF.Exp, scale=SCALE),
                                     R=[spsb], W=[pTb[pi]])
                                if diag:
                                    S.op("pool", lambda e: e.memset(pT_t[64:128, pi, q0:q0 + 64], 0.0), W=[pTb[pi]])
                                cur = (kt, q0, nk, pi, vi)
                            if pend is not None:
                                kt_, q0_, nk_, pi_, vi_ = pend
                                first = (vi_ == 0)
                                last = (vi_ == len(vis) - 1)
                                S.op("pe", lambda e: e.matmul(pnum[0:64, q0_:n], lhsT=vst_t[0:nk_, kt_, h * 64:(h + 1) * 64],
                                                              rhs=pT_t[0:nk_, pi_, q0_:n], start=first, stop=last),
                                     R=[vstb[kt_], pTb[pi_]], W=[pnumb])
                                S.op("pe", lambda e: e.matmul(pden[0:64, q0_:n], lhsT=ones_t[0:nk_, 0:64],
                                                              rhs=pT_t[0:nk_, pi_, q0_:n], start=first, stop=last),
                                     R=[ones_b, pTb[pi_]], W=[pdenb])
                            pend = cur
                        rb = h % 2
                        S.op("dve", lambda e: e.reciprocal(out=rden_t[0:64, rb, 0:n], in_=pden[0:64, 0:n]),
                             R=[pdenb], W=[rdenb[rb]])
                        S.op("dve", lambda e, h=h: e.tensor_tensor(out=oT_t[0:64, h, 0:n], in0=pnum[0:64, 0:n],
                                                                   in1=rden_t[0:64, rb, 0:n], op=ALU.mult),
                             R=[pnumb, rdenb[rb]], W=[oTb[h]])

                    while conv_thunks:
                        conv_thunks.pop(0)()
                    p1, p1b = ps_next()
                    p2, p2b = ps_next()
                    for c in range(4):
                        S.op("act", lambda e, c=c: e.activation(out=yh_t[:, c, 0:n], in_=y_t[:, c, 0:n], func=AF.Copy),
                             R=[yb[c]], W=[yhb[c]])
                        S.op("act", lambda e, c=c: e.activation(out=ysq_t[:, c, 0:n], in_=y_t[:, c, 0:n], func=AF.Square),
                             R=[yb[c]], W=[ysqb[c]])
                    for c in range(4):
                        S.op("pe", lambda e, c=c: e.matmul(p1[:, 0:n], lhsT=ones_t[:], rhs=yh_t[:, c, 0:n],
                                                           start=(c == 0), stop=(c == 3)),
                             R=[ones_b, yhb[c]], W=[p1b])
                    for c in range(4):
                        S.op("pe", lambda e, c=c: e.matmul(p2[:, 0:n], lhsT=ones_t[:], rhs=ysq_t[:, c, 0:n],
                                                           start=(c == 0), stop=(c == 3)),
                             R=[ones_b, ysqb[c]], W=[p2b])
                    S.op("dve", lambda e: e.tensor_scalar(out=mu_t[:, 0:n], in0=p1[:, 0:n], scalar1=1.0 / 512,
                                                          scalar2=None, op0=ALU.mult),
                         R=[p1b], W=[mu_b])
                    S.op("dve", lambda e: e.tensor_tensor(out=var_t[:, 0:n], in0=mu_t[:, 0:n], in1=mu_t[:, 0:n],
                                                          op=ALU.mult),
                         R=[mu_b], W=[var_b])
                    S.op("dve", lambda e: e.scalar_tensor_tensor(out=var_t[:, 0:n], in0=p2[:, 0:n], scalar=1.0 / 512,
                                                                 in1=var_t[:, 0:n], op0=ALU.mult, op1=ALU.subtract),
                         R=[p2b, var_b], W=[var_b])
                    S.op("act", lambda e: e.activation(out=ln_t[:, 0:n], in_=var_t[:, 0:n], func=AF.Ln,
                                                       bias=eps_t[:, :], scale=1.0),
                         R=[var_b, eps_b], W=[ln_b])
                    S.op("act", lambda e: e.activation(out=rs_t[:, 0:n], in_=ln_t[:, 0:n], func=AF.Exp, scale=-0.5),
                         R=[ln_b], W=[rs_b])
                    for c in range(4):
                        S.op("dve", lambda e, c=c: e.tensor_tensor(out=y_t[:, c, 0:n], in0=y_t[:, c, 0:n],
                                                                   in1=mu_t[:, 0:n], op=ALU.subtract),
                             R=[yb[c], mu_b], W=[yb[c]])
                        S.op("dve", lambda e, c=c: e.tensor_tensor(out=y_t[:, c, 0:n], in0=y_t[:, c, 0:n],
                                                                   in1=rs_t[:, 0:n], op=ALU.mult),
                             R=[yb[c], rs_b], W=[yb[c]])
                        S.op("act", lambda e, c=c: e.activation(
                            out=actc_t[:, c, 0:n], in_=y_t[:, c, 0:n], func=AF.Silu,
                            bias=vt[:, V_LNB + c:V_LNB + c + 1], scale=vt[:, V_LNG + c:V_LNG + c + 1]),
                            R=[yb[c], vb], W=[actcb[c]])

                    for c in range(8):
                        pg1, pg1b = zproj(128)
                        S.op("act", lambda e: e.activation(out=sig_t[:, 0, 0:n], in_=pg1[:, 0:n], func=AF.Sigmoid),
                             R=[pg1b], W=[sigb[0]])
                        wt, wb = w_take()
                        pc, pcb = ps_next()
                        for kc in range(4):
                            S.op("pe", lambda e, kc=kc: e.matmul(pc[:, 0:n], lhsT=wt[:, kc * 128:(kc + 1) * 128],
                                                                 rhs=actc_t[:, kc, 0:n], start=(kc == 0), stop=(kc == 3)),
                                 R=[wb, actcb[kc]], W=[pcb])
                        w_issue()
                        S.op("dve", lambda e: e.tensor_tensor(out=m1_t[:, 0, 0:n], in0=pc[:, 0:n], in1=sig_t[:, 0, 0:n],
                                                              op=ALU.mult),
                             R=[pcb, sigb[0]], W=[m1b[0]])
                        pg2, pg2b = zproj(128)
                        S.op("act", lambda e: e.activation(out=sig_t[:, 1, 0:n], in_=pg2[:, 0:n], func=AF.Sigmoid),
                             R=[pg2b], W=[sigb[1]])
                        wt2, wb2 = w_take()
                        pm, pmb = ps_next()
                        for h in range(8):
                            S.op("pe", lambda e, h=h: e.matmul(pm[:, 0:n], lhsT=wt2[0:64, h * 128:(h + 1) * 128],
                                                               rhs=oT_t[0:64, h, 0:n], start=(h == 0), stop=(h == 7)),
                                 R=[wb2, oTb[h]], W=[pmb])
                        w_issue()
                        S.op("dve", lambda e: e.tensor_tensor(out=m1_t[:, 1, 0:n], in0=pm[:, 0:n], in1=sig_t[:, 1, 0:n],
                                                              op=ALU.mult),
                             R=[pmb, sigb[1]], W=[m1b[1]])
                        S.op("dve", lambda e, c=c: e.tensor_tensor(out=mg_t[:, c, 0:n], in0=m1_t[:, 0, 0:n],
                                                                   in1=m1_t[:, 1, 0:n], op=ALU.add),
                             R=[m1b[0], m1b[1]], W=[mgb[c]])
                    for c in range(8):
                        wt, wb = w_take()
                        po, pob = ps_next()
                        for kc in range(8):
                            S.op("pe", lambda e, kc=kc: e.matmul(po[:, 0:n], lhsT=wt[:, kc * 128:(kc + 1) * 128],
                                                                 rhs=mg_t[:, kc, 0:n], start=(kc == 0), stop=(kc == 7)),
                                 R=[wb, mgb[kc]], W=[pob])
                        w_issue()
                        S.op("dve", lambda e, c=c: e.tensor_tensor(out=x_t[:, c, c0:c0 + n], in0=x_t[:, c, c0:c0 + n],
                                                                   in1=po[:, 0:n], op=ALU.add),
                             R=[pob, xb[bi][c]], W=[xb[bi][c]])
                while cstate["i"] < NCV:
                    conv_step()
                S.barrier(stgb_m)

            if do_peer:
              with contextlib.ExitStack() as stp:
                A = A0.sub(stp)
                wq_t, wq_b = A.sb([128, 8 * 2048], BF16, "wq")
                ky_t, ky_b = A.sb([128, 16 * 128], BF16, "keysT")
                hT_t, _ = A.sb([128, 8, 128], BF16, "h2T")
                hTb = [Buf("h2T%d" % c) for c in range(8)]
                sq_t, _ = A.sb([128, 2, 128], BF16, "sq")
                sqb = [Buf("sq0"), Buf("sq1")]
                ln_t, ln_b = A.sb([128, 128], F32, "ln")
                rs_t, rs_b = A.sb([128, 128], F32, "rs")
                htok_t, htok_b = A.sb([128, D], BF16, "h2tok")
                qT_t, _ = A.sb([128, 16, 128], BF16, "qT")
                qTb = [Buf("qT%d" % g) for g in range(4)]
                s_t, s_b = A.sb([128, 16, 128], F32, "s")
                s2_t, s2_b = A.sb([128, 16, 128], F32, "s2")
                sv_t, sv_b = A.sb([128, 16, 16], F32, "sv")
                si_t, si_b = A.sb([128, 16, 16], U32, "si")
                sif_t, sif_b = A.sb([128, 16, 16], F32, "sif")
                cand_t, cand_b = A.sb([128, 8, 256], F32, "cand")
                cand2_t, cand2_b = s2_t[:, :, :].rearrange("p g n -> p (g n)").rearrange("p (h c) -> p h c", h=8), s2_b
                top_t, top_b = A.sb([128, 8, 16], F32, "top")
                pos_t, pos_b = A.sb([128, 8, 16], U32, "pos")
                ai_t, ai_b = A.sb([128, 8, 16], U32, "ai")
                bi_t, bi_b = A.sb([128, 8, 16], U32, "bi")
                af_t, af_b = A.sb([128, 8, 16], F32, "af")
                bf_t, bf_b = A.sb([128, 8, 16], F32, "bf")
                oh_t, oh_b = s_t[:, :, :].rearrange("p g n -> p (g n)").rearrange("p (h c) -> p h c", h=8), s_b
                tmp_t, tmp_b = cand_t, cand_b
                isel_t, isel_b = A.sb([128, 128], F32, "isel")
                jsel_t, jsel_b = A.sb([128, 128], F32, "jsel")
                eidf_t, eidf_b = A.sb([128, 128], F32, "eidf")
                eid_t, eid_b = A.sb([128, 128], I32, "eid")
                ew_t, ew_b = A.sb([128, 128], F32, "ew")
                ssum_t, ssum_b = A.sb([128, 8], F32, "ssum")
                gw_t, gw_b = A.sb([128, 128], F32, "gw")
                a_t, a_b = A.sb([128, 128], F32, "a")
                g1_t, g1_b = A.sb([128, 128], F32, "g1")
                g2_t, g2_b = A.sb([128, 128], F32, "g2")
                actw_t, actw_b = A.sb([128, 128], F32, "actw")
                junk_t, junk_b = A.sb([128, D], BF16, "junk")
                gb_t, _ = A.sb([128, NG, 2 * D], BF16, "gbuf")
                gbb = [Buf("gb%d" % i) for i in range(NG)]
                acc_t, acc_b = gb_t[:, 0, :].bitcast(F32), gbb[0]
                dg_t, _ = A.sb([128, 4, 128], BF16, "diag")
                dgb = [Buf("dg%d" % i) for i in range(4)]
                g1b = [Buf("g1_%d" % i) for i in range(32)]
                g2b = [Buf("g2_%d" % i) for i in range(32)]
                awb = [Buf("aw_%d" % i) for i in range(32)]
                if cstate["i"] < NCV:
                    stg_t, _ = A.sb([128, 2, 2, D], BF16, "stg")
                    stgb_p = [Buf("stg0"), Buf("stg1")]
                    cstate["stg"] = (stg_t, stgb_p)
                    while cstate["i"] < NCV:
                        conv_step()
                    S.barrier(stgb_p)

                posf_t, posf_b = A.sb([128, 8, 16], F32, "posf")
                thr_t, thr_b = A.sb([128, 16], F32, "thr")
                S.op("dve", lambda e: e.tensor_scalar(out=thr_t[:], in0=iota16, scalar1=16.0, scalar2=16.0,
                                                      op0=ALU.mult, op1=ALU.add), R=[cst_b], W=[thr_b])
                S.dma("pool", wq_t[:], wq_d[l], W=[wq_b])
                S.dma("pool", ky_t[:], keys_d[l], W=[ky_b])
                gi = [0]
                ptiles = list(range(NKT))
                if l == n_layers - 1:
                    ptiles = ptiles[1:]
                for tt in ptiles:
                    c0, np_ = ktiles[tt]
                    bi = 0 if tt == 0 else 1 + (tt - 1) // 2
                    xs = lambda c: x_t[:, c, c0:c0 + np_]
                    rms_feature(A, xs, xb[bi], 8, 128, np_, float(D), V_FFNG, vt, vb,
                                lambda c: hT_t[:, c, 0:np_], hTb, sq_t, sqb, ln_t, ln_b, rs_t, rs_b)
                    ptk = pst[0][:, :].bitcast(BF16)
                    for c in range(8):
                        S.op("pe", lambda e, c=c: e.transpose(out=ptk[0:np_, c * 128:(c + 1) * 128],
                                                              in_=hT_t[:, c, 0:np_], identity=identb_t[:]),
                             R=[hTb[c], identb_b], W=[psb[0]])
                    S.op("act", lambda e: e.activation(out=htok_t[0:np_, :], in_=ptk[0:np_, :], func=AF.Copy),
                         R=[psb[0]], W=[htok_b])
                    for gq in range(4):
                        pq, pqb = ps_next()
                        for gg in range(4):
                            g = gq * 4 + gg
                            for kc in range(8):
                                S.op("pe", lambda e, g=g, gg=gg, kc=kc: e.matmul(
                                    pq[:, gg * 128:gg * 128 + np_],
                                    lhsT=wq_t[:, kc * 2048 + g * 128:kc * 2048 + (g + 1) * 128],
                                    rhs=hT_t[:, kc, 0:np_], start=(kc == 0), stop=(kc == 7)),
                                    R=[wq_b, hTb[kc]], W=[pqb])
                        S.op("act", lambda e, gq=gq: e.activation(
                            out=qT_t[:, gq * 4:(gq + 1) * 4, 0:np_],
                            in_=pq[:, :].rearrange("p (g t) -> p g t", g=4)[:, :, 0:np_], func=AF.Copy),
                            R=[pqb], W=[qTb[gq]])
                    for gq in range(4):
                        pss, pssb = ps_next()
                        for gg in range(4):
                            g = gq * 4 + gg
                            S.op("pe", lambda e, g=g, gg=gg: e.matmul(
                                pss[0:np_, gg * 128:(gg + 1) * 128], lhsT=qT_t[:, g, 0:np_],
                                rhs=ky_t[:, g * 128:(g + 1) * 128], start=True, stop=True),
                                R=[qTb[gq], ky_b], W=[pssb])
                        S.op("act", lambda e, gq=gq: e.activation(
                            out=s_t[0:np_, gq * 4:(gq + 1) * 4, :],
                            in_=pss[0:np_, :].rearrange("p (g n) -> p g n", g=4), func=AF.Copy),
                            R=[pssb], W=[s_b])
                    P = np_
                    for g in range(16):
                        S.op("dve", lambda e, g=g: e.max(out=sv_t[0:P, g, 0:8], in_=s_t[0:P, g, :]), R=[s_b], W=[sv_b])
                    for g in range(16):
                        S.op("dve", lambda e, g=g: e.max_index(out=si_t[0:P, g, 0:8], in_max=sv_t[0:P, g, 0:8],
                                                               in_values=s_t[0:P, g, :]), R=[s_b, sv_b], W=[si_b])
                    for g in range(16):
                        S.op("dve", lambda e, g=g: e.match_replace(out=s2_t[0:P, g, :], in_to_replace=sv_t[0:P, g, 0:8],
                                                                   in_values=s_t[0:P, g, :], imm_value=-1e30),
                             R=[s_b, sv_b], W=[s2_b])
                    for g in range(16):
                        S.op("dve", lambda e, g=g: e.max(out=sv_t[0:P, g, 8:16], in_=s2_t[0:P, g, :]), R=[s2_b], W=[sv_b])
                    for g in range(16):
                        S.op("dve", lambda e, g=g: e.max_index(out=si_t[0:P, g, 8:16], in_max=sv_t[0:P, g, 8:16],
                                                               in_values=s2_t[0:P, g, :]), R=[s2_b, sv_b], W=[si_b])
                    S.op("dve", lambda e: e.tensor_copy(out=sif_t[0:P], in_=si_t[0:P]), R=[si_b], W=[sif_b])
                    cand4 = cand_t[0:P].rearrange("p h (a b) -> p h a b", a=16)
                    S.op("dve", lambda e: e.tensor_tensor(
                        out=cand4, in0=sv_t[0:P, 0::2, :].unsqueeze(3).broadcast_to([P, 8, 16, 16]),
                        in1=sv_t[0:P, 1::2, :].unsqueeze(2).broadcast_to([P, 8, 16, 16]), op=ALU.add),
                        R=[sv_b], W=[cand_b])
                    for h in range(8):
                        S.op("dve", lambda e, h=h: e.max(out=top_t[0:P, h, 0:8], in_=cand_t[0:P, h, :]), R=[cand_b], W=[top_b])
                    for h in range(8):
                        S.op("dve", lambda e, h=h: e.max_index(out=pos_t[0:P, h, 0:8], in_max=top_t[0:P, h, 0:8],
                                                               in_values=cand_t[0:P, h, :]), R=[cand_b, top_b], W=[pos_b])
                    for h in range(8):
                        S.op("dve", lambda e, h=h: e.match_replace(out=cand2_t[0:P, h, :], in_to_replace=top_t[0:P, h, 0:8],
                                                                   in_values=cand_t[0:P, h, :], imm_value=-1e30),
                             R=[cand_b, top_b], W=[cand2_b])
                    for h in range(8):
                        S.op("dve", lambda e, h=h: e.max(out=top_t[0:P, h, 8:16], in_=cand2_t[0:P, h, :]), R=[cand2_b], W=[top_b])
                    for h in range(8):
                        S.op("dve", lambda e, h=h: e.max_index(out=pos_t[0:P, h, 8:16], in_max=top_t[0:P, h, 8:16],
                                                               in_values=cand2_t[0:P, h, :]), R=[cand2_b, top_b], W=[pos_b])
                    oh4 = oh_t[0:P].rearrange("p h (k a) -> p h k a", k=16)
                    S.op("dve", lambda e: e.tensor_copy(out=posf_t[0:P], in_=pos_t[0:P]), R=[pos_b], W=[posf_b])
                    S.op("dve", lambda e: e.tensor_tensor(
                        out=oh4, in0=posf_t[0:P].unsqueeze(3).broadcast_to([P, 8, 16, 16]),
                        in1=thr_t[0:P, :].unsqueeze(1).unsqueeze(1).broadcast_to([P, 8, 16, 16]), op=ALU.is_ge),
                        R=[posf_b, thr_b], W=[oh_b])
                    S.op("dve", lambda e: e.tensor_reduce(
                        out=af_t[0:P].rearrange("p h k -> p (h k)"),
                        in_=oh_t[0:P].rearrange("p h (k a) -> p (h k) a", k=16), axis=AX.X, op=ALU.add),
                        R=[oh_b], W=[af_b])
                    S.op("dve", lambda e: e.scalar_tensor_tensor(
                        out=bf_t[0:P].rearrange("p h k -> p (h k)"), in0=af_t[0:P].rearrange("p h k -> p (h k)"),
                        scalar=-16.0, in1=posf_t[0:P].rearrange("p h k -> p (h k)"), op0=ALU.mult, op1=ALU.add),
                        R=[af_b, posf_b], W=[bf_b])
                    oh4 = oh_t[0:P].rearrange("p h (k a) -> p h k a", k=16)
                    tmp4 = tmp_t[0:P].rearrange("p h (k a) -> p h k a", k=16)
                    io4 = iota16[0:P, :].unsqueeze(1).unsqueeze(1).broadcast_to([P, 8, 16, 16])
                    for (xf_t, xf_b, par, dst_t, dst_b) in ((af_t, af_b, 0, isel_t, isel_b),
                                                            (bf_t, bf_b, 1, jsel_t, jsel_b)):
                        S.op("dve", lambda e, xf_t=xf_t: e.tensor_tensor(
                            out=oh4, in0=xf_t[0:P].unsqueeze(3).broadcast_to([P, 8, 16, 16]), in1=io4,
                            op=ALU.is_equal), R=[xf_b, cst_b], W=[oh_b])
                        S.op("dve", lambda e, par=par: e.tensor_tensor(
                            out=tmp4, in0=oh4,
                            in1=sif_t[0:P, par::2, :].unsqueeze(2).broadcast_to([P, 8, 16, 16]), op=ALU.mult),
                            R=[oh_b, sif_b], W=[tmp_b])
                        S.op("dve", lambda e, dst_t=dst_t: e.tensor_reduce(
                            out=dst_t[0:P, :], in_=tmp_t[0:P].rearrange("p h (k a) -> p (h k) a", k=16),
                            axis=AX.X, op=ALU.add), R=[tmp_b], W=[dst_b])
                    S.op("dve", lambda e: e.scalar_tensor_tensor(out=eidf_t[0:P, :], in0=isel_t[0:P, :], scalar=128.0,
                                                                 in1=jsel_t[0:P, :], op0=ALU.mult, op1=ALU.add),
                         R=[isel_b, jsel_b], W=[eidf_b])
                    S.op("dve", lambda e: e.tensor_copy(out=eid_t[0:P, :], in_=eidf_t[0:P, :]), R=[eidf_b], W=[eid_b])
                    ew3 = ew_t[0:P, :].rearrange("p (h k) -> p h k", h=8)
                    S.op("dve", lambda e: e.tensor_tensor(out=ew3, in0=top_t[0:P],
                                                          in1=top_t[0:P, :, 0:1].broadcast_to([P, 8, 16]),
                                                          op=ALU.subtract), R=[top_b], W=[ew_b])
                    S.op("act", lambda e: e.activation(out=ew_t[0:P, :], in_=ew_t[0:P, :], func=AF.Exp),
                         R=[ew_b], W=[ew_b])
                    S.op("dve", lambda e: e.tensor_reduce(out=ssum_t[0:P, :], in_=ew3, axis=AX.X, op=ALU.add),
                         R=[ew_b], W=[ssum_b])
                    S.op("dve", lambda e: e.reciprocal(out=ssum_t[0:P, :], in_=ssum_t[0:P, :]), R=[ssum_b], W=[ssum_b])
                    S.op("dve", lambda e: e.tensor_tensor(out=gw_t[0:P, :].rearrange("p (h k) -> p h k", h=8), in0=ew3,
                                                          in1=ssum_t[0:P, :].unsqueeze(2).broadcast_to([P, 8, 16]),
                                                          op=ALU.mult), R=[ew_b, ssum_b], W=[gw_b])
                    acc0, acc0b = pst[0], psb[0]
                    acc1, acc1b = pst[1], psb[1]
                    di = [0]
                    for k in range(128):
                        gbi = gi[0] % NG
                        gi[0] += 1
                        S.dma("pool", gb_t[0:P, gbi, :], uv_d[l],
                              R=[eid_b], W=[gbb[gbi]],
                              indirect=bass.IndirectOffsetOnAxis(ap=eid_t[0:P, k:k + 1], axis=0))
                        S.op("dve", lambda e: e.scalar_tensor_tensor(
                            out=junk_t[0:P, :], in0=gb_t[0:P, gbi, 0:D], scalar=1.0, in1=htok_t[0:P, :],
                            op0=ALU.mult, op1=ALU.mult, accum_out=a_t[0:P, k:k + 1]),
                            R=[gbb[gbi], htok_b], W=[junk_b, a_b])
                        if k % 4 != 3:
                            continue
                        g = k // 4
                        sl = slice(4 * g, 4 * g + 4)
                        S.op("dve", lambda e: e.tensor_tensor(out=g1_t[0:P, sl], in0=a_t[0:P, sl], in1=a_t[0:P, sl], op=ALU.mult),
                             R=[a_b], W=[g1b[g]])
                        S.op("dve", lambda e: e.tensor_scalar(out=g1_t[0:P, sl], in0=g1_t[0:P, sl], scalar1=0.044715, scalar2=1.0,
                                                              op0=ALU.mult, op1=ALU.add), R=[g1b[g]], W=[g1b[g]])
                        S.op("dve", lambda e: e.tensor_tensor(out=g1_t[0:P, sl], in0=g1_t[0:P, sl], in1=a_t[0:P, sl], op=ALU.mult),
                             R=[g1b[g], a_b], W=[g1b[g]])
                        S.op("act", lambda e: e.activation(out=g2_t[0:P, sl], in_=g1_t[0:P, sl], func=AF.Sigmoid,
                                                           scale=1.5957691216057308), R=[g1b[g]], W=[g2b[g]])
                        S.op("dve", lambda e: e.tensor_tensor(out=g2_t[0:P, sl], in0=g2_t[0:P, sl], in1=a_t[0:P, sl], op=ALU.mult),
                             R=[g2b[g], a_b], W=[g2b[g]])
                        S.op("dve", lambda e: e.tensor_tensor(out=actw_t[0:P, sl], in0=g2_t[0:P, sl], in1=gw_t[0:P, sl], op=ALU.mult),
                             R=[g2b[g], gw_b], W=[awb[g]])
                        for kk in range(4 * g, 4 * g + 4):
                            gbk = (gi[0] - 128 + 0) % NG
                            gbk = (gi[0] - (k + 1) + kk) % NG
                            dj = di[0] % 4
                            di[0] += 1
                            S.op("act", lambda e: e.activation(out=dg_t[0:P, dj, 0:P], in_=identb_t[0:P, 0:P], func=AF.Copy,
                                                               scale=actw_t[0:P, kk:kk + 1]),
                                 R=[identb_b, awb[g]], W=[dgb[dj]])
                            S.op("pe", lambda e: e.matmul(acc0[0:P, 0:512], lhsT=dg_t[0:P, dj, 0:P], rhs=gb_t[0:P, gbk, D:D + 512],
                                                          start=(kk == 0), stop=(kk == 127)),
                                 R=[dgb[dj], gbb[gbk]], W=[acc0b])
                            S.op("pe", lambda e: e.matmul(acc1[0:P, 0:512], lhsT=dg_t[0:P, dj, 0:P], rhs=gb_t[0:P, gbk, D + 512:2 * D],
                                                          start=(kk == 0), stop=(kk == 127)),
                                 R=[dgb[dj], gbb[gbk]], W=[acc1b])
                    S.op("act", lambda e: e.activation(out=acc_t[0:P, 0:512], in_=acc0[0:P, 0:512], func=AF.Copy),
                         R=[acc0b], W=[acc_b])
                    S.op("act", lambda e: e.activation(out=acc_t[0:P, 512:1024], in_=acc1[0:P, 0:512], func=AF.Copy),
                         R=[acc1b], W=[acc_b])
                    for c in range(8):
                        ptr, ptrb = ps_next()
                        S.op("pe", lambda e, c=c: e.transpose(out=ptr[:, 0:P], in_=acc_t[0:P, c * 128:(c + 1) * 128],
                                                              identity=identf[0:P, 0:P]),
                             R=[acc_b, cst_b], W=[ptrb])
                        S.op("dve", lambda e, c=c: e.tensor_tensor(out=x_t[:, c, c0:c0 + P], in0=x_t[:, c, c0:c0 + P],
                                                                   in1=ptr[:, 0:P], op=ALU.add),
                             R=[ptrb, xb[bi][c]], W=[xb[bi][c]])
                S.barrier()

        outb = Buf("out")
        for c in range(8):
            rl = [xb[1 + b][c] for b in range(NRB)]
            S.dma("sp", out_d[c * 128:(c + 1) * 128, :], x_t[:, c, NMETA:L], R=rl, sembuf=outb)
        S.wait_all("sp", [outb] + [xb[1 + b][c] for b in range(NRB) for c in range(8)])
        nc._stats = (S.nins, S.nwait, S.ndsem, dict(S.cnt))
    return nc


def _wstream(w_in, w_conv_out, w_mla_out, w_out):
    tiles = []

    def ztile(c0, m):
        t = np.zeros((128, 8, 128), np.float32)
        t[:, :, :m] = w_in[:, c0:c0 + m].reshape(8, 128, m).transpose(1, 0, 2)
        tiles.append(t.reshape(128, 1024))

    for c in range(4):
        ztile(128 * c, 128)
        ztile(512 + 128 * c, 128)
    ztile(1024, 128)
    ztile(1152, 128)
    ztile(1280, 128)
    ztile(1344, 96)
    for c in range(8):
        ztile(1440 + 128 * c, 128)
        t = np.zeros((128, 8, 128), np.float32)
        t[:, 0:4, :] = w_conv_out[:, 128 * c:128 * (c + 1)].reshape(4, 128, 128).transpose(1, 0, 2)
        tiles.append(t.reshape(128, 1024))
        ztile(2464 + 128 * c, 128)
        t = np.zeros((128, 8, 128), np.float32)
        t[0:64, :, :] = w_mla_out[:, 128 * c:128 * (c + 1)].reshape(8, 64, 128).transpose(1, 0, 2)
        tiles.append(t.reshape(128, 1024))
    for c in range(8):
        t = w_out[:, 128 * c:128 * (c + 1)].reshape(8, 128, 128).transpose(1, 0, 2)
        tiles.append(np.ascontiguousarray(t).reshape(128, 1024))
    assert len(tiles) == NWT
    return np.stack(tiles)


def _prep_shared(inp):
    f = lambda a: np.ascontiguousarray(np.asarray(a, dtype=np.float32))
    sh = {}
    sh["wstream"] = np.stack([_wstream(f(inp["w_in"][l]), f(inp["w_conv_out"][l]), f(inp["w_mla_out"][l]),
                                       f(inp["w_out"][l])) for l in range(DEPTH)])
    sh["wuq"] = f(np.stack([f(inp["w_uq"][l]).reshape(2, 128, 768).transpose(1, 0, 2).reshape(128, 1536)
                            for l in range(DEPTH)]))
    wukv = f(inp["w_ukv"]).reshape(DEPTH, 128, 8, 128)
    sh["wkn"] = f(wukv[:, :, :, 0:64].reshape(DEPTH, 128, 512))
    sh["wv"] = f(wukv[:, :, :, 64:128].reshape(DEPTH, 128, 512))
    vecs = np.zeros((DEPTH, 128, NVEC), np.float32)
    for l in range(DEPTH):
        vecs[l, :, V_MIXG:V_MIXG + 8] = f(inp["mix_norm_g"][l]).reshape(8, 128).T
        vecs[l, :, V_FFNG:V_FFNG + 8] = f(inp["ffn_norm_g"][l]).reshape(8, 128).T
        cw = f(inp["conv_w"][l]).reshape(31, 4, 128)
        vecs[l, :, V_CONVW:V_CONVW + 124] = cw.transpose(2, 1, 0).reshape(128, 124)
        vecs[l, :, V_CONVB:V_CONVB + 4] = f(inp["conv_b"][l]).reshape(4, 128).T
        vecs[l, :, V_LNG:V_LNG + 4] = f(inp["conv_ln_g"][l]).reshape(4, 128).T
        vecs[l, :, V_LNB:V_LNB + 4] = f(inp["conv_ln_b"][l]).reshape(4, 128).T
        vecs[l, :, V_QAG:V_QAG + 2] = f(inp["q_a_norm_g"][l]).reshape(2, 128).T
        vecs[l, :, V_KVAG] = f(inp["kv_a_norm_g"][l])
        vecs[l, 0:96, V_QNG] = f(inp["q_norm_g"][l])
        vecs[l, 0:96, V_KNG] = f(inp["k_norm_g"][l])
    sh["vecs"] = vecs
    sh["wq"] = f(np.stack([f(inp["peer_wq"][l]).reshape(8, 128, 2048).transpose(1, 0, 2).reshape(128, 8 * 2048)
                           for l in range(DEPTH)]))
    ky = f(inp["peer_keys"]).reshape(DEPTH, 16, 128, 128)
    sh["keysT"] = f(ky.transpose(0, 3, 1, 2).reshape(DEPTH, 128, 16 * 128))
    for l in range(DEPTH):
        sh["peer_u%d" % l] = f(inp["peer_u"][l])
        sh["peer_v%d" % l] = f(inp["peer_v"][l])
    pos = np.arange(L, dtype=np.float32)
    inv = (1.0 / (np.float32(10000.0) ** (np.arange(0, 32, 2, dtype=np.float32) / np.float32(32)))).astype(np.float32)
    ang = pos[:, None] * inv[None, :]
    ang = np.concatenate([ang, ang], axis=-1)
    cosT = np.zeros((96, L), np.float32)
    sinT = np.zeros((96, L), np.float32)
    cosT[64:96] = np.cos(ang).T
    sinT[64:96] = np.sin(ang).T
    sh["cosT"] = cosT
    sh["sinT"] = sinT
    cst = np.zeros((128, 240), np.float32)
    cst[:, 0:128] = np.eye(128, dtype=np.float32)
    prot = np.zeros((128, 96), np.float32)
    for m in range(16):
        prot[64 + m + 16, 64 + m] = -1.0
        prot[64 + m, 64 + m + 16] = 1.0
    cst[:, 128:224] = prot
    cst[:, 224:240] = np.arange(16, dtype=np.float32)[None, :]
    sh["consts"] = cst
    sh["metaT"] = f(f(inp["meta_tokens"]).T)
    return sh


_CACHE = {}


def kernel(**inputs):
    x = np.asarray(inputs["x"], dtype=np.float32)
    nb = x.shape[0]
    sh = _prep_shared(inputs)
    if "nc" not in _CACHE:
        _CACHE["nc"] = build_program()
    nc = _CACHE["nc"]
    in_maps = []
    for b in range(nb):
        m = dict(sh)
        m["xT"] = np.ascontiguousarray(x[b].T)
        in_maps.append(m)
    res = run_bass_kernel_spmd(nc, in_maps, core_ids=list(range(nb)))
    out = np.stack([np.asarray(r["outT"], dtype=np.float32).T for r in res.results], axis=0)
    return np.ascontiguousarray(out)
```

```python
import contextlib
import numpy as np
import concourse.bass as bass
import concourse.mybir as mybir
from concourse.bass_utils import run_bass_kernel_spmd

F32 = mybir.dt.float32
BF16 = mybir.dt.bfloat16
I32 = mybir.dt.int32
U32 = mybir.dt.uint32
ALU = mybir.AluOpType
AF = mybir.ActivationFunctionType
AX = mybir.AxisListType

D = 1024
SEQ = 2048
NMETA = 16
L = SEQ + NMETA
DEPTH = 2
NB = 256
NRB = SEQ // NB
NSLOT = 6
NWT = 52
NKT = 1 + SEQ // 128
NEXP = 16384
NG = 12
NVEC = 160
V_MIXG, V_FFNG, V_CONVW, V_CONVB, V_LNG, V_LNB, V_QAG, V_KVAG, V_QNG, V_KNG = (
    0, 8, 16, 140, 144, 148, 152, 154, 155, 156)
EPS = 1e-6
SCALE = 96.0 ** -0.5


class Buf:
    __slots__ = ("name", "w", "r", "dsem", "dval")

    def __init__(self, name):
        self.name = name
        self.w = None
        self.r = {}
        self.dsem = None
        self.dval = 0


class Sched:
    def __init__(self, nc, stack):
        self.nc = nc
        self.stack = stack
        self.eng = {"pe": nc.tensor, "act": nc.scalar, "dve": nc.vector,
                    "pool": nc.gpsimd, "sp": nc.sync}
        self.sems = {}
        self.cnt = {}
        self.known = {}
        self.snap = {}
        for e in self.eng:
            self.sems[e] = stack.enter_context(nc.semaphore("s_" + e))
            self.cnt[e] = 0
            self.known[e] = {}
            self.snap[e] = {}
        self.ndsem = 0
        self.nwait = 0
        self.nins = 0

    def new_dsem(self):
        k = "d%d" % self.ndsem
        self.ndsem += 1
        self.sems[k] = self.stack.enter_context(self.nc.semaphore("s_" + k))
        return k

    def _need(self, e, deps, key, val):
        if self.known[e].get(key, 0) >= val:
            return
        if deps.get(key, 0) < val:
            deps[key] = val

    def _collect(self, e, R, W):
        deps = {}
        for b in R:
            if b.w is not None:
                k, v = b.w
                if k == e and e == "pe":
                    continue
                self._need(e, deps, k, v)
        for b in W:
            if b.w is not None:
                k, v = b.w
                if k != e:
                    self._need(e, deps, k, v)
            for k, v in b.r.items():
                if k != e:
                    self._need(e, deps, k, v)
        return deps

    def _emit_waits(self, e, deps):
        for k, v in deps.items():
            self.eng[e].wait_ge(self.sems[k], v)
            self.nwait += 1
            kn = self.known[e]
            if kn.get(k, 0) < v:
                kn[k] = v
            sn = self.snap.get(k, {}).get(v)
            if sn:
                for kk, vv in sn.items():
                    if kn.get(kk, 0) < vv:
                        kn[kk] = vv

    def op(self, e, fn, R=(), W=()):
        deps = self._collect(e, R, W)
        self._emit_waits(e, deps)
        ins = fn(self.eng[e])
        self.cnt[e] += 1
        n = self.cnt[e]
        ins.then_inc(self.sems[e], 1)
        self.nins += 1
        self.snap[e][n] = dict(self.known[e])
        for b in R:
            b.r[e] = n
        for b in W:
            b.w = (e, n)
            b.r = {}
        return ins

    def dma(self, q, out, in_, R=(), W=(), sembuf=None, indirect=None):
        deps = self._collect(q, R, W)
        self._emit_waits(q, deps)
        sb = sembuf if sembuf is not None else (W[0] if W else R[0])
        if sb.dsem is None:
            sb.dsem = self.new_dsem()
        if indirect is not None:
            ins = self.eng[q].indirect_dma_start(out=out, out_offset=None, in_=in_,
                                                 in_offset=indirect)
        else:
            ins = self.eng[q].dma_start(out=out, in_=in_)
        sb.dval += 16
        ins.then_inc(self.sems[sb.dsem], 16)
        tok = (sb.dsem, sb.dval)
        for b in R:
            b.r[tok[0]] = tok[1]
        for b in W:
            b.w = tok
            b.r = {}
        return tok

    def wait_all(self, e, bufs):
        deps = {}
        for b in bufs:
            if b.w is not None:
                self._need(e, deps, b.w[0], b.w[1])
            for k, v in b.r.items():
                self._need(e, deps, k, v)
        self._emit_waits(e, deps)

    def barrier(self, bufs=()):
        for e in self.eng:
            deps = {}
            for f in self.eng:
                if f != e and self.cnt[f] > 0:
                    self._need(e, deps, f, self.cnt[f])
            for b in bufs:
                if b.w is not None:
                    self._need(e, deps, b.w[0], b.w[1])
                for k, v in b.r.items():
                    self._need(e, deps, k, v)
            self._emit_waits(e, deps)


class Alloc:
    def __init__(self, nc, stack):
        self.nc = nc
        self.stack = stack
        self.n = [0]

    def sub(self, stack):
        a = Alloc(self.nc, stack)
        a.n = self.n
        return a

    def sb(self, shape, dt, name="t"):
        self.n[0] += 1
        nm = "%s_%d" % (name, self.n[0])
        t = self.stack.enter_context(self.nc.sbuf_tensor(nm, list(shape), dt))
        return t, Buf(nm)

    def ps(self, shape, dt, name="p"):
        self.n[0] += 1
        nm = "%s_%d" % (name, self.n[0])
        t = self.stack.enter_context(self.nc.psum_tensor(nm, list(shape), dt))
        return t, Buf(nm)


class _Proxy:
    def __init__(self):
        self.call = None

    def __getattr__(self, name):
        def f(*a, **k):
            self.call = (name, a, k)
            return self
        return f


class Rec:
    def __init__(self, S):
        self.S = S
        self.th = []

    def op(self, e, fn, R=(), W=()):
        p = _Proxy()
        fn(p)
        self.th.append((e, p.call, list(R), list(W)))

    def flush(self, n=None):
        k = len(self.th) if n is None else min(n, len(self.th))
        for _ in range(k):
            e, (name, a, kw), R, W = self.th.pop(0)
            self.S.op(e, lambda eng: getattr(eng, name)(*a, **kw), R=R, W=W)


def build_program(n_layers=DEPTH, do_mixer=True, do_peer=True):
    nc = bass.Bass("TRN2", target_bir_lowering=False)

    def din(name, shape, dt=F32):
        return nc.dram_tensor(name, list(shape), dt, kind="ExternalInput").ap()

    xT_d = din("xT", [D, SEQ])
    meta_d = din("metaT", [D, NMETA])
    wst_d = din("wstream", [DEPTH, NWT, 128, 1024])
    wuq_d = din("wuq", [DEPTH, 128, 2 * 768])
    wkn_d = din("wkn", [DEPTH, 128, 512])
    wv_d = din("wv", [DEPTH, 128, 512])
    vecs_d = din("vecs", [DEPTH, 128, NVEC])
    wq_d = din("wq", [DEPTH, 128, 8 * 2048])
    keys_d = din("keysT", [DEPTH, 128, 16 * 128])
    pu_d = [din("peer_u%d" % l, [NEXP, D]) for l in range(DEPTH)]
    pv_d = [din("peer_v%d" % l, [NEXP, D]) for l in range(DEPTH)]
    uv_d = [nc.dram_tensor("uvbf%d" % l, [NEXP, 2 * D], BF16).ap() for l in range(DEPTH)]
    wbf_d = nc.dram_tensor("wbf", [DEPTH, NWT, 128, 1024], BF16).ap()
    cos_d = din("cosT", [96, L])
    sin_d = din("sinT", [96, L])
    cst_d = din("consts", [128, 128 + 96 + 16])
    out_d = nc.dram_tensor("outT", [D, SEQ], F32, kind="ExternalOutput").ap()

    with contextlib.ExitStack() as st0:
        S = Sched(nc, st0)
        A0 = Alloc(nc, st0)

        x_t, _ = A0.sb([128, 8, L], F32, "x")
        xb = [[Buf("x%d_%d" % (b, c)) for c in range(8)] for b in range(1 + NRB)]
        cst_t, cst_b = A0.sb([128, 240], F32, "cst")
        identf = cst_t[:, 0:128]
        iota16 = cst_t[:, 224:240]
        identb_t, identb_b = A0.sb([128, 128], BF16, "identb")
        prot_t, prot_b = A0.sb([128, 96], BF16, "prot")
        ones_t, ones_b = A0.sb([128, 128], BF16, "ones")
        eps_t, eps_b = A0.sb([128, 1], F32, "eps")
        vecs_t = []
        vecs_b = []
        for l in range(DEPTH):
            t, b = A0.sb([128, NVEC], F32, "vecs")
            vecs_t.append(t)
            vecs_b.append(b)
        ps_all, _ = A0.ps([128, 8, 512], F32, "psum")
        pst = [ps_all[:, i, :] for i in range(8)]
        psb = [Buf("bank%d" % i) for i in range(8)]
        psrr = [0]
        pprr = [0]

        def ps_next():
            i = 4 + psrr[0] % 4
            psrr[0] += 1
            return pst[i], psb[i]

        def ps_pair():
            j = 4 + 2 * (pprr[0] % 2)
            pprr[0] += 1
            return ps_all[:, j:j + 2, :].rearrange("p b n -> p (b n)"), [psb[j], psb[j + 1]]

        S.dma("sp", cst_t[:], cst_d, W=[cst_b])
        for l in range(DEPTH):
            S.dma("sp", vecs_t[l][:], vecs_d[l], W=[vecs_b[l]])
        for c in range(8):
            S.dma("sp", x_t[:, c, 0:NMETA], meta_d[c * 128:(c + 1) * 128, :], W=[xb[0][c]])
        for c in range(8):
            wl = [xb[1 + b][c] for b in range(NRB)]
            S.dma("sp", x_t[:, c, NMETA:L], xT_d[c * 128:(c + 1) * 128, :], W=wl, sembuf=wl[0])
        S.op("pool", lambda e: e.memset(ones_t[:], 1.0), W=[ones_b])
        S.op("pool", lambda e: e.memset(eps_t[:], EPS), W=[eps_b])
        S.op("dve", lambda e: e.tensor_copy(out=identb_t[:], in_=identf), R=[cst_b], W=[identb_b])
        S.op("dve", lambda e: e.tensor_copy(out=prot_t[:], in_=cst_t[:, 128:224]), R=[cst_b], W=[prot_b])

        blocks = [(0, NMETA)] + [(NMETA + NB * b, NB) for b in range(NRB)]
        ktiles = [(0, NMETA)] + [(NMETA + 128 * m, 128) for m in range(SEQ // 128)]

        def rms_feature(X, src_fn, src_bufs, nchunk, npart, n, dim, gcol, vt, vb, dst_fn, dst_bufs,
                        sq_t, sq_b, ln_t, ln_b, rs_t, rs_b):
            pt, pb = ps_next()
            for c in range(nchunk):
                X.op("act", lambda e, c=c: e.activation(out=sq_t[0:npart, c % 2, 0:n], in_=src_fn(c),
                                                        func=AF.Square),
                     R=[src_bufs[c]], W=[sq_b[c % 2]])
                X.op("pe", lambda e, c=c: e.matmul(pt[0:npart, 0:n], lhsT=ones_t[0:npart, 0:npart],
                                                   rhs=sq_t[0:npart, c % 2, 0:n],
                                                   start=(c == 0), stop=(c == nchunk - 1)),
                     R=[ones_b, sq_b[c % 2]], W=[pb])
            X.op("act", lambda e: e.activation(out=ln_t[0:npart, 0:n], in_=pt[0:npart, 0:n], func=AF.Ln,
                                               bias=eps_t[0:npart, :], scale=1.0 / dim),
                 R=[pb, eps_b], W=[ln_b])
            X.op("act", lambda e: e.activation(out=rs_t[0:npart, 0:n], in_=ln_t[0:npart, 0:n], func=AF.Exp,
                                               scale=-0.5),
                 R=[ln_b], W=[rs_b])
            for c in range(nchunk):
                X.op("dve", lambda e, c=c: e.scalar_tensor_tensor(
                    out=dst_fn(c), in0=src_fn(c), scalar=vt[0:npart, gcol + c:gcol + c + 1],
                    in1=rs_t[0:npart, 0:n], op0=ALU.mult, op1=ALU.mult),
                    R=[src_bufs[c], vb, rs_b], W=[dst_bufs[c]])

        with contextlib.ExitStack() as stw:
            Aw = A0.sub(stw)
            ws_t, _ = Aw.sb([128, 8, 1024], BF16, "wstg")
            wsb = [Buf("wstg%d" % i) for i in range(8)]
            wso = [Buf("wstgo%d" % i) for i in range(8)]
            for i in range(n_layers * NWT):
                l_, ti = divmod(i, NWT)
                sbi = i % 8
                S.dma("pool", ws_t[:, sbi, :], wst_d[l_, ti], W=[wsb[sbi]])
                S.dma("sp", wbf_d[l_, ti], ws_t[:, sbi, :], R=[wsb[sbi]], sembuf=wso[sbi])
            S.barrier(wsb)

        for l in range(n_layers):
            vt = vecs_t[l]
            vb = vecs_b[l]
            cstate = {"i": 0, "stg": None, "osem": [Buf("stgo0"), Buf("stgo1")]}
            NCV = 2 * (NEXP // 256)

            def conv_step():
                i = cstate["i"]
                if i >= NCV:
                    return
                cstate["i"] += 1
                stg_t, stgb = cstate["stg"]
                src = (pu_d[l], pv_d[l])[i % 2]
                dst = uv_d[l][:, (i % 2) * D:(i % 2 + 1) * D]
                r0 = (i // 2) * 256
                sb_ = i % 2
                S.dma("pool", stg_t[:, sb_, :, :], src[r0:r0 + 256, :].rearrange("(p j) d -> p j d", j=2),
                      W=[stgb[sb_]])
                S.dma("sp", dst[r0:r0 + 256, :].rearrange("(p j) d -> p j d", j=2), stg_t[:, sb_, :, :],
                      R=[stgb[sb_]], sembuf=cstate["osem"][sb_])
            if do_mixer:
              with contextlib.ExitStack() as stm:
                A = A0.sub(stm)
                kst_t, _ = A.sb([128, 8, L], BF16, "kst")
                kstb = [[Buf("k%d_%d" % (b, h)) for h in range(8)] for b in range(1 + NRB)]
                vst_t, _ = A.sb([128, NKT, 512], BF16, "vst")
                vstb = [Buf("v%d" % i) for i in range(NKT)]
                ring_t, _ = A.sb([128, NSLOT, 1024], BF16, "ring")
                ringb = [Buf("ring%d" % i) for i in range(NSLOT)]
                wuq_t, wuq_b = A.sb([128, 2 * 768], BF16, "wuq")
                wkn_t, wkn_b = A.sb([128, 512], BF16, "wkn")
                wv_t, wv_b = A.sb([128, 512], BF16, "wv")
                cos_t, cos_b = A.sb([128, NB], F32, "cos")
                sin_t, sin_b = A.sb([128, NB], F32, "sin")
                hT_t, _ = A.sb([128, 8, NB], BF16, "hT")
                hTb = [Buf("hT%d" % c) for c in range(8)]
                sq_t, _ = A.sb([128, 2, NB], BF16, "sq")
                sqb = [Buf("sq0"), Buf("sq1")]
                ln_t, ln_b = A.sb([128, NB], F32, "ln")
                rs_t, rs_b = A.sb([128, NB], F32, "rs")
                sig_t, _ = A.sb([128, 2, NB], F32, "sig")
                sigb = [Buf("sig0"), Buf("sig1")]
                ub_t, _ = A.sb([128, 4, 30 + NB], F32, "ubuf")
                ubb = [Buf("ub%d" % c) for c in range(4)]
                halo_t, _ = A.sb([128, 4, 30], F32, "halo")
                halob = [Buf("halo%d" % c) for c in range(4)]
                y_t, _ = A.sb([128, 4, NB], F32, "y")
                yb = [Buf("y%d" % c) for c in range(4)]
                yh_t, _ = A.sb([128, 4, NB], BF16, "yh")
                yhb = [Buf("yh%d" % c) for c in range(4)]
                ysq_t, _ = A.sb([128, 4, NB], BF16, "ysq")
                ysqb = [Buf("ysq%d" % c) for c in range(4)]
                mu_t, mu_b = A.sb([128, NB], F32, "mu")
                var_t, var_b = A.sb([128, NB], F32, "var")
                actc_t, _ = A.sb([128, 4, NB], BF16, "actc")
                actcb = [Buf("actc%d" % c) for c in range(4)]
                cq_t, _ = A.sb([128, 2, NB], F32, "cq")
                cqb = [Buf("cq0"), Buf("cq1")]
                cqn_t, _ = A.sb([128, 2, NB], BF16, "cqn")
                cqnb = [Buf("cqn0"), Buf("cqn1")]
                ckv_t, ckv_b = A.sb([128, NB], F32, "ckv")
                ckvn_t, ckvn_b = A.sb([128, NB], BF16, "ckvn")
                hc4_t, hc4_b = A.sb([128, 4 * NB], F32, "hc4")
                sq4_t, sq4_b = A.sb([128, 4 * NB], BF16, "sq4")
                r4_t, r4_b = A.sb([128, 4 * NB], F32, "r4")
                t24_t, t24_b = A.sb([128, 4 * NB], BF16, "t24")
                hnh4_t, hnh4_b = A.sb([128, 4 * NB], BF16, "hnh4")
                q_t, _ = A.sb([128, 8, NB], BF16, "qblk")
                qb = [Buf("q%d" % h) for h in range(8)]
                pT_t, _ = A.sb([128, 4, NB], BF16, "pT")
                pTb = [Buf("pT%d" % i) for i in range(4)]
                rden_t, _ = A.sb([64, 2, NB], F32, "rden")
                rdenb = [Buf("rden0"), Buf("rden1")]
                oT_t, _ = A.sb([64, 8, NB], BF16, "oT")
                oTb = [Buf("oT%d" % h) for h in range(8)]
                m1_t, _ = A.sb([128, 2, NB], F32, "m1")
                m1b = [Buf("m1_0"), Buf("m1_1")]
                mg_t, _ = A.sb([128, 8, NB], BF16, "merged")
                mgb = [Buf("mg%d" % c) for c in range(8)]

                stg_t, _ = A.sb([128, 2, 2, D], BF16, "stg")
                stgb_m = [Buf("stg0"), Buf("stg1")]
                cstate["stg"] = (stg_t, stgb_m)
                S.dma("pool", wuq_t[:], wuq_d[l], W=[wuq_b])
                S.dma("pool", wkn_t[:], wkn_d[l], W=[wkn_b])
                S.dma("pool", wv_t[:], wv_d[l], W=[wv_b])
                for c in range(4):
                    S.op("pool", lambda e, c=c: e.memset(halo_t[:, c, :], 0.0), W=[halob[c]])

                wstate = {"issued": 0, "used": 0}
                total_tiles = NWT * len(blocks)

                def w_issue():
                    i = wstate["issued"]
                    if i >= total_tiles:
                        return
                    s = i % NSLOT
                    S.dma("sp", ring_t[:, s, :], wbf_d[l, i % NWT], W=[ringb[s]])
                    wstate["issued"] += 1

                def w_take():
                    i = wstate["used"]
                    wstate["used"] += 1
                    s = i % NSLOT
                    return ring_t[:, s, :], ringb[s]

                for _ in range(NSLOT):
                    w_issue()

                for bi, (c0, n) in enumerate(blocks):
                    xs = lambda c: x_t[:, c, c0:c0 + n]
                    S.dma("sp", cos_t[64:96, 0:n], cos_d[64:96, c0:c0 + n], W=[cos_b])
                    S.dma("sp", sin_t[64:96, 0:n], sin_d[64:96, c0:c0 + n], W=[sin_b])
                    rms_feature(S, xs, xb[bi], 8, 128, n, float(D), V_MIXG, vt, vb,
                                lambda c: hT_t[:, c, 0:n], hTb, sq_t, sqb, ln_t, ln_b, rs_t, rs_b)

                    def zproj(M):
                        wt, wb = w_take()
                        pt, pb = ps_next()
                        for kc in range(8):
                            S.op("pe", lambda e, kc=kc: e.matmul(pt[0:M, 0:n], lhsT=wt[:, kc * 128:kc * 128 + M],
                                                                 rhs=hT_t[:, kc, 0:n],
                                                                 start=(kc == 0), stop=(kc == 7)),
                                 R=[wb, hTb[kc]], W=[pb])
                        w_issue()
                        return pt, pb

                    for c in range(4):
                        pa, pab = zproj(128)
                        pg, pgb = zproj(128)
                        sg = c % 2
                        S.op("act", lambda e: e.activation(out=sig_t[:, sg, 0:n], in_=pg[:, 0:n], func=AF.Sigmoid),
                             R=[pgb], W=[sigb[sg]])
                        S.op("pool", lambda e, c=c: e.tensor_copy(out=ub_t[:, c, 0:30], in_=halo_t[:, c, :]),
                             R=[halob[c]], W=[ubb[c]])
                        S.op("dve", lambda e, c=c: e.tensor_tensor(out=ub_t[:, c, 30:30 + n], in0=pa[:, 0:n],
                                                                   in1=sig_t[:, sg, 0:n], op=ALU.mult),
                             R=[pab, sigb[sg]], W=[ubb[c]])
                        S.op("pool", lambda e, c=c: e.tensor_copy(out=halo_t[:, c, :], in_=ub_t[:, c, n:n + 30]),
                             R=[ubb[c]], W=[halob[c]])
                    for c in range(2):
                        pq, pqb = zproj(128)
                        S.op("act", lambda e, c=c: e.activation(out=cq_t[:, c, 0:n], in_=pq[:, 0:n], func=AF.Copy),
                             R=[pqb], W=[cqb[c]])
                    pk, pkb = zproj(128)
                    S.op("act", lambda e: e.activation(out=ckv_t[:, 0:n], in_=pk[:, 0:n], func=AF.Copy),
                         R=[pkb], W=[ckv_b])
                    pr, prb = zproj(96)
                    kr_t, kr_b = ln_t, ln_b
                    if "krope" not in wstate:
                        wstate["krope"] = A.sb([128, NB], F32, "krope")
                    kr_t, kr_b = wstate["krope"]
                    S.op("act", lambda e: e.activation(out=kr_t[64:96, 0:n], in_=pr[64:96, 0:n], func=AF.Copy),
                         R=[prb], W=[kr_b])
                    rms_feature(S, lambda c: cq_t[:, c, 0:n], cqb, 2, 128, n, 256.0, V_QAG, vt, vb,
                                lambda c: cqn_t[:, c, 0:n], cqnb, sq_t, sqb, ln_t, ln_b, rs_t, rs_b)
                    rms_feature(S, lambda c: ckv_t[:, 0:n], [ckv_b], 1, 128, n, 128.0, V_KVAG, vt, vb,
                                lambda c: ckvn_t[:, 0:n], [ckvn_b], sq_t, sqb, ln_t, ln_b, rs_t, rs_b)

                    conv_thunks = []
                    for c in range(4):
                        conv_thunks.append(lambda c=c: S.op("dve", lambda e: e.tensor_scalar(
                            out=y_t[:, c, 0:n], in0=ub_t[:, c, 0:n],
                            scalar1=vt[:, V_CONVW + c * 31:V_CONVW + c * 31 + 1],
                            scalar2=vt[:, V_CONVB + c:V_CONVB + c + 1], op0=ALU.mult, op1=ALU.add),
                            R=[ubb[c], vb], W=[yb[c]]))
                        for k in range(1, 31):
                            conv_thunks.append(lambda c=c, k=k: S.op("dve", lambda e: e.scalar_tensor_tensor(
                                out=y_t[:, c, 0:n], in0=ub_t[:, c, k:k + n],
                                scalar=vt[:, V_CONVW + c * 31 + k:V_CONVW + c * 31 + k + 1],
                                in1=y_t[:, c, 0:n], op0=ALU.mult, op1=ALU.add),
                                R=[ubb[c], vb, yb[c]], W=[yb[c]]))
                    bkt = [0] if bi == 0 else [1 + 2 * (bi - 1), 2 + 2 * (bi - 1)]
                    for j, kt in enumerate(bkt):
                        kc0, nk = ktiles[kt]
                        off = kc0 - c0
                        pv_, pvb_ = ps_next()
                        S.op("pe", lambda e: e.matmul(pv_[0:nk, 0:512], lhsT=ckvn_t[:, off:off + nk], rhs=wv_t[:],
                                                      start=True, stop=True),
                             R=[ckvn_b, wv_b], W=[pvb_])
                        S.op("act", lambda e: e.activation(out=vst_t[0:nk, kt, :], in_=pv_[0:nk, 0:512], func=AF.Copy),
                             R=[pvb_], W=[vstb[kt]])

                    W4 = 4 * n

                    def v3(ap):
                        return ap.rearrange("p (h n) -> p h n", h=4)

                    def chunks512():
                        return [(o, min(512, W4 - o)) for o in range(0, W4, 512)]

                    def head_batch(kind, h0):
                        PP, PPb = ps_pair()
                        if kind == "k":
                            for hh in range(4):
                                h = h0 + hh
                                S.op("pe", lambda e: e.matmul(PP[0:64, hh * n:(hh + 1) * n], lhsT=wkn_t[:, h * 64:(h + 1) * 64],
                                                              rhs=ckvn_t[:, 0:n], start=True, stop=True),
                                     R=[wkn_b, ckvn_b], W=PPb)
                            S.op("act", lambda e: e.activation(out=hc4_t[0:64, 0:W4], in_=PP[0:64, 0:W4], func=AF.Copy),
                                 R=PPb, W=[hc4_b])
                            S.op("pool", lambda e: e.tensor_copy(
                                out=v3(hc4_t[64:96, 0:W4]), in_=kr_t[64:96, 0:n].unsqueeze(1).broadcast_to([32, 4, n])),
                                R=[kr_b], W=[hc4_b])
                            gcol = V_KNG
                            dst = lambda r0, r1: kst_t[r0:r1, h0:h0 + 4, c0:c0 + n]
                            dstb = [kstb[bi][h0 + i] for i in range(4)]
                        else:
                            for hh in range(4):
                                h = h0 + hh
                                for kc in range(2):
                                    S.op("pe", lambda e: e.matmul(
                                        PP[0:96, hh * n:(hh + 1) * n],
                                        lhsT=wuq_t[:, kc * 768 + h * 96:kc * 768 + (h + 1) * 96],
                                        rhs=cqn_t[:, kc, 0:n], start=(kc == 0), stop=(kc == 1)),
                                        R=[wuq_b, cqnb[kc]], W=PPb)
                            S.op("act", lambda e: e.activation(out=hc4_t[0:96, 0:W4], in_=PP[0:96, 0:W4], func=AF.Copy),
                                 R=PPb, W=[hc4_b])
                            gcol = V_QNG
                            dst = lambda r0, r1: q_t[r0:r1, h0:h0 + 4, 0:n]
                            dstb = [qb[h0 + i] for i in range(4)]
                        S.op("act", lambda e: e.activation(out=sq4_t[0:96, 0:W4], in_=hc4_t[0:96, 0:W4], func=AF.Square),
                             R=[hc4_b], W=[sq4_b])
                        PQ, PQb = ps_pair()
                        for (o, w) in chunks512():
                            S.op("pe", lambda e: e.matmul(PQ[0:96, o:o + w], lhsT=ones_t[0:96, 0:96], rhs=sq4_t[0:96, o:o + w],
                                                          start=True, stop=True),
                                 R=[ones_b, sq4_b], W=PQb)
                        S.op("act", lambda e: e.activation(out=r4_t[0:96, 0:W4], in_=PQ[0:96, 0:W4], func=AF.Ln,
                                                           bias=eps_t[0:96, :], scale=1.0 / 96),
                             R=PQb + [eps_b], W=[r4_b])
                        S.op("act", lambda e: e.activation(out=r4_t[0:96, 0:W4], in_=r4_t[0:96, 0:W4], func=AF.Exp, scale=-0.5),
                             R=[r4_b], W=[r4_b])
                        S.op("dve", lambda e: e.scalar_tensor_tensor(
                            out=hc4_t[0:96, 0:W4], in0=hc4_t[0:96, 0:W4], scalar=vt[0:96, gcol:gcol + 1],
                            in1=r4_t[0:96, 0:W4], op0=ALU.mult, op1=ALU.mult),
                            R=[hc4_b, vb, r4_b], W=[hc4_b])
                        S.op("act", lambda e: e.activation(out=dst(0, 64), in_=v3(hc4_t[0:64, 0:W4]), func=AF.Copy),
                             R=[hc4_b], W=dstb)
                        S.op("dve", lambda e: e.tensor_copy(out=hnh4_t[64:96, 0:W4], in_=hc4_t[64:96, 0:W4]),
                             R=[hc4_b], W=[hnh4_b])
                        PR, PRb = ps_pair()
                        for (o, w) in chunks512():
                            S.op("pe", lambda e: e.matmul(PR[0:96, o:o + w], lhsT=prot_t[64:96, 0:96], rhs=hnh4_t[64:96, o:o + w],
                                                          start=True, stop=True),
                                 R=[prot_b, hnh4_b], W=PRb)
                        S.op("dve", lambda e: e.tensor_tensor(
                            out=v3(t24_t[64:96, 0:W4]), in0=v3(PR[64:96, 0:W4]),
                            in1=sin_t[64:96, 0:n].unsqueeze(1).broadcast_to([32, 4, n]), op=ALU.mult),
                            R=PRb + [sin_b], W=[t24_b])
                        S.op("dve", lambda e: e.tensor_tensor(
                            out=v3(r4_t[64:96, 0:W4]), in0=v3(hc4_t[64:96, 0:W4]),
                            in1=cos_t[64:96, 0:n].unsqueeze(1).broadcast_to([32, 4, n]), op=ALU.mult),
                            R=[hc4_b, cos_b, r4_b], W=[r4_b])
                        S.op("dve", lambda e: e.tensor_tensor(out=dst(64, 96), in0=v3(r4_t[64:96, 0:W4]),
                                                              in1=v3(t24_t[64:96, 0:W4]), op=ALU.add),
                             R=[r4_b, t24_b], W=dstb)

                    for h0 in (0, 4):
                        head_batch("k", h0)
                        head_batch("q", h0)

                    if bi == 0:
                        vis = [(0, 0, False)]
                    else:
                        B = bi - 1
                        vis = [(0, 0, False)] + [(1 + m, 0, False) for m in range(2 * B)]
                        vis += [(1 + 2 * B + j, 128 * j, True) for j in range(2)]
                    pti = [0]
                    for h in range(8):
                        for _ in range(16):
                            if conv_thunks:
                                conv_thunks.pop(0)()
                        conv_step()
                        conv_step()
                        pnum, pnumb = pst[2 * (h % 2)], psb[2 * (h % 2)]
                        pden, pdenb = pst[2 * (h % 2) + 1], psb[2 * (h % 2) + 1]
                        pend = None
                        for vi in range(len(vis) + 1):
                            cur = None
                            if vi < len(vis):
                                kt, q0, diag = vis[vi]
                                kc0, nk = ktiles[kt]
                                kbi = 0 if kt == 0 else 1 + (kt - 1) // 2
                                sps, spsb = ps_next()
                                S.op("pe", lambda e: e.matmul(sps[0:nk, q0:n], lhsT=kst_t[0:96, h, kc0:kc0 + nk],
                                                              rhs=q_t[0:96, h, q0:n], start=True, stop=True),
                                     R=[kstb[kbi][h], qb[h]], W=[spsb])
                                pi = pti[0] % 4
                                pti[0] += 1
                                S.op("act", lambda e: e.activation(out=pT_t[0:nk, pi, q0:n], in_=sps[0:nk, q0:n],
                                                                   func=AF.Exp, scale=SCALE),
                                     R=[spsb], W=[pTb[pi]])
                                if diag:
                                    S.op("pool", lambda e: e.memset(pT_t[64:128, pi, q0:q0 + 64], 0.0), W=[pTb[pi]])
                                cur = (kt, q0, nk, pi, vi)
                            if pend is not None:
                                kt_, q0_, nk_, pi_, vi_ = pend
                                first = (vi_ == 0)
                                last = (vi_ == len(vis) - 1)
                                S.op("pe", lambda e: e.matmul(pnum[0:64, q0_:n], lhsT=vst_t[0:nk_, kt_, h * 64:(h + 1) * 64],
                                                              rhs=pT_t[0:nk_, pi_, q0_:n], start=first, stop=last),
                                     R=[vstb[kt_], pTb[pi_]], W=[pnumb])
                                S.op("pe", lambda e: e.matmul(pden[0:64, q0_:n], lhsT=ones_t[0:nk_, 0:64],
                                                              rhs=pT_t[0:nk_, pi_, q0_:n], start=first, stop=last),
                                     R=[ones_b, pTb[pi_]], W=[pdenb])
                            pend = cur
                        rb = h % 2
                        S.op("dve", lambda e: e.reciprocal(out=rden_t[0:64, rb, 0:n], in_=pden[0:64, 0:n]),
                             R=[pdenb], W=[rdenb[rb]])
                        S.op("dve", lambda e, h=h: e.tensor_tensor(out=oT_t[0:64, h, 0:n], in0=pnum[0:64, 0:n],
                                                                   in1=rden_t[0:64, rb, 0:n], op=ALU.mult),
                             R=[pnumb, rdenb[rb]], W=[oTb[h]])

                    while conv_thunks:
                        conv_thunks.pop(0)()
                    p1, p1b = ps_next()
                    p2, p2b = ps_next()
                    for c in range(4):
                        S.op("act", lambda e, c=c: e.activation(out=yh_t[:, c, 0:n], in_=y_t[:, c, 0:n], func=AF.Copy),
                             R=[yb[c]], W=[yhb[c]])
                        S.op("act", lambda e, c=c: e.activation(out=ysq_t[:, c, 0:n], in_=y_t[:, c, 0:n], func=AF.Square),
                             R=[yb[c]], W=[ysqb[c]])
                    for c in range(4):
                        S.op("pe", lambda e, c=c: e.matmul(p1[:, 0:n], lhsT=ones_t[:], rhs=yh_t[:, c, 0:n],
                                                           start=(c == 0), stop=(c == 3)),
                             R=[ones_b, yhb[c]], W=[p1b])
                    for c in range(4):
                        S.op("pe", lambda e, c=c: e.matmul(p2[:, 0:n], lhsT=ones_t[:], rhs=ysq_t[:, c, 0:n],
                                                           start=(c == 0), stop=(c == 3)),
                             R=[ones_b, ysqb[c]], W=[p2b])
                    S.op("dve", lambda e: e.tensor_scalar(out=mu_t[:, 0:n], in0=p1[:, 0:n], scalar1=1.0 / 512,
                                                          scalar2=None, op0=ALU.mult),
                         R=[p1b], W=[mu_b])
                    S.op("dve", lambda e: e.tensor_tensor(out=var_t[:, 0:n], in0=mu_t[:, 0:n], in1=mu_t[:, 0:n],
                                                          op=ALU.mult),
                         R=[mu_b], W=[var_b])
                    S.op("dve", lambda e: e.scalar_tensor_tensor(out=var_t[:, 0:n], in0=p2[:, 0:n], scalar=1.0 / 512,
                                                                 in1=var_t[:, 0:n], op0=ALU.mult, op1=ALU.subtract),
                         R=[p2b, var_b], W=[var_b])
                    S.op("act", lambda e: e.activation(out=ln_t[:, 0:n], in_=var_t[:, 0:n], func=AF.Ln,
                                                       bias=eps_t[:, :], scale=1.0),
                         R=[var_b, eps_b], W=[ln_b])
                    S.op("act", lambda e: e.activation(out=rs_t[:, 0:n], in_=ln_t[:, 0:n], func=AF.Exp, scale=-0.5),
                         R=[ln_b], W=[rs_b])
                    for c in range(4):
                        S.op("dve", lambda e, c=c: e.tensor_tensor(out=y_t[:, c, 0:n], in0=y_t[:, c, 0:n],
                                                                   in1=mu_t[:, 0:n], op=ALU.subtract),
                             R=[yb[c], mu_b], W=[yb[c]])
                        S.op("dve", lambda e, c=c: e.tensor_tensor(out=y_t[:, c, 0:n], in0=y_t[:, c, 0:n],
                                                                   in1=rs_t[:, 0:n], op=ALU.mult),
                             R=[yb[c], rs_b], W=[yb[c]])
                        S.op("act", lambda e, c=c: e.activation(
                            out=actc_t[:, c, 0:n], in_=y_t[:, c, 0:n], func=AF.Silu,
                            bias=vt[:, V_LNB + c:V_LNB + c + 1], scale=vt[:, V_LNG + c:V_LNG + c + 1]),
                            R=[yb[c], vb], W=[actcb[c]])

                    for c in range(8):
                        pg1, pg1b = zproj(128)
                        S.op("act", lambda e: e.activation(out=sig_t[:, 0, 0:n], in_=pg1[:, 0:n], func=AF.Sigmoid),
                             R=[pg1b], W=[sigb[0]])
                        wt, wb = w_take()
                        pc, pcb = ps_next()
                        for kc in range(4):
                            S.op("pe", lambda e, kc=kc: e.matmul(pc[:, 0:n], lhsT=wt[:, kc * 128:(kc + 1) * 128],
                                                                 rhs=actc_t[:, kc, 0:n], start=(kc == 0), stop=(kc == 3)),
                                 R=[wb, actcb[kc]], W=[pcb])
                        w_issue()
                        S.op("dve", lambda e: e.tensor_tensor(out=m1_t[:, 0, 0:n], in0=pc[:, 0:n], in1=sig_t[:, 0, 0:n],
                                                              op=ALU.mult),
                             R=[pcb, sigb[0]], W=[m1b[0]])
                        pg2, pg2b = zproj(128)
                        S.op("act", lambda e: e.activation(out=sig_t[:, 1, 0:n], in_=pg2[:, 0:n], func=AF.Sigmoid),
                             R=[pg2b], W=[sigb[1]])
                        wt2, wb2 = w_take()
                        pm, pmb = ps_next()
                        for h in range(8):
                            S.op("pe", lambda e, h=h: e.matmul(pm[:, 0:n], lhsT=wt2[0:64, h * 128:(h + 1) * 128],
                                                               rhs=oT_t[0:64, h, 0:n], start=(h == 0), stop=(h == 7)),
                                 R=[wb2, oTb[h]], W=[pmb])
                        w_issue()
                        S.op("dve", lambda e: e.tensor_tensor(out=m1_t[:, 1, 0:n], in0=pm[:, 0:n], in1=sig_t[:, 1, 0:n],
                                                              op=ALU.mult),
                             R=[pmb, sigb[1]], W=[m1b[1]])
                        S.op("dve", lambda e, c=c: e.tensor_tensor(out=mg_t[:, c, 0:n], in0=m1_t[:, 0, 0:n],
                                                                   in1=m1_t[:, 1, 0:n], op=ALU.add),
                             R=[m1b[0], m1b[1]], W=[mgb[c]])
                    for c in range(8):
                        wt, wb = w_take()
                        po, pob = ps_next()
                        for kc in range(8):
                            S.op("pe", lambda e, kc=kc: e.matmul(po[:, 0:n], lhsT=wt[:, kc * 128:(kc + 1) * 128],
                                                                 rhs=mg_t[:, kc, 0:n], start=(kc == 0), stop=(kc == 7)),
                                 R=[wb, mgb[kc]], W=[pob])
                        w_issue()
                        S.op("dve", lambda e, c=c: e.tensor_tensor(out=x_t[:, c, c0:c0 + n], in0=x_t[:, c, c0:c0 + n],
                                                                   in1=po[:, 0:n], op=ALU.add),
                             R=[pob, xb[bi][c]], W=[xb[bi][c]])
                while cstate["i"] < NCV:
                    conv_step()
                S.barrier(stgb_m)

            if do_peer:
              with contextlib.ExitStack() as stp:
                A = A0.sub(stp)
                wq_t, wq_b = A.sb([128, 8 * 2048], BF16, "wq")
                ky_t, ky_b = A.sb([128, 16 * 128], BF16, "keysT")
                hT_t, _ = A.sb([128, 8, 128], BF16, "h2T")
                hTb = [Buf("h2T%d" % c) for c in range(8)]
                sq_t, _ = A.sb([128, 2, 128], BF16, "sq")
                sqb = [Buf("sq0"), Buf("sq1")]
                ln_t, ln_b = A.sb([128, 128], F32, "ln")
                rs_t, rs_b = A.sb([128, 128], F32, "rs")
                htok2_t, _ = A.sb([128, 2, D], BF16, "h2tok")
                htok2_b = [Buf("htok0"), Buf("htok1")]
                qT_t, _ = A.sb([128, 16, 128], BF16, "qT")
                qTb = [Buf("qT%d" % g) for g in range(4)]
                s_t, s_b = A.sb([128, 16, 128], F32, "s")
                s2_t, s2_b = A.sb([128, 16, 128], F32, "s2")
                sv_t, sv_b = A.sb([128, 16, 16], F32, "sv")
                si_t, si_b = A.sb([128, 16, 16], U32, "si")
                sif_t, sif_b = A.sb([128, 16, 16], F32, "sif")
                cand_t, cand_b = A.sb([128, 8, 256], F32, "cand")
                cand2_t, cand2_b = s2_t[:, :, :].rearrange("p g n -> p (g n)").rearrange("p (h c) -> p h c", h=8), s2_b
                top_t, top_b = A.sb([128, 8, 16], F32, "top")
                pos_t, pos_b = A.sb([128, 8, 16], U32, "pos")
                ai_t, ai_b = A.sb([128, 8, 16], U32, "ai")
                bi_t, bi_b = A.sb([128, 8, 16], U32, "bi")
                af_t, af_b = A.sb([128, 8, 16], F32, "af")
                bf_t, bf_b = A.sb([128, 8, 16], F32, "bf")
                oh_t, oh_b = s_t[:, :, :].rearrange("p g n -> p (g n)").rearrange("p (h c) -> p h c", h=8), s_b
                tmp_t, tmp_b = cand_t, cand_b
                isel_t, isel_b = A.sb([128, 128], F32, "isel")
                jsel_t, jsel_b = A.sb([128, 128], F32, "jsel")
                eidf_t, eidf_b = A.sb([128, 128], F32, "eidf")
                eid2_t, _ = A.sb([128, 2, 128], I32, "eid")
                eid2_b = [Buf("eid0"), Buf("eid1")]
                ew_t, ew_b = A.sb([128, 128], F32, "ew")
                ssum_t, ssum_b = A.sb([128, 8], F32, "ssum")
                gw2_t, _ = A.sb([128, 2, 128], F32, "gw")
                gw2_b = [Buf("gw0"), Buf("gw1")]
                a_t, a_b = A.sb([128, 128], F32, "a")
                g1_t, g1_b = A.sb([128, 128], F32, "g1")
                g2_t, g2_b = A.sb([128, 128], F32, "g2")
                actw_t, actw_b = A.sb([128, 128], F32, "actw")
                junk_t, junk_b = A.sb([128, D], BF16, "junk")
                ng = NG if do_mixer else NG - 2
                gb_t, _ = A.sb([128, ng, 2 * D], BF16, "gbuf")
                gbb = [Buf("gb%d" % i) for i in range(ng)]
                acc_t, acc_b = gb_t[:, 0, :].bitcast(F32), gbb[0]
                dg_t, _ = A.sb([128, 4, 128], BF16, "diag")
                dgb = [Buf("dg%d" % i) for i in range(4)]
                g1b = [Buf("g1_%d" % i) for i in range(32)]
                g2b = [Buf("g2_%d" % i) for i in range(32)]
                awb = [Buf("aw_%d" % i) for i in range(32)]
                if cstate["i"] < NCV:
                    stg_t, _ = A.sb([128, 2, 2, D], BF16, "stg")
                    stgb_p = [Buf("stg0"), Buf("stg1")]
                    cstate["stg"] = (stg_t, stgb_p)
                    while cstate["i"] < NCV:
                        conv_step()
                    S.barrier(stgb_p)

                posf_t, posf_b = A.sb([128, 8, 16], F32, "posf")
                thr_t, thr_b = A.sb([128, 16], F32, "thr")
                S.op("dve", lambda e: e.tensor_scalar(out=thr_t[:], in0=iota16, scalar1=16.0, scalar2=16.0,
                                                      op0=ALU.mult, op1=ALU.add), R=[cst_b], W=[thr_b])
                S.dma("pool", wq_t[:], wq_d[l], W=[wq_b])
                S.dma("pool", ky_t[:], keys_d[l], W=[ky_b])
                gi = [0]
                ptiles = list(range(NKT))
                if l == n_layers - 1:
                    ptiles = ptiles[1:]
                def front(X, tt, par):
                    htok_t, htok_b = htok2_t[:, par, :], htok2_b[par]
                    eid_t, eid_b = eid2_t[:, par, :], eid2_b[par]
                    gw_t, gw_b = gw2_t[:, par, :], gw2_b[par]
                    c0, np_ = ktiles[tt]
                    bi = 0 if tt == 0 else 1 + (tt - 1) // 2
                    xs = lambda c: x_t[:, c, c0:c0 + np_]
                    rms_feature(X, xs, xb[bi], 8, 128, np_, float(D), V_FFNG, vt, vb,
                                lambda c: hT_t[:, c, 0:np_], hTb, sq_t, sqb, ln_t, ln_b, rs_t, rs_b)
                    ptk_f, ptk_b = ps_next()
                    ptk = ptk_f.bitcast(BF16)
                    for c in range(8):
                        X.op("pe", lambda e, c=c: e.transpose(out=ptk[0:np_, c * 128:(c + 1) * 128],
                                                              in_=hT_t[:, c, 0:np_], identity=identb_t[:]),
                             R=[hTb[c], identb_b], W=[ptk_b])
                    X.op("act", lambda e: e.activation(out=htok_t[0:np_, :], in_=ptk[0:np_, :], func=AF.Copy),
                         R=[ptk_b], W=[htok_b])
                    for gq in range(4):
                        pq, pqb = ps_next()
                        for gg in range(4):
                            g = gq * 4 + gg
                            for kc in range(8):
                                X.op("pe", lambda e, g=g, gg=gg, kc=kc: e.matmul(
                                    pq[:, gg * 128:gg * 128 + np_],
                                    lhsT=wq_t[:, kc * 2048 + g * 128:kc * 2048 + (g + 1) * 128],
                                    rhs=hT_t[:, kc, 0:np_], start=(kc == 0), stop=(kc == 7)),
                                    R=[wq_b, hTb[kc]], W=[pqb])
                        X.op("act", lambda e, gq=gq: e.activation(
                            out=qT_t[:, gq * 4:(gq + 1) * 4, 0:np_],
                            in_=pq[:, :].rearrange("p (g t) -> p g t", g=4)[:, :, 0:np_], func=AF.Copy),
                            R=[pqb], W=[qTb[gq]])
                    for gq in range(4):
                        pss, pssb = ps_next()
                        for gg in range(4):
                            g = gq * 4 + gg
                            X.op("pe", lambda e, g=g, gg=gg: e.matmul(
                                pss[0:np_, gg * 128:(gg + 1) * 128], lhsT=qT_t[:, g, 0:np_],
                                rhs=ky_t[:, g * 128:(g + 1) * 128], start=True, stop=True),
                                R=[qTb[gq], ky_b], W=[pssb])
                        X.op("act", lambda e, gq=gq: e.activation(
                            out=s_t[0:np_, gq * 4:(gq + 1) * 4, :],
                            in_=pss[0:np_, :].rearrange("p (g n) -> p g n", g=4), func=AF.Copy),
                            R=[pssb], W=[s_b])
                    P = np_
                    for g in range(16):
                        X.op("dve", lambda e, g=g: e.max(out=sv_t[0:P, g, 0:8], in_=s_t[0:P, g, :]), R=[s_b], W=[sv_b])
                    for g in range(16):
                        X.op("dve", lambda e, g=g: e.max_index(out=si_t[0:P, g, 0:8], in_max=sv_t[0:P, g, 0:8],
                                                               in_values=s_t[0:P, g, :]), R=[s_b, sv_b], W=[si_b])
                    for g in range(16):
                        X.op("dve", lambda e, g=g: e.match_replace(out=s2_t[0:P, g, :], in_to_replace=sv_t[0:P, g, 0:8],
                                                                   in_values=s_t[0:P, g, :], imm_value=-1e30),
                             R=[s_b, sv_b], W=[s2_b])
                    for g in range(16):
                        X.op("dve", lambda e, g=g: e.max(out=sv_t[0:P, g, 8:16], in_=s2_t[0:P, g, :]), R=[s2_b], W=[sv_b])
                    for g in range(16):
                        X.op("dve", lambda e, g=g: e.max_index(out=si_t[0:P, g, 8:16], in_max=sv_t[0:P, g, 8:16],
                                                               in_values=s2_t[0:P, g, :]), R=[s2_b, sv_b], W=[si_b])
                    X.op("dve", lambda e: e.tensor_copy(out=sif_t[0:P], in_=si_t[0:P]), R=[si_b], W=[sif_b])
                    cand4 = cand_t[0:P].rearrange("p h (a b) -> p h a b", a=16)
                    X.op("dve", lambda e: e.tensor_tensor(
                        out=cand4, in0=sv_t[0:P, 0::2, :].unsqueeze(3).broadcast_to([P, 8, 16, 16]),
                        in1=sv_t[0:P, 1::2, :].unsqueeze(2).broadcast_to([P, 8, 16, 16]), op=ALU.add),
                        R=[sv_b], W=[cand_b])
                    for h in range(8):
                        X.op("dve", lambda e, h=h: e.max(out=top_t[0:P, h, 0:8], in_=cand_t[0:P, h, :]), R=[cand_b], W=[top_b])
                    for h in range(8):
                        X.op("dve", lambda e, h=h: e.max_index(out=pos_t[0:P, h, 0:8], in_max=top_t[0:P, h, 0:8],
                                                               in_values=cand_t[0:P, h, :]), R=[cand_b, top_b], W=[pos_b])
                    for h in range(8):
                        X.op("dve", lambda e, h=h: e.match_replace(out=cand2_t[0:P, h, :], in_to_replace=top_t[0:P, h, 0:8],
                                                                   in_values=cand_t[0:P, h, :], imm_value=-1e30),
                             R=[cand_b, top_b], W=[cand2_b])
                    for h in range(8):
                        X.op("dve", lambda e, h=h: e.max(out=top_t[0:P, h, 8:16], in_=cand2_t[0:P, h, :]), R=[cand2_b], W=[top_b])
                    for h in range(8):
                        X.op("dve", lambda e, h=h: e.max_index(out=pos_t[0:P, h, 8:16], in_max=top_t[0:P, h, 8:16],
                                                               in_values=cand2_t[0:P, h, :]), R=[cand2_b, top_b], W=[pos_b])
                    oh4 = oh_t[0:P].rearrange("p h (k a) -> p h k a", k=16)
                    X.op("dve", lambda e: e.tensor_copy(out=posf_t[0:P], in_=pos_t[0:P]), R=[pos_b], W=[posf_b])
                    X.op("dve", lambda e: e.tensor_tensor(
                        out=oh4, in0=posf_t[0:P].unsqueeze(3).broadcast_to([P, 8, 16, 16]),
                        in1=thr_t[0:P, :].unsqueeze(1).unsqueeze(1).broadcast_to([P, 8, 16, 16]), op=ALU.is_ge),
                        R=[posf_b, thr_b], W=[oh_b])
                    X.op("dve", lambda e: e.tensor_reduce(
                        out=af_t[0:P].rearrange("p h k -> p (h k)"),
                        in_=oh_t[0:P].rearrange("p h (k a) -> p (h k) a", k=16), axis=AX.X, op=ALU.add),
                        R=[oh_b], W=[af_b])
                    X.op("dve", lambda e: e.scalar_tensor_tensor(
                        out=bf_t[0:P].rearrange("p h k -> p (h k)"), in0=af_t[0:P].rearrange("p h k -> p (h k)"),
                        scalar=-16.0, in1=posf_t[0:P].rearrange("p h k -> p (h k)"), op0=ALU.mult, op1=ALU.add),
                        R=[af_b, posf_b], W=[bf_b])
                    oh4 = oh_t[0:P].rearrange("p h (k a) -> p h k a", k=16)
                    tmp4 = tmp_t[0:P].rearrange("p h (k a) -> p h k a", k=16)
                    io4 = iota16[0:P, :].unsqueeze(1).unsqueeze(1).broadcast_to([P, 8, 16, 16])
                    for (xf_t, xf_b, par, dst_t, dst_b) in ((af_t, af_b, 0, isel_t, isel_b),
                                                            (bf_t, bf_b, 1, jsel_t, jsel_b)):
                        X.op("dve", lambda e, xf_t=xf_t: e.tensor_tensor(
                            out=oh4, in0=xf_t[0:P].unsqueeze(3).broadcast_to([P, 8, 16, 16]), in1=io4,
                            op=ALU.is_equal), R=[xf_b, cst_b], W=[oh_b])
                        X.op("dve", lambda e, par=par: e.tensor_tensor(
                            out=tmp4, in0=oh4,
                            in1=sif_t[0:P, par::2, :].unsqueeze(2).broadcast_to([P, 8, 16, 16]), op=ALU.mult),
                            R=[oh_b, sif_b], W=[tmp_b])
                        X.op("dve", lambda e, dst_t=dst_t: e.tensor_reduce(
                            out=dst_t[0:P, :], in_=tmp_t[0:P].rearrange("p h (k a) -> p (h k) a", k=16),
                            axis=AX.X, op=ALU.add), R=[tmp_b], W=[dst_b])
                    X.op("dve", lambda e: e.scalar_tensor_tensor(out=eidf_t[0:P, :], in0=isel_t[0:P, :], scalar=128.0,
                                                                 in1=jsel_t[0:P, :], op0=ALU.mult, op1=ALU.add),
                         R=[isel_b, jsel_b], W=[eidf_b])
                    X.op("dve", lambda e: e.tensor_copy(out=eid_t[0:P, :], in_=eidf_t[0:P, :]), R=[eidf_b], W=[eid_b])
                    ew3 = ew_t[0:P, :].rearrange("p (h k) -> p h k", h=8)
                    X.op("dve", lambda e: e.tensor_tensor(out=ew3, in0=top_t[0:P],
                                                          in1=top_t[0:P, :, 0:1].broadcast_to([P, 8, 16]),
                                                          op=ALU.subtract), R=[top_b], W=[ew_b])
                    X.op("act", lambda e: e.activation(out=ew_t[0:P, :], in_=ew_t[0:P, :], func=AF.Exp),
                         R=[ew_b], W=[ew_b])
                    X.op("dve", lambda e: e.tensor_reduce(out=ssum_t[0:P, :], in_=ew3, axis=AX.X, op=ALU.add),
                         R=[ew_b], W=[ssum_b])
                    X.op("dve", lambda e: e.reciprocal(out=ssum_t[0:P, :], in_=ssum_t[0:P, :]), R=[ssum_b], W=[ssum_b])
                    X.op("dve", lambda e: e.tensor_tensor(out=gw_t[0:P, :].rearrange("p (h k) -> p h k", h=8), in0=ew3,
                                                          in1=ssum_t[0:P, :].unsqueeze(2).broadcast_to([P, 8, 16]),
                                                          op=ALU.mult), R=[ew_b, ssum_b], W=[gw_b])
                def back(tt, par, nxt):
                    c0, np_ = ktiles[tt]
                    P = np_
                    bi = 0 if tt == 0 else 1 + (tt - 1) // 2
                    htok_t, htok_b = htok2_t[:, par, :], htok2_b[par]
                    eid_t, eid_b = eid2_t[:, par, :], eid2_b[par]
                    gw_t, gw_b = gw2_t[:, par, :], gw2_b[par]
                    per = (len(nxt.th) + 31) // 32 if nxt is not None else 0
                    acc0, acc0b = pst[0], psb[0]
                    acc1, acc1b = pst[1], psb[1]
                    di = [0]
                    for k in range(128):
                        gbi = gi[0] % ng
                        gi[0] += 1
                        S.dma("pool", gb_t[0:P, gbi, :], uv_d[l],
                              R=[eid_b], W=[gbb[gbi]],
                              indirect=bass.IndirectOffsetOnAxis(ap=eid_t[0:P, k:k + 1], axis=0))
                        S.op("dve", lambda e: e.scalar_tensor_tensor(
                            out=junk_t[0:P, :], in0=gb_t[0:P, gbi, 0:D], scalar=1.0, in1=htok_t[0:P, :],
                            op0=ALU.mult, op1=ALU.mult, accum_out=a_t[0:P, k:k + 1]),
                            R=[gbb[gbi], htok_b], W=[junk_b, a_b])
                        if k % 4 != 3:
                            continue
                        g = k // 4
                        sl = slice(4 * g, 4 * g + 4)
                        S.op("act", lambda e: e.activation(out=g2_t[0:P, sl], in_=a_t[0:P, sl], func=AF.Gelu_apprx_tanh),
                             R=[a_b], W=[g2b[g]])
                        S.op("dve", lambda e: e.tensor_tensor(out=actw_t[0:P, sl], in0=g2_t[0:P, sl], in1=gw_t[0:P, sl], op=ALU.mult),
                             R=[g2b[g], gw_b], W=[awb[g]])
                        for kk in range(4 * g, 4 * g + 4):
                            gbk = (gi[0] - (k + 1) + kk) % ng
                            dj = di[0] % 4
                            di[0] += 1
                            S.op("act", lambda e: e.activation(out=dg_t[0:P, dj, 0:P], in_=identb_t[0:P, 0:P], func=AF.Copy,
                                                               scale=actw_t[0:P, kk:kk + 1]),
                                 R=[identb_b, awb[g]], W=[dgb[dj]])
                            S.op("pe", lambda e: e.matmul(acc0[0:P, 0:512], lhsT=dg_t[0:P, dj, 0:P], rhs=gb_t[0:P, gbk, D:D + 512],
                                                          start=(kk == 0), stop=(kk == 127)),
                                 R=[dgb[dj], gbb[gbk]], W=[acc0b])
                            S.op("pe", lambda e: e.matmul(acc1[0:P, 0:512], lhsT=dg_t[0:P, dj, 0:P], rhs=gb_t[0:P, gbk, D + 512:2 * D],
                                                          start=(kk == 0), stop=(kk == 127)),
                                 R=[dgb[dj], gbb[gbk]], W=[acc1b])
                        if nxt is not None:
                            nxt.flush(per)
                    S.op("act", lambda e: e.activation(out=acc_t[0:P, 0:512], in_=acc0[0:P, 0:512], func=AF.Copy),
                         R=[acc0b], W=[acc_b])
                    S.op("act", lambda e: e.activation(out=acc_t[0:P, 512:1024], in_=acc1[0:P, 0:512], func=AF.Copy),
                         R=[acc1b], W=[acc_b])
                    for c in range(8):
                        ptr, ptrb = ps_next()
                        S.op("pe", lambda e, c=c: e.transpose(out=ptr[:, 0:P], in_=acc_t[0:P, c * 128:(c + 1) * 128],
                                                              identity=identf[0:P, 0:P]),
                             R=[acc_b, cst_b], W=[ptrb])
                        S.op("dve", lambda e, c=c: e.tensor_tensor(out=x_t[:, c, c0:c0 + P], in0=x_t[:, c, c0:c0 + P],
                                                                   in1=ptr[:, 0:P], op=ALU.add),
                             R=[ptrb, xb[bi][c]], W=[xb[bi][c]])
                r0 = Rec(S)
                front(r0, ptiles[0], 0)
                r0.flush()
                for idx, tt in enumerate(ptiles):
                    nxt = None
                    if idx + 1 < len(ptiles):
                        nxt = Rec(S)
                        front(nxt, ptiles[idx + 1], (idx + 1) % 2)
                    back(tt, idx % 2, nxt)
                    if nxt is not None:
                        nxt.flush()
                S.barrier()

        outb = Buf("out")
        for c in range(8):
            rl = [xb[1 + b][c] for b in range(NRB)]
            S.dma("sp", out_d[c * 128:(c + 1) * 128, :], x_t[:, c, NMETA:L], R=rl, sembuf=outb)
        S.wait_all("sp", [outb] + [xb[1 + b][c] for b in range(NRB) for c in range(8)])
        nc._stats = (S.nins, S.nwait, S.ndsem, dict(S.cnt))
    return nc


def _wstream(w_in, w_conv_out, w_mla_out, w_out):
    tiles = []

    def ztile(c0, m):
        t = np.zeros((128, 8, 128), np.float32)
        t[:, :, :m] = w_in[:, c0:c0 + m].reshape(8, 128, m).transpose(1, 0, 2)
        tiles.append(t.reshape(128, 1024))

    for c in range(4):
        ztile(128 * c, 128)
        ztile(512 + 128 * c, 128)
    ztile(1024, 128)
    ztile(1152, 128)
    ztile(1280, 128)
    ztile(1344, 96)
    for c in range(8):
        ztile(1440 + 128 * c, 128)
        t = np.zeros((128, 8, 128), np.float32)
        t[:, 0:4, :] = w_conv_out[:, 128 * c:128 * (c + 1)].reshape(4, 128, 128).transpose(1, 0, 2)
        tiles.append(t.reshape(128, 1024))
        ztile(2464 + 128 * c, 128)
        t = np.zeros((128, 8, 128), np.float32)
        t[0:64, :, :] = w_mla_out[:, 128 * c:128 * (c + 1)].reshape(8, 64, 128).transpose(1, 0, 2)
        tiles.append(t.reshape(128, 1024))
    for c in range(8):
        t = w_out[:, 128 * c:128 * (c + 1)].reshape(8, 128, 128).transpose(1, 0, 2)
        tiles.append(np.ascontiguousarray(t).reshape(128, 1024))
    assert len(tiles) == NWT
    return np.stack(tiles)


def _prep_shared(inp):
    f = lambda a: np.ascontiguousarray(np.asarray(a, dtype=np.float32))
    sh = {}
    sh["wstream"] = np.stack([_wstream(f(inp["w_in"][l]), f(inp["w_conv_out"][l]), f(inp["w_mla_out"][l]),
                                       f(inp["w_out"][l])) for l in range(DEPTH)])
    sh["wuq"] = f(np.stack([f(inp["w_uq"][l]).reshape(2, 128, 768).transpose(1, 0, 2).reshape(128, 1536)
                            for l in range(DEPTH)]))
    wukv = f(inp["w_ukv"]).reshape(DEPTH, 128, 8, 128)
    sh["wkn"] = f(wukv[:, :, :, 0:64].reshape(DEPTH, 128, 512))
    sh["wv"] = f(wukv[:, :, :, 64:128].reshape(DEPTH, 128, 512))
    vecs = np.zeros((DEPTH, 128, NVEC), np.float32)
    for l in range(DEPTH):
        vecs[l, :, V_MIXG:V_MIXG + 8] = f(inp["mix_norm_g"][l]).reshape(8, 128).T
        vecs[l, :, V_FFNG:V_FFNG + 8] = f(inp["ffn_norm_g"][l]).reshape(8, 128).T
        cw = f(inp["conv_w"][l]).reshape(31, 4, 128)
        vecs[l, :, V_CONVW:V_CONVW + 124] = cw.transpose(2, 1, 0).reshape(128, 124)
        vecs[l, :, V_CONVB:V_CONVB + 4] = f(inp["conv_b"][l]).reshape(4, 128).T
        vecs[l, :, V_LNG:V_LNG + 4] = f(inp["conv_ln_g"][l]).reshape(4, 128).T
        vecs[l, :, V_LNB:V_LNB + 4] = f(inp["conv_ln_b"][l]).reshape(4, 128).T
        vecs[l, :, V_QAG:V_QAG + 2] = f(inp["q_a_norm_g"][l]).reshape(2, 128).T
        vecs[l, :, V_KVAG] = f(inp["kv_a_norm_g"][l])
        vecs[l, 0:96, V_QNG] = f(inp["q_norm_g"][l])
        vecs[l, 0:96, V_KNG] = f(inp["k_norm_g"][l])
    sh["vecs"] = vecs
    sh["wq"] = f(np.stack([f(inp["peer_wq"][l]).reshape(8, 128, 2048).transpose(1, 0, 2).reshape(128, 8 * 2048)
                           for l in range(DEPTH)]))
    ky = f(inp["peer_keys"]).reshape(DEPTH, 16, 128, 128)
    sh["keysT"] = f(ky.transpose(0, 3, 1, 2).reshape(DEPTH, 128, 16 * 128))
    for l in range(DEPTH):
        sh["peer_u%d" % l] = f(inp["peer_u"][l])
        sh["peer_v%d" % l] = f(inp["peer_v"][l])
    pos = np.arange(L, dtype=np.float32)
    inv = (1.0 / (np.float32(10000.0) ** (np.arange(0, 32, 2, dtype=np.float32) / np.float32(32)))).astype(np.float32)
    ang = pos[:, None] * inv[None, :]
    ang = np.concatenate([ang, ang], axis=-1)
    cosT = np.zeros((96, L), np.float32)
    sinT = np.zeros((96, L), np.float32)
    cosT[64:96] = np.cos(ang).T
    sinT[64:96] = np.sin(ang).T
    sh["cosT"] = cosT
    sh["sinT"] = sinT
    cst = np.zeros((128, 240), np.float32)
    cst[:, 0:128] = np.eye(128, dtype=np.float32)
    prot = np.zeros((128, 96), np.float32)
    for m in range(16):
        prot[64 + m + 16, 64 + m] = -1.0
        prot[64 + m, 64 + m + 16] = 1.0
    cst[:, 128:224] = prot
    cst[:, 224:240] = np.arange(16, dtype=np.float32)[None, :]
    sh["consts"] = cst
    sh["metaT"] = f(f(inp["meta_tokens"]).T)
    return sh


_CACHE = {}


def kernel(**inputs):
    x = np.asarray(inputs["x"], dtype=np.float32)
    nb = x.shape[0]
    sh = _prep_shared(inputs)
    if "nc" not in _CACHE:
        _CACHE["nc"] = build_program()
    nc = _CACHE["nc"]
    in_maps = []
    for b in range(nb):
        m = dict(sh)
        m["xT"] = np.ascontiguousarray(x[b].T)
        in_maps.append(m)
    res = run_bass_kernel_spmd(nc, in_maps, core_ids=list(range(nb)))
    out = np.stack([np.asarray(r["outT"], dtype=np.float32).T for r in res.results], axis=0)
    return np.ascontiguousarray(out)
```

```python
import contextlib
import numpy as np
import concourse.bass as bass
import concourse.mybir as mybir
from concourse.bass_utils import run_bass_kernel_spmd

F32 = mybir.dt.float32
BF16 = mybir.dt.bfloat16
I32 = mybir.dt.int32
U32 = mybir.dt.uint32
ALU = mybir.AluOpType
AF = mybir.ActivationFunctionType
AX = mybir.AxisListType

D = 1024
SEQ = 2048
NMETA = 16
L = SEQ + NMETA
DEPTH = 2
NB = 256
NRB = SEQ // NB
NSLOT = 6
NWT = 52
NKT = 1 + SEQ // 128
NEXP = 16384
NG = 12
NVEC = 160
V_MIXG, V_FFNG, V_CONVW, V_CONVB, V_LNG, V_LNB, V_QAG, V_KVAG, V_QNG, V_KNG = (
    0, 8, 16, 140, 144, 148, 152, 154, 155, 156)
EPS = 1e-6
SCALE = 96.0 ** -0.5


class Buf:
    __slots__ = ("name", "w", "r", "dsem", "dval")

    def __init__(self, name):
        self.name = name
        self.w = None
        self.r = {}
        self.dsem = None
        self.dval = 0


class Sched:
    def __init__(self, nc, stack):
        self.nc = nc
        self.stack = stack
        self.eng = {"pe": nc.tensor, "act": nc.scalar, "dve": nc.vector,
                    "pool": nc.gpsimd, "sp": nc.sync}
        self.sems = {}
        self.cnt = {}
        self.known = {}
        self.snap = {}
        for e in self.eng:
            self.sems[e] = stack.enter_context(nc.semaphore("s_" + e))
            self.cnt[e] = 0
            self.known[e] = {}
            self.snap[e] = {}
        self.ndsem = 0
        self.nwait = 0
        self.nins = 0

    def new_dsem(self):
        k = "d%d" % self.ndsem
        self.ndsem += 1
        self.sems[k] = self.stack.enter_context(self.nc.semaphore("s_" + k))
        return k

    def _need(self, e, deps, key, val):
        if self.known[e].get(key, 0) >= val:
            return
        if deps.get(key, 0) < val:
            deps[key] = val

    def _collect(self, e, R, W):
        deps = {}
        for b in R:
            if b.w is not None:
                k, v = b.w
                if k == e and e == "pe":
                    continue
                self._need(e, deps, k, v)
        for b in W:
            if b.w is not None:
                k, v = b.w
                if k != e:
                    self._need(e, deps, k, v)
            for k, v in b.r.items():
                if k != e:
                    self._need(e, deps, k, v)
        return deps

    def _emit_waits(self, e, deps):
        for k, v in deps.items():
            self.eng[e].wait_ge(self.sems[k], v)
            self.nwait += 1
            kn = self.known[e]
            if kn.get(k, 0) < v:
                kn[k] = v
            sn = self.snap.get(k, {}).get(v)
            if sn:
                for kk, vv in sn.items():
                    if kn.get(kk, 0) < vv:
                        kn[kk] = vv

    def op(self, e, fn, R=(), W=()):
        deps = self._collect(e, R, W)
        self._emit_waits(e, deps)
        ins = fn(self.eng[e])
        self.cnt[e] += 1
        n = self.cnt[e]
        ins.then_inc(self.sems[e], 1)
        self.nins += 1
        self.snap[e][n] = dict(self.known[e])
        for b in R:
            b.r[e] = n
        for b in W:
            b.w = (e, n)
            b.r = {}
        return ins

    def dma(self, q, out, in_, R=(), W=(), sembuf=None, indirect=None):
        deps = self._collect(q, R, W)
        self._emit_waits(q, deps)
        sb = sembuf if sembuf is not None else (W[0] if W else R[0])
        if sb.dsem is None:
            sb.dsem = self.new_dsem()
        if indirect is not None:
            ins = self.eng[q].indirect_dma_start(out=out, out_offset=None, in_=in_,
                                                 in_offset=indirect)
        else:
            ins = self.eng[q].dma_start(out=out, in_=in_)
        sb.dval += 16
        ins.then_inc(self.sems[sb.dsem], 16)
        tok = (sb.dsem, sb.dval)
        for b in R:
            b.r[tok[0]] = tok[1]
        for b in W:
            b.w = tok
            b.r = {}
        return tok

    def wait_all(self, e, bufs):
        deps = {}
        for b in bufs:
            if b.w is not None:
                self._need(e, deps, b.w[0], b.w[1])
            for k, v in b.r.items():
                self._need(e, deps, k, v)
        self._emit_waits(e, deps)

    def barrier(self, bufs=()):
        for e in self.eng:
            deps = {}
            for f in self.eng:
                if f != e and self.cnt[f] > 0:
                    self._need(e, deps, f, self.cnt[f])
            for b in bufs:
                if b.w is not None:
                    self._need(e, deps, b.w[0], b.w[1])
                for k, v in b.r.items():
                    self._need(e, deps, k, v)
            self._emit_waits(e, deps)


class Alloc:
    def __init__(self, nc, stack):
        self.nc = nc
        self.stack = stack
        self.n = [0]

    def sub(self, stack):
        a = Alloc(self.nc, stack)
        a.n = self.n
        return a

    def sb(self, shape, dt, name="t"):
        self.n[0] += 1
        nm = "%s_%d" % (name, self.n[0])
        t = self.stack.enter_context(self.nc.sbuf_tensor(nm, list(shape), dt))
        return t, Buf(nm)

    def ps(self, shape, dt, name="p"):
        self.n[0] += 1
        nm = "%s_%d" % (name, self.n[0])
        t = self.stack.enter_context(self.nc.psum_tensor(nm, list(shape), dt))
        return t, Buf(nm)


class _Proxy:
    def __init__(self):
        self.call = None

    def __getattr__(self, name):
        def f(*a, **k):
            self.call = (name, a, k)
            return self
        return f


class Rec:
    def __init__(self, S):
        self.S = S
        self.th = []

    def op(self, e, fn, R=(), W=()):
        p = _Proxy()
        fn(p)
        self.th.append((e, p.call, list(R), list(W)))

    def flush(self, n=None):
        k = len(self.th) if n is None else min(n, len(self.th))
        for _ in range(k):
            e, (name, a, kw), R, W = self.th.pop(0)
            self.S.op(e, lambda eng: getattr(eng, name)(*a, **kw), R=R, W=W)


def build_program(n_layers=DEPTH, do_mixer=True, do_peer=True):
    nc = bass.Bass("TRN2", target_bir_lowering=False)

    def din(name, shape, dt=F32):
        return nc.dram_tensor(name, list(shape), dt, kind="ExternalInput").ap()

    xT_d = din("xT", [D, SEQ])
    meta_d = din("metaT", [D, NMETA])
    wst_d = din("wstream", [DEPTH, NWT, 128, 1024])
    wuq_d = din("wuq", [DEPTH, 128, 2 * 768])
    wkn_d = din("wkn", [DEPTH, 128, 512])
    wv_d = din("wv", [DEPTH, 128, 512])
    vecs_d = din("vecs", [DEPTH, 128, NVEC])
    wq_d = din("wq", [DEPTH, 128, 8 * 2048])
    keys_d = din("keysT", [DEPTH, 128, 16 * 128])
    pu_d = [din("peer_u%d" % l, [NEXP, D]) for l in range(DEPTH)]
    pv_d = [din("peer_v%d" % l, [NEXP, D]) for l in range(DEPTH)]
    uv_d = [nc.dram_tensor("uvbf%d" % l, [NEXP, 2 * D], BF16).ap() for l in range(DEPTH)]
    wbf_d = nc.dram_tensor("wbf", [DEPTH, NWT, 128, 1024], BF16).ap()
    cos_d = din("cosT", [96, L])
    sin_d = din("sinT", [96, L])
    cst_d = din("consts", [128, 128 + 96 + 16])
    out_d = nc.dram_tensor("outT", [D, SEQ], F32, kind="ExternalOutput").ap()

    with contextlib.ExitStack() as st0:
        S = Sched(nc, st0)
        A0 = Alloc(nc, st0)

        x_t, _ = A0.sb([128, 8, L], F32, "x")
        xb = [[Buf("x%d_%d" % (b, c)) for c in range(8)] for b in range(1 + NRB)]
        cst_t, cst_b = A0.sb([128, 240], F32, "cst")
        identf = cst_t[:, 0:128]
        iota16 = cst_t[:, 224:240]
        identb_t, identb_b = A0.sb([128, 128], BF16, "identb")
        prot_t, prot_b = A0.sb([128, 96], BF16, "prot")
        ones_t, ones_b = A0.sb([128, 128], BF16, "ones")
        eps_t, eps_b = A0.sb([128, 1], F32, "eps")
        vecs_t = []
        vecs_b = []
        for l in range(DEPTH):
            t, b = A0.sb([128, NVEC], F32, "vecs")
            vecs_t.append(t)
            vecs_b.append(b)
        ps_all, _ = A0.ps([128, 8, 512], F32, "psum")
        pst = [ps_all[:, i, :] for i in range(8)]
        psb = [Buf("bank%d" % i) for i in range(8)]
        psrr = [0]
        pprr = [0]

        def ps_next():
            i = 4 + psrr[0] % 4
            psrr[0] += 1
            return pst[i], psb[i]

        def ps_pair():
            j = 4 + 2 * (pprr[0] % 2)
            pprr[0] += 1
            return ps_all[:, j:j + 2, :].rearrange("p b n -> p (b n)"), [psb[j], psb[j + 1]]

        S.dma("sp", cst_t[:], cst_d, W=[cst_b])
        for l in range(DEPTH):
            S.dma("sp", vecs_t[l][:], vecs_d[l], W=[vecs_b[l]])
        for c in range(8):
            S.dma("sp", x_t[:, c, 0:NMETA], meta_d[c * 128:(c + 1) * 128, :], W=[xb[0][c]])
        for c in range(8):
            wl = [xb[1 + b][c] for b in range(NRB)]
            S.dma("sp", x_t[:, c, NMETA:L], xT_d[c * 128:(c + 1) * 128, :], W=wl, sembuf=wl[0])
        S.op("pool", lambda e: e.memset(ones_t[:], 1.0), W=[ones_b])
        S.op("pool", lambda e: e.memset(eps_t[:], EPS), W=[eps_b])
        S.op("dve", lambda e: e.tensor_copy(out=identb_t[:], in_=identf), R=[cst_b], W=[identb_b])
        S.op("dve", lambda e: e.tensor_copy(out=prot_t[:], in_=cst_t[:, 128:224]), R=[cst_b], W=[prot_b])

        blocks = [(0, NMETA)] + [(NMETA + NB * b, NB) for b in range(NRB)]
        ktiles = [(0, NMETA)] + [(NMETA + 128 * m, 128) for m in range(SEQ // 128)]

        def rms_feature(X, src_fn, src_bufs, nchunk, npart, n, dim, gcol, vt, vb, dst_fn, dst_bufs,
                        sq_t, sq_b, ln_t, ln_b, rs_t, rs_b):
            pt, pb = ps_next()
            for c in range(nchunk):
                X.op("act", lambda e, c=c: e.activation(out=sq_t[0:npart, c % 2, 0:n], in_=src_fn(c),
                                                        func=AF.Square),
                     R=[src_bufs[c]], W=[sq_b[c % 2]])
                X.op("pe", lambda e, c=c: e.matmul(pt[0:npart, 0:n], lhsT=ones_t[0:npart, 0:npart],
                                                   rhs=sq_t[0:npart, c % 2, 0:n],
                                                   start=(c == 0), stop=(c == nchunk - 1)),
                     R=[ones_b, sq_b[c % 2]], W=[pb])
            X.op("act", lambda e: e.activation(out=ln_t[0:npart, 0:n], in_=pt[0:npart, 0:n], func=AF.Ln,
                                               bias=eps_t[0:npart, :], scale=1.0 / dim),
                 R=[pb, eps_b], W=[ln_b])
            X.op("act", lambda e: e.activation(out=rs_t[0:npart, 0:n], in_=ln_t[0:npart, 0:n], func=AF.Exp,
                                               scale=-0.5),
                 R=[ln_b], W=[rs_b])
            for c in range(nchunk):
                X.op("dve", lambda e, c=c: e.scalar_tensor_tensor(
                    out=dst_fn(c), in0=src_fn(c), scalar=vt[0:npart, gcol + c:gcol + c + 1],
                    in1=rs_t[0:npart, 0:n], op0=ALU.mult, op1=ALU.mult),
                    R=[src_bufs[c], vb, rs_b], W=[dst_bufs[c]])

        with contextlib.ExitStack() as stw:
            Aw = A0.sub(stw)
            ws_t, _ = Aw.sb([128, 8, 1024], BF16, "wstg")
            wsb = [Buf("wstg%d" % i) for i in range(8)]
            wso = [Buf("wstgo%d" % i) for i in range(8)]
            for i in range(n_layers * NWT):
                l_, ti = divmod(i, NWT)
                sbi = i % 8
                S.dma("pool", ws_t[:, sbi, :], wst_d[l_, ti], W=[wsb[sbi]])
                S.dma("sp", wbf_d[l_, ti], ws_t[:, sbi, :], R=[wsb[sbi]], sembuf=wso[sbi])
            S.barrier(wsb)

        for l in range(n_layers):
            vt = vecs_t[l]
            vb = vecs_b[l]
            cstate = {"i": 0, "stg": None, "osem": [Buf("stgo0"), Buf("stgo1")]}
            NCV = 2 * (NEXP // 256)

            def conv_step():
                i = cstate["i"]
                if i >= NCV:
                    return
                cstate["i"] += 1
                stg_t, stgb = cstate["stg"]
                src = (pu_d[l], pv_d[l])[i % 2]
                dst = uv_d[l][:, (i % 2) * D:(i % 2 + 1) * D]
                r0 = (i // 2) * 256
                sb_ = i % 2
                S.dma("pool", stg_t[:, sb_, :, :], src[r0:r0 + 256, :].rearrange("(p j) d -> p j d", j=2),
                      W=[stgb[sb_]])
                S.dma("sp", dst[r0:r0 + 256, :].rearrange("(p j) d -> p j d", j=2), stg_t[:, sb_, :, :],
                      R=[stgb[sb_]], sembuf=cstate["osem"][sb_])
            if do_mixer:
              with contextlib.ExitStack() as stm:
                A = A0.sub(stm)
                kst_t, _ = A.sb([128, 8, L], BF16, "kst")
                kstb = [[Buf("k%d_%d" % (b, h)) for h in range(8)] for b in range(1 + NRB)]
                vst_t, _ = A.sb([128, NKT, 512], BF16, "vst")
                vstb = [Buf("v%d" % i) for i in range(NKT)]
                ring_t, _ = A.sb([128, NSLOT, 1024], BF16, "ring")
                ringb = [Buf("ring%d" % i) for i in range(NSLOT)]
                wuq_t, wuq_b = A.sb([128, 2 * 768], BF16, "wuq")
                wkn_t, wkn_b = A.sb([128, 512], BF16, "wkn")
                wv_t, wv_b = A.sb([128, 512], BF16, "wv")
                cos_t, cos_b = A.sb([128, NB], F32, "cos")
                sin_t, sin_b = A.sb([128, NB], F32, "sin")
                hT_t, _ = A.sb([128, 8, NB], BF16, "hT")
                hTb = [Buf("hT%d" % c) for c in range(8)]
                sq_t, _ = A.sb([128, 2, NB], BF16, "sq")
                sqb = [Buf("sq0"), Buf("sq1")]
                ln_t, ln_b = A.sb([128, NB], F32, "ln")
                rs_t, rs_b = A.sb([128, NB], F32, "rs")
                sig_t, _ = A.sb([128, 2, NB], F32, "sig")
                sigb = [Buf("sig0"), Buf("sig1")]
                ub_t, _ = A.sb([128, 4, 30 + NB], F32, "ubuf")
                ubb = [Buf("ub%d" % c) for c in range(4)]
                halo_t, _ = A.sb([128, 4, 30], F32, "halo")
                halob = [Buf("halo%d" % c) for c in range(4)]
                y_t, _ = A.sb([128, 4, NB], F32, "y")
                yb = [Buf("y%d" % c) for c in range(4)]
                yh_t, _ = A.sb([128, 4, NB], BF16, "yh")
                yhb = [Buf("yh%d" % c) for c in range(4)]
                ysq_t, _ = A.sb([128, 4, NB], BF16, "ysq")
                ysqb = [Buf("ysq%d" % c) for c in range(4)]
                mu_t, mu_b = A.sb([128, NB], F32, "mu")
                var_t, var_b = A.sb([128, NB], F32, "var")
                actc_t, _ = A.sb([128, 4, NB], BF16, "actc")
                actcb = [Buf("actc%d" % c) for c in range(4)]
                cq_t, _ = A.sb([128, 2, NB], F32, "cq")
                cqb = [Buf("cq0"), Buf("cq1")]
                cqn_t, _ = A.sb([128, 2, NB], BF16, "cqn")
                cqnb = [Buf("cqn0"), Buf("cqn1")]
                ckv_t, ckv_b = A.sb([128, NB], F32, "ckv")
                ckvn_t, ckvn_b = A.sb([128, NB], BF16, "ckvn")
                hc4_t, hc4_b = A.sb([128, 4 * NB], F32, "hc4")
                sq4_t, sq4_b = A.sb([128, 4 * NB], BF16, "sq4")
                r4_t, r4_b = A.sb([128, 4 * NB], F32, "r4")
                t24_t, t24_b = A.sb([128, 4 * NB], BF16, "t24")
                hnh4_t, hnh4_b = A.sb([128, 4 * NB], BF16, "hnh4")
                q_t, _ = A.sb([128, 8, NB], BF16, "qblk")
                qb = [Buf("q%d" % h) for h in range(8)]
                pT_t, _ = A.sb([128, 4, NB], BF16, "pT")
                pTb = [Buf("pT%d" % i) for i in range(4)]
                rden_t, _ = A.sb([128, 2, NB], F32, "rden")
                rdenb = [Buf("rden0"), Buf("rden1")]
                oT_t, _ = A.sb([64, 8, NB], BF16, "oT")
                oTb = [Buf("oT%d" % h) for h in range(8)]
                m1_t, _ = A.sb([128, 2, NB], F32, "m1")
                m1b = [Buf("m1_0"), Buf("m1_1")]
                mg_t, _ = A.sb([128, 8, NB], BF16, "merged")
                mgb = [Buf("mg%d" % c) for c in range(8)]

                stg_t, _ = A.sb([128, 2, 2, D], BF16, "stg")
                stgb_m = [Buf("stg0"), Buf("stg1")]
                cstate["stg"] = (stg_t, stgb_m)
                S.dma("pool", wuq_t[:], wuq_d[l], W=[wuq_b])
                S.dma("pool", wkn_t[:], wkn_d[l], W=[wkn_b])
                S.dma("pool", wv_t[:], wv_d[l], W=[wv_b])
                for c in range(4):
                    S.op("pool", lambda e, c=c: e.memset(halo_t[:, c, :], 0.0), W=[halob[c]])

                wstate = {"issued": 0, "used": 0}
                total_tiles = NWT * len(blocks)

                def w_issue():
                    i = wstate["issued"]
                    if i >= total_tiles:
                        return
                    s = i % NSLOT
                    S.dma("sp", ring_t[:, s, :], wbf_d[l, i % NWT], W=[ringb[s]])
                    wstate["issued"] += 1

                def w_take():
                    i = wstate["used"]
                    wstate["used"] += 1
                    s = i % NSLOT
                    return ring_t[:, s, :], ringb[s]

                for _ in range(NSLOT):
                    w_issue()

                for bi, (c0, n) in enumerate(blocks):
                    xs = lambda c: x_t[:, c, c0:c0 + n]
                    S.dma("sp", cos_t[64:96, 0:n], cos_d[64:96, c0:c0 + n], W=[cos_b])
                    S.dma("sp", sin_t[64:96, 0:n], sin_d[64:96, c0:c0 + n], W=[sin_b])
                    rms_feature(S, xs, xb[bi], 8, 128, n, float(D), V_MIXG, vt, vb,
                                lambda c: hT_t[:, c, 0:n], hTb, sq_t, sqb, ln_t, ln_b, rs_t, rs_b)

                    def zproj(M):
                        wt, wb = w_take()
                        pt, pb = ps_next()
                        for kc in range(8):
                            S.op("pe", lambda e, kc=kc: e.matmul(pt[0:M, 0:n], lhsT=wt[:, kc * 128:kc * 128 + M],
                                                                 rhs=hT_t[:, kc, 0:n],
                                                                 start=(kc == 0), stop=(kc == 7)),
                                 R=[wb, hTb[kc]], W=[pb])
                        w_issue()
                        return pt, pb

                    for c in range(4):
                        pa, pab = zproj(128)
                        pg, pgb = zproj(128)
                        sg = c % 2
                        S.op("act", lambda e: e.activation(out=sig_t[:, sg, 0:n], in_=pg[:, 0:n], func=AF.Sigmoid),
                             R=[pgb], W=[sigb[sg]])
                        S.op("pool", lambda e, c=c: e.tensor_copy(out=ub_t[:, c, 0:30], in_=halo_t[:, c, :]),
                             R=[halob[c]], W=[ubb[c]])
                        S.op("dve", lambda e, c=c: e.tensor_tensor(out=ub_t[:, c, 30:30 + n], in0=pa[:, 0:n],
                                                                   in1=sig_t[:, sg, 0:n], op=ALU.mult),
                             R=[pab, sigb[sg]], W=[ubb[c]])
                        S.op("pool", lambda e, c=c: e.tensor_copy(out=halo_t[:, c, :], in_=ub_t[:, c, n:n + 30]),
                             R=[ubb[c]], W=[halob[c]])
                    for c in range(2):
                        pq, pqb = zproj(128)
                        S.op("act", lambda e, c=c: e.activation(out=cq_t[:, c, 0:n], in_=pq[:, 0:n], func=AF.Copy),
                             R=[pqb], W=[cqb[c]])
                    pk, pkb = zproj(128)
                    S.op("act", lambda e: e.activation(out=ckv_t[:, 0:n], in_=pk[:, 0:n], func=AF.Copy),
                         R=[pkb], W=[ckv_b])
                    pr, prb = zproj(96)
                    kr_t, kr_b = ln_t, ln_b
                    if "krope" not in wstate:
                        wstate["krope"] = A.sb([128, NB], F32, "krope")
                    kr_t, kr_b = wstate["krope"]
                    S.op("act", lambda e: e.activation(out=kr_t[64:96, 0:n], in_=pr[64:96, 0:n], func=AF.Copy),
                         R=[prb], W=[kr_b])
                    rms_feature(S, lambda c: cq_t[:, c, 0:n], cqb, 2, 128, n, 256.0, V_QAG, vt, vb,
                                lambda c: cqn_t[:, c, 0:n], cqnb, sq_t, sqb, ln_t, ln_b, rs_t, rs_b)
                    rms_feature(S, lambda c: ckv_t[:, 0:n], [ckv_b], 1, 128, n, 128.0, V_KVAG, vt, vb,
                                lambda c: ckvn_t[:, 0:n], [ckvn_b], sq_t, sqb, ln_t, ln_b, rs_t, rs_b)

                    conv_thunks = []
                    for c in range(4):
                        conv_thunks.append(lambda c=c: S.op("dve", lambda e: e.tensor_scalar(
                            out=y_t[:, c, 0:n], in0=ub_t[:, c, 0:n],
                            scalar1=vt[:, V_CONVW + c * 31:V_CONVW + c * 31 + 1],
                            scalar2=vt[:, V_CONVB + c:V_CONVB + c + 1], op0=ALU.mult, op1=ALU.add),
                            R=[ubb[c], vb], W=[yb[c]]))
                        for k in range(1, 31):
                            conv_thunks.append(lambda c=c, k=k: S.op("dve", lambda e: e.scalar_tensor_tensor(
                                out=y_t[:, c, 0:n], in0=ub_t[:, c, k:k + n],
                                scalar=vt[:, V_CONVW + c * 31 + k:V_CONVW + c * 31 + k + 1],
                                in1=y_t[:, c, 0:n], op0=ALU.mult, op1=ALU.add),
                                R=[ubb[c], vb, yb[c]], W=[yb[c]]))
                    bkt = [0] if bi == 0 else [1 + 2 * (bi - 1), 2 + 2 * (bi - 1)]
                    for j, kt in enumerate(bkt):
                        kc0, nk = ktiles[kt]
                        off = kc0 - c0
                        pv_, pvb_ = ps_next()
                        S.op("pe", lambda e: e.matmul(pv_[0:nk, 0:512], lhsT=ckvn_t[:, off:off + nk], rhs=wv_t[:],
                                                      start=True, stop=True),
                             R=[ckvn_b, wv_b], W=[pvb_])
                        S.op("act", lambda e: e.activation(out=vst_t[0:nk, kt, :], in_=pv_[0:nk, 0:512], func=AF.Copy),
                             R=[pvb_], W=[vstb[kt]])

                    W4 = 4 * n

                    def v3(ap):
                        return ap.rearrange("p (h n) -> p h n", h=4)

                    def chunks512():
                        return [(o, min(512, W4 - o)) for o in range(0, W4, 512)]

                    def head_batch(kind, h0):
                        PP, PPb = ps_pair()
                        if kind == "k":
                            for hh in range(4):
                                h = h0 + hh
                                S.op("pe", lambda e: e.matmul(PP[0:64, hh * n:(hh + 1) * n], lhsT=wkn_t[:, h * 64:(h + 1) * 64],
                                                              rhs=ckvn_t[:, 0:n], start=True, stop=True),
                                     R=[wkn_b, ckvn_b], W=PPb)
                            S.op("act", lambda e: e.activation(out=hc4_t[0:64, 0:W4], in_=PP[0:64, 0:W4], func=AF.Copy),
                                 R=PPb, W=[hc4_b])
                            S.op("pool", lambda e: e.tensor_copy(
                                out=v3(hc4_t[64:96, 0:W4]), in_=kr_t[64:96, 0:n].unsqueeze(1).broadcast_to([32, 4, n])),
                                R=[kr_b], W=[hc4_b])
                            gcol = V_KNG
                            dst = lambda r0, r1: kst_t[r0:r1, h0:h0 + 4, c0:c0 + n]
                            dstb = [kstb[bi][h0 + i] for i in range(4)]
                        else:
                            for hh in range(4):
                                h = h0 + hh
                                for kc in range(2):
                                    S.op("pe", lambda e: e.matmul(
                                        PP[0:96, hh * n:(hh + 1) * n],
                                        lhsT=wuq_t[:, kc * 768 + h * 96:kc * 768 + (h + 1) * 96],
                                        rhs=cqn_t[:, kc, 0:n], start=(kc == 0), stop=(kc == 1)),
                                        R=[wuq_b, cqnb[kc]], W=PPb)
                            S.op("act", lambda e: e.activation(out=hc4_t[0:96, 0:W4], in_=PP[0:96, 0:W4], func=AF.Copy),
                                 R=PPb, W=[hc4_b])
                            gcol = V_QNG
                            dst = lambda r0, r1: q_t[r0:r1, h0:h0 + 4, 0:n]
                            dstb = [qb[h0 + i] for i in range(4)]
                        S.op("act", lambda e: e.activation(out=sq4_t[0:96, 0:W4], in_=hc4_t[0:96, 0:W4], func=AF.Square),
                             R=[hc4_b], W=[sq4_b])
                        PQ, PQb = ps_pair()
                        for (o, w) in chunks512():
                            S.op("pe", lambda e: e.matmul(PQ[0:96, o:o + w], lhsT=ones_t[0:96, 0:96], rhs=sq4_t[0:96, o:o + w],
                                                          start=True, stop=True),
                                 R=[ones_b, sq4_b], W=PQb)
                        S.op("act", lambda e: e.activation(out=r4_t[0:96, 0:W4], in_=PQ[0:96, 0:W4], func=AF.Ln,
                                                           bias=eps_t[0:96, :], scale=1.0 / 96),
                             R=PQb + [eps_b], W=[r4_b])
                        S.op("act", lambda e: e.activation(out=r4_t[0:96, 0:W4], in_=r4_t[0:96, 0:W4], func=AF.Exp, scale=-0.5),
                             R=[r4_b], W=[r4_b])
                        S.op("dve", lambda e: e.scalar_tensor_tensor(
                            out=hc4_t[0:96, 0:W4], in0=hc4_t[0:96, 0:W4], scalar=vt[0:96, gcol:gcol + 1],
                            in1=r4_t[0:96, 0:W4], op0=ALU.mult, op1=ALU.mult),
                            R=[hc4_b, vb, r4_b], W=[hc4_b])
                        S.op("act", lambda e: e.activation(out=dst(0, 64), in_=v3(hc4_t[0:64, 0:W4]), func=AF.Copy),
                             R=[hc4_b], W=dstb)
                        S.op("dve", lambda e: e.tensor_copy(out=hnh4_t[64:96, 0:W4], in_=hc4_t[64:96, 0:W4]),
                             R=[hc4_b], W=[hnh4_b])
                        PR, PRb = ps_pair()
                        for (o, w) in chunks512():
                            S.op("pe", lambda e: e.matmul(PR[0:96, o:o + w], lhsT=prot_t[64:96, 0:96], rhs=hnh4_t[64:96, o:o + w],
                                                          start=True, stop=True),
                                 R=[prot_b, hnh4_b], W=PRb)
                        S.op("dve", lambda e: e.tensor_tensor(
                            out=v3(t24_t[64:96, 0:W4]), in0=v3(PR[64:96, 0:W4]),
                            in1=sin_t[64:96, 0:n].unsqueeze(1).broadcast_to([32, 4, n]), op=ALU.mult),
                            R=PRb + [sin_b], W=[t24_b])
                        S.op("dve", lambda e: e.tensor_tensor(
                            out=v3(r4_t[64:96, 0:W4]), in0=v3(hc4_t[64:96, 0:W4]),
                            in1=cos_t[64:96, 0:n].unsqueeze(1).broadcast_to([32, 4, n]), op=ALU.mult),
                            R=[hc4_b, cos_b, r4_b], W=[r4_b])
                        S.op("dve", lambda e: e.tensor_tensor(out=dst(64, 96), in0=v3(r4_t[64:96, 0:W4]),
                                                              in1=v3(t24_t[64:96, 0:W4]), op=ALU.add),
                             R=[r4_b, t24_b], W=dstb)

                    for h0 in (0, 4):
                        head_batch("k", h0)
                        head_batch("q", h0)

                    if bi == 0:
                        vis = [(0, 0, False)]
                    else:
                        B = bi - 1
                        vis = [(0, 0, False)] + [(1 + m, 0, False) for m in range(2 * B)]
                        vis += [(1 + 2 * B + j, 128 * j, True) for j in range(2)]
                    items = [(h, vi) for h in range(8) for vi in range(len(vis))]
                    info = {}
                    DEPTH_S = 2

                    def emit_S(idx):
                        h, vi = items[idx]
                        kt, q0, diag = vis[vi]
                        kc0, nk = ktiles[kt]
                        kbi = 0 if kt == 0 else 1 + (kt - 1) // 2
                        sps, spsb = ps_next()
                        S.op("pe", lambda e: e.matmul(sps[0:nk, q0:n], lhsT=kst_t[0:96, h, kc0:kc0 + nk],
                                                      rhs=q_t[0:96, h, q0:n], start=True, stop=True),
                             R=[kstb[kbi][h], qb[h]], W=[spsb])
                        pi = idx % 4
                        S.op("act", lambda e: e.activation(out=pT_t[0:nk, pi, q0:n], in_=sps[0:nk, q0:n],
                                                           func=AF.Exp, scale=SCALE),
                             R=[spsb], W=[pTb[pi]])
                        if diag:
                            S.op("pool", lambda e: e.memset(pT_t[64:128, pi, q0:q0 + 64], 0.0), W=[pTb[pi]])
                        info[idx] = (kt, q0, nk, pi)

                    def emit_PV(idx):
                        h, vi = items[idx]
                        kt_, q0_, nk_, pi_ = info.pop(idx)
                        pnum, pnumb = pst[2 * (h % 2)], psb[2 * (h % 2)]
                        pden, pdenb = pst[2 * (h % 2) + 1], psb[2 * (h % 2) + 1]
                        first = (vi == 0)
                        last = (vi == len(vis) - 1)
                        hp = h - (h % 2)
                        r0 = 64 * (h % 2)
                        S.op("pe", lambda e: e.matmul(pnum[0:128, q0_:n], lhsT=vst_t[0:nk_, kt_, hp * 64:(hp + 2) * 64],
                                                      rhs=pT_t[0:nk_, pi_, q0_:n], start=first, stop=last),
                             R=[vstb[kt_], pTb[pi_]], W=[pnumb])
                        S.op("pe", lambda e: e.matmul(pden[0:128, q0_:n], lhsT=ones_t[0:nk_, 0:128],
                                                      rhs=pT_t[0:nk_, pi_, q0_:n], start=first, stop=last),
                             R=[ones_b, pTb[pi_]], W=[pdenb])
                        if last:
                            rb = h % 2
                            S.op("dve", lambda e: e.reciprocal(out=rden_t[r0:r0 + 64, rb, 0:n], in_=pden[r0:r0 + 64, 0:n]),
                                 R=[pdenb], W=[rdenb[rb]])
                            S.op("dve", lambda e: e.tensor_tensor(out=oT_t[0:64, h, 0:n], in0=pnum[r0:r0 + 64, 0:n],
                                                                  in1=rden_t[r0:r0 + 64, rb, 0:n], op=ALU.mult),
                                 R=[pnumb, rdenb[rb]], W=[oTb[h]])

                    for idx in range(len(items) + DEPTH_S):
                        if idx < len(items):
                            if items[idx][1] == 0:
                                for _ in range(16):
                                    if conv_thunks:
                                        conv_thunks.pop(0)()
                                conv_step()
                                conv_step()
                            emit_S(idx)
                        if idx - DEPTH_S >= 0:
                            emit_PV(idx - DEPTH_S)

                    while conv_thunks:
                        conv_thunks.pop(0)()
                    p1, p1b = ps_next()
                    p2, p2b = ps_next()
                    for c in range(4):
                        S.op("act", lambda e, c=c: e.activation(out=yh_t[:, c, 0:n], in_=y_t[:, c, 0:n], func=AF.Copy),
                             R=[yb[c]], W=[yhb[c]])
                        S.op("act", lambda e, c=c: e.activation(out=ysq_t[:, c, 0:n], in_=y_t[:, c, 0:n], func=AF.Square),
                             R=[yb[c]], W=[ysqb[c]])
                    for c in range(4):
                        S.op("pe", lambda e, c=c: e.matmul(p1[:, 0:n], lhsT=ones_t[:], rhs=yh_t[:, c, 0:n],
                                                           start=(c == 0), stop=(c == 3)),
                             R=[ones_b, yhb[c]], W=[p1b])
                    for c in range(4):
                        S.op("pe", lambda e, c=c: e.matmul(p2[:, 0:n], lhsT=ones_t[:], rhs=ysq_t[:, c, 0:n],
                                                           start=(c == 0), stop=(c == 3)),
                             R=[ones_b, ysqb[c]], W=[p2b])
                    S.op("dve", lambda e: e.tensor_scalar(out=mu_t[:, 0:n], in0=p1[:, 0:n], scalar1=1.0 / 512,
                                                          scalar2=None, op0=ALU.mult),
                         R=[p1b], W=[mu_b])
                    S.op("dve", lambda e: e.tensor_tensor(out=var_t[:, 0:n], in0=mu_t[:, 0:n], in1=mu_t[:, 0:n],
                                                          op=ALU.mult),
                         R=[mu_b], W=[var_b])
                    S.op("dve", lambda e: e.scalar_tensor_tensor(out=var_t[:, 0:n], in0=p2[:, 0:n], scalar=1.0 / 512,
                                                                 in1=var_t[:, 0:n], op0=ALU.mult, op1=ALU.subtract),
                         R=[p2b, var_b], W=[var_b])
                    S.op("act", lambda e: e.activation(out=ln_t[:, 0:n], in_=var_t[:, 0:n], func=AF.Ln,
                                                       bias=eps_t[:, :], scale=1.0),
                         R=[var_b, eps_b], W=[ln_b])
                    S.op("act", lambda e: e.activation(out=rs_t[:, 0:n], in_=ln_t[:, 0:n], func=AF.Exp, scale=-0.5),
                         R=[ln_b], W=[rs_b])
                    for c in range(4):
                        S.op("dve", lambda e, c=c: e.tensor_tensor(out=y_t[:, c, 0:n], in0=y_t[:, c, 0:n],
                                                                   in1=mu_t[:, 0:n], op=ALU.subtract),
                             R=[yb[c], mu_b], W=[yb[c]])
                        S.op("dve", lambda e, c=c: e.tensor_tensor(out=y_t[:, c, 0:n], in0=y_t[:, c, 0:n],
                                                                   in1=rs_t[:, 0:n], op=ALU.mult),
                             R=[yb[c], rs_b], W=[yb[c]])
                        S.op("act", lambda e, c=c: e.activation(
                            out=actc_t[:, c, 0:n], in_=y_t[:, c, 0:n], func=AF.Silu,
                            bias=vt[:, V_LNB + c:V_LNB + c + 1], scale=vt[:, V_LNG + c:V_LNG + c + 1]),
                            R=[yb[c], vb], W=[actcb[c]])

                    for c in range(8):
                        pg1, pg1b = zproj(128)
                        S.op("act", lambda e: e.activation(out=sig_t[:, 0, 0:n], in_=pg1[:, 0:n], func=AF.Sigmoid),
                             R=[pg1b], W=[sigb[0]])
                        wt, wb = w_take()
                        pc, pcb = ps_next()
                        for kc in range(4):
                            S.op("pe", lambda e, kc=kc: e.matmul(pc[:, 0:n], lhsT=wt[:, kc * 128:(kc + 1) * 128],
                                                                 rhs=actc_t[:, kc, 0:n], start=(kc == 0), stop=(kc == 3)),
                                 R=[wb, actcb[kc]], W=[pcb])
                        w_issue()
                        S.op("dve", lambda e: e.tensor_tensor(out=m1_t[:, 0, 0:n], in0=pc[:, 0:n], in1=sig_t[:, 0, 0:n],
                                                              op=ALU.mult),
                             R=[pcb, sigb[0]], W=[m1b[0]])
                        pg2, pg2b = zproj(128)
                        S.op("act", lambda e: e.activation(out=sig_t[:, 1, 0:n], in_=pg2[:, 0:n], func=AF.Sigmoid),
                             R=[pg2b], W=[sigb[1]])
                        wt2, wb2 = w_take()
                        pm, pmb = ps_next()
                        for h in range(8):
                            S.op("pe", lambda e, h=h: e.matmul(pm[:, 0:n], lhsT=wt2[0:64, h * 128:(h + 1) * 128],
                                                               rhs=oT_t[0:64, h, 0:n], start=(h == 0), stop=(h == 7)),
                                 R=[wb2, oTb[h]], W=[pmb])
                        w_issue()
                        S.op("dve", lambda e: e.tensor_tensor(out=m1_t[:, 1, 0:n], in0=pm[:, 0:n], in1=sig_t[:, 1, 0:n],
                                                              op=ALU.mult),
                             R=[pmb, sigb[1]], W=[m1b[1]])
                        S.op("dve", lambda e, c=c: e.tensor_tensor(out=mg_t[:, c, 0:n], in0=m1_t[:, 0, 0:n],
                                                                   in1=m1_t[:, 1, 0:n], op=ALU.add),
                             R=[m1b[0], m1b[1]], W=[mgb[c]])
                    for c in range(8):
                        wt, wb = w_take()
                        po, pob = ps_next()
                        for kc in range(8):
                            S.op("pe", lambda e, kc=kc: e.matmul(po[:, 0:n], lhsT=wt[:, kc * 128:(kc + 1) * 128],
                                                                 rhs=mg_t[:, kc, 0:n], start=(kc == 0), stop=(kc == 7)),
                                 R=[wb, mgb[kc]], W=[pob])
                        w_issue()
                        S.op("dve", lambda e, c=c: e.tensor_tensor(out=x_t[:, c, c0:c0 + n], in0=x_t[:, c, c0:c0 + n],
                                                                   in1=po[:, 0:n], op=ALU.add),
                             R=[pob, xb[bi][c]], W=[xb[bi][c]])
                while cstate["i"] < NCV:
                    conv_step()
                S.barrier(stgb_m)

            if do_peer:
              with contextlib.ExitStack() as stp:
                A = A0.sub(stp)
                wq_t, wq_b = A.sb([128, 8 * 2048], BF16, "wq")
                ky_t, ky_b = A.sb([128, 16 * 128], BF16, "keysT")
                hT_t, _ = A.sb([128, 8, 128], BF16, "h2T")
                hTb = [Buf("h2T%d" % c) for c in range(8)]
                sq_t, _ = A.sb([128, 2, 128], BF16, "sq")
                sqb = [Buf("sq0"), Buf("sq1")]
                ln_t, ln_b = A.sb([128, 128], F32, "ln")
                rs_t, rs_b = A.sb([128, 128], F32, "rs")
                htok2_t, _ = A.sb([128, 2, D], BF16, "h2tok")
                htok2_b = [Buf("htok0"), Buf("htok1")]
                qT_t, _ = A.sb([128, 16, 128], BF16, "qT")
                qTb = [Buf("qT%d" % g) for g in range(4)]
                s_t, s_b = A.sb([128, 16, 128], F32, "s")
                s2_t, s2_b = A.sb([128, 16, 128], F32, "s2")
                sv_t, sv_b = A.sb([128, 16, 16], F32, "sv")
                si_t, si_b = A.sb([128, 16, 16], U32, "si")
                sif_t, sif_b = A.sb([128, 16, 16], F32, "sif")
                cand_t, cand_b = A.sb([128, 8, 256], F32, "cand")
                cand2_t, cand2_b = s2_t[:, :, :].rearrange("p g n -> p (g n)").rearrange("p (h c) -> p h c", h=8), s2_b
                top_t, top_b = A.sb([128, 8, 16], F32, "top")
                pos_t, pos_b = A.sb([128, 8, 16], U32, "pos")
                ai_t, ai_b = A.sb([128, 8, 16], U32, "ai")
                bi_t, bi_b = A.sb([128, 8, 16], U32, "bi")
                af_t, af_b = A.sb([128, 8, 16], F32, "af")
                bf_t, bf_b = A.sb([128, 8, 16], F32, "bf")
                oh_t, oh_b = s_t[:, :, :].rearrange("p g n -> p (g n)").rearrange("p (h c) -> p h c", h=8), s_b
                tmp_t, tmp_b = cand_t, cand_b
                isel_t, isel_b = A.sb([128, 128], F32, "isel")
                jsel_t, jsel_b = A.sb([128, 128], F32, "jsel")
                eidf_t, eidf_b = A.sb([128, 128], F32, "eidf")
                eid2_t, _ = A.sb([128, 2, 128], I32, "eid")
                eid2_b = [Buf("eid0"), Buf("eid1")]
                ew_t, ew_b = A.sb([128, 128], F32, "ew")
                ssum_t, ssum_b = A.sb([128, 8], F32, "ssum")
                gw2_t, _ = A.sb([128, 2, 128], F32, "gw")
                gw2_b = [Buf("gw0"), Buf("gw1")]
                a_t, a_b = A.sb([128, 128], F32, "a")
                g1_t, g1_b = A.sb([128, 128], F32, "g1")
                g2_t, g2_b = A.sb([128, 128], F32, "g2")
                actw_t, actw_b = A.sb([128, 128], F32, "actw")
                junk_t, junk_b = A.sb([128, D], BF16, "junk")
                ng = NG if do_mixer else NG - 2
                gb_t, _ = A.sb([128, ng, 2 * D], BF16, "gbuf")
                gbb = [Buf("gb%d" % i) for i in range(ng)]
                acc_t, acc_b = gb_t[:, 0, :].bitcast(F32), gbb[0]
                dg_t, _ = A.sb([128, 4, 128], BF16, "diag")
                dgb = [Buf("dg%d" % i) for i in range(4)]
                g1b = [Buf("g1_%d" % i) for i in range(32)]
                g2b = [Buf("g2_%d" % i) for i in range(32)]
                awb = [Buf("aw_%d" % i) for i in range(32)]
                if cstate["i"] < NCV:
                    stg_t, _ = A.sb([128, 2, 2, D], BF16, "stg")
                    stgb_p = [Buf("stg0"), Buf("stg1")]
                    cstate["stg"] = (stg_t, stgb_p)
                    while cstate["i"] < NCV:
                        conv_step()
                    S.barrier(stgb_p)

                posf_t, posf_b = A.sb([128, 8, 16], F32, "posf")
                thr_t, thr_b = A.sb([128, 16], F32, "thr")
                S.op("dve", lambda e: e.tensor_scalar(out=thr_t[:], in0=iota16, scalar1=16.0, scalar2=16.0,
                                                      op0=ALU.mult, op1=ALU.add), R=[cst_b], W=[thr_b])
                S.dma("pool", wq_t[:], wq_d[l], W=[wq_b])
                S.dma("pool", ky_t[:], keys_d[l], W=[ky_b])
                gi = [0]
                ptiles = list(range(NKT))
                if l == n_layers - 1:
                    ptiles = ptiles[1:]
                def front(X, tt, par):
                    htok_t, htok_b = htok2_t[:, par, :], htok2_b[par]
                    eid_t, eid_b = eid2_t[:, par, :], eid2_b[par]
                    gw_t, gw_b = gw2_t[:, par, :], gw2_b[par]
                    c0, np_ = ktiles[tt]
                    bi = 0 if tt == 0 else 1 + (tt - 1) // 2
                    xs = lambda c: x_t[:, c, c0:c0 + np_]
                    rms_feature(X, xs, xb[bi], 8, 128, np_, float(D), V_FFNG, vt, vb,
                                lambda c: hT_t[:, c, 0:np_], hTb, sq_t, sqb, ln_t, ln_b, rs_t, rs_b)
                    ptk_f, ptk_b = ps_next()
                    ptk = ptk_f.bitcast(BF16)
                    for c in range(8):
                        X.op("pe", lambda e, c=c: e.transpose(out=ptk[0:np_, c * 128:(c + 1) * 128],
                                                              in_=hT_t[:, c, 0:np_], identity=identb_t[:]),
                             R=[hTb[c], identb_b], W=[ptk_b])
                    X.op("act", lambda e: e.activation(out=htok_t[0:np_, :], in_=ptk[0:np_, :], func=AF.Copy),
                         R=[ptk_b], W=[htok_b])
                    for gq in range(4):
                        pq, pqb = ps_next()
                        for gg in range(4):
                            g = gq * 4 + gg
                            for kc in range(8):
                                X.op("pe", lambda e, g=g, gg=gg, kc=kc: e.matmul(
                                    pq[:, gg * 128:gg * 128 + np_],
                                    lhsT=wq_t[:, kc * 2048 + g * 128:kc * 2048 + (g + 1) * 128],
                                    rhs=hT_t[:, kc, 0:np_], start=(kc == 0), stop=(kc == 7)),
                                    R=[wq_b, hTb[kc]], W=[pqb])
                        X.op("act", lambda e, gq=gq: e.activation(
                            out=qT_t[:, gq * 4:(gq + 1) * 4, 0:np_],
                            in_=pq[:, :].rearrange("p (g t) -> p g t", g=4)[:, :, 0:np_], func=AF.Copy),
                            R=[pqb], W=[qTb[gq]])
                    for gq in range(4):
                        pss, pssb = ps_next()
                        for gg in range(4):
                            g = gq * 4 + gg
                            X.op("pe", lambda e, g=g, gg=gg: e.matmul(
                                pss[0:np_, gg * 128:(gg + 1) * 128], lhsT=qT_t[:, g, 0:np_],
                                rhs=ky_t[:, g * 128:(g + 1) * 128], start=True, stop=True),
                                R=[qTb[gq], ky_b], W=[pssb])
                        X.op("act", lambda e, gq=gq: e.activation(
                            out=s_t[0:np_, gq * 4:(gq + 1) * 4, :],
                            in_=pss[0:np_, :].rearrange("p (g n) -> p g n", g=4), func=AF.Copy),
                            R=[pssb], W=[s_b])
                    P = np_
                    for g in range(16):
                        X.op("dve", lambda e, g=g: e.max(out=sv_t[0:P, g, 0:8], in_=s_t[0:P, g, :]), R=[s_b], W=[sv_b])
                    for g in range(16):
                        X.op("dve", lambda e, g=g: e.max_index(out=si_t[0:P, g, 0:8], in_max=sv_t[0:P, g, 0:8],
                                                               in_values=s_t[0:P, g, :]), R=[s_b, sv_b], W=[si_b])
                    for g in range(16):
                        X.op("dve", lambda e, g=g: e.match_replace(out=s2_t[0:P, g, :], in_to_replace=sv_t[0:P, g, 0:8],
                                                                   in_values=s_t[0:P, g, :], imm_value=-1e30),
                             R=[s_b, sv_b], W=[s2_b])
                    for g in range(16):
                        X.op("dve", lambda e, g=g: e.max(out=sv_t[0:P, g, 8:16], in_=s2_t[0:P, g, :]), R=[s2_b], W=[sv_b])
                    for g in range(16):
                        X.op("dve", lambda e, g=g: e.max_index(out=si_t[0:P, g, 8:16], in_max=sv_t[0:P, g, 8:16],
                                                               in_values=s2_t[0:P, g, :]), R=[s2_b, sv_b], W=[si_b])
                    X.op("dve", lambda e: e.tensor_copy(out=sif_t[0:P], in_=si_t[0:P]), R=[si_b], W=[sif_b])
                    cand4 = cand_t[0:P].rearrange("p h (a b) -> p h a b", a=16)
                    X.op("dve", lambda e: e.tensor_tensor(
                        out=cand4, in0=sv_t[0:P, 0::2, :].unsqueeze(3).broadcast_to([P, 8, 16, 16]),
                        in1=sv_t[0:P, 1::2, :].unsqueeze(2).broadcast_to([P, 8, 16, 16]), op=ALU.add),
                        R=[sv_b], W=[cand_b])
                    for h in range(8):
                        X.op("dve", lambda e, h=h: e.max(out=top_t[0:P, h, 0:8], in_=cand_t[0:P, h, :]), R=[cand_b], W=[top_b])
                    for h in range(8):
                        X.op("dve", lambda e, h=h: e.max_index(out=pos_t[0:P, h, 0:8], in_max=top_t[0:P, h, 0:8],
                                                               in_values=cand_t[0:P, h, :]), R=[cand_b, top_b], W=[pos_b])
                    for h in range(8):
                        X.op("dve", lambda e, h=h: e.match_replace(out=cand2_t[0:P, h, :], in_to_replace=top_t[0:P, h, 0:8],
                                                                   in_values=cand_t[0:P, h, :], imm_value=-1e30),
                             R=[cand_b, top_b], W=[cand2_b])
                    for h in range(8):
                        X.op("dve", lambda e, h=h: e.max(out=top_t[0:P, h, 8:16], in_=cand2_t[0:P, h, :]), R=[cand2_b], W=[top_b])
                    for h in range(8):
                        X.op("dve", lambda e, h=h: e.max_index(out=pos_t[0:P, h, 8:16], in_max=top_t[0:P, h, 8:16],
                                                               in_values=cand2_t[0:P, h, :]), R=[cand2_b, top_b], W=[pos_b])
                    oh4 = oh_t[0:P].rearrange("p h (k a) -> p h k a", k=16)
                    X.op("dve", lambda e: e.tensor_copy(out=posf_t[0:P], in_=pos_t[0:P]), R=[pos_b], W=[posf_b])
                    X.op("dve", lambda e: e.tensor_tensor(
                        out=oh4, in0=posf_t[0:P].unsqueeze(3).broadcast_to([P, 8, 16, 16]),
                        in1=thr_t[0:P, :].unsqueeze(1).unsqueeze(1).broadcast_to([P, 8, 16, 16]), op=ALU.is_ge),
                        R=[posf_b, thr_b], W=[oh_b])
                    X.op("dve", lambda e: e.tensor_reduce(
                        out=af_t[0:P].rearrange("p h k -> p (h k)"),
                        in_=oh_t[0:P].rearrange("p h (k a) -> p (h k) a", k=16), axis=AX.X, op=ALU.add),
                        R=[oh_b], W=[af_b])
                    X.op("dve", lambda e: e.scalar_tensor_tensor(
                        out=bf_t[0:P].rearrange("p h k -> p (h k)"), in0=af_t[0:P].rearrange("p h k -> p (h k)"),
                        scalar=-16.0, in1=posf_t[0:P].rearrange("p h k -> p (h k)"), op0=ALU.mult, op1=ALU.add),
                        R=[af_b, posf_b], W=[bf_b])
                    oh4 = oh_t[0:P].rearrange("p h (k a) -> p h k a", k=16)
                    tmp4 = tmp_t[0:P].rearrange("p h (k a) -> p h k a", k=16)
                    io4 = iota16[0:P, :].unsqueeze(1).unsqueeze(1).broadcast_to([P, 8, 16, 16])
                    for (xf_t, xf_b, par, dst_t, dst_b) in ((af_t, af_b, 0, isel_t, isel_b),
                                                            (bf_t, bf_b, 1, jsel_t, jsel_b)):
                        X.op("dve", lambda e, xf_t=xf_t: e.tensor_tensor(
                            out=oh4, in0=xf_t[0:P].unsqueeze(3).broadcast_to([P, 8, 16, 16]), in1=io4,
                            op=ALU.is_equal), R=[xf_b, cst_b], W=[oh_b])
                        X.op("dve", lambda e, par=par: e.tensor_tensor(
                            out=tmp4, in0=oh4,
                            in1=sif_t[0:P, par::2, :].unsqueeze(2).broadcast_to([P, 8, 16, 16]), op=ALU.mult),
                            R=[oh_b, sif_b], W=[tmp_b])
                        X.op("dve", lambda e, dst_t=dst_t: e.tensor_reduce(
                            out=dst_t[0:P, :], in_=tmp_t[0:P].rearrange("p h (k a) -> p (h k) a", k=16),
                            axis=AX.X, op=ALU.add), R=[tmp_b], W=[dst_b])
                    X.op("dve", lambda e: e.scalar_tensor_tensor(out=eidf_t[0:P, :], in0=isel_t[0:P, :], scalar=128.0,
                                                                 in1=jsel_t[0:P, :], op0=ALU.mult, op1=ALU.add),
                         R=[isel_b, jsel_b], W=[eidf_b])
                    X.op("dve", lambda e: e.tensor_copy(out=eid_t[0:P, :], in_=eidf_t[0:P, :]), R=[eidf_b], W=[eid_b])
                    ew3 = ew_t[0:P, :].rearrange("p (h k) -> p h k", h=8)
                    X.op("dve", lambda e: e.tensor_tensor(out=ew3, in0=top_t[0:P],
                                                          in1=top_t[0:P, :, 0:1].broadcast_to([P, 8, 16]),
                                                          op=ALU.subtract), R=[top_b], W=[ew_b])
                    X.op("act", lambda e: e.activation(out=ew_t[0:P, :], in_=ew_t[0:P, :], func=AF.Exp),
                         R=[ew_b], W=[ew_b])
                    X.op("dve", lambda e: e.tensor_reduce(out=ssum_t[0:P, :], in_=ew3, axis=AX.X, op=ALU.add),
                         R=[ew_b], W=[ssum_b])
                    X.op("dve", lambda e: e.reciprocal(out=ssum_t[0:P, :], in_=ssum_t[0:P, :]), R=[ssum_b], W=[ssum_b])
                    X.op("dve", lambda e: e.tensor_tensor(out=gw_t[0:P, :].rearrange("p (h k) -> p h k", h=8), in0=ew3,
                                                          in1=ssum_t[0:P, :].unsqueeze(2).broadcast_to([P, 8, 16]),
                                                          op=ALU.mult), R=[ew_b, ssum_b], W=[gw_b])
                def back(tt, par, nxt):
                    c0, np_ = ktiles[tt]
                    P = np_
                    bi = 0 if tt == 0 else 1 + (tt - 1) // 2
                    htok_t, htok_b = htok2_t[:, par, :], htok2_b[par]
                    eid_t, eid_b = eid2_t[:, par, :], eid2_b[par]
                    gw_t, gw_b = gw2_t[:, par, :], gw2_b[par]
                    per = (len(nxt.th) + 31) // 32 if nxt is not None else 0
                    acc0, acc0b = pst[0], psb[0]
                    acc1, acc1b = pst[1], psb[1]
                    di = [0]
                    for k in range(128):
                        gbi = gi[0] % ng
                        gi[0] += 1
                        S.dma("pool", gb_t[0:P, gbi, :], uv_d[l],
                              R=[eid_b], W=[gbb[gbi]],
                              indirect=bass.IndirectOffsetOnAxis(ap=eid_t[0:P, k:k + 1], axis=0))
                        S.op("dve", lambda e: e.scalar_tensor_tensor(
                            out=junk_t[0:P, :], in0=gb_t[0:P, gbi, 0:D], scalar=1.0, in1=htok_t[0:P, :],
                            op0=ALU.mult, op1=ALU.mult, accum_out=a_t[0:P, k:k + 1]),
                            R=[gbb[gbi], htok_b], W=[junk_b, a_b])
                        if k % 4 != 3:
                            continue
                        g = k // 4
                        sl = slice(4 * g, 4 * g + 4)
                        S.op("act", lambda e: e.activation(out=g2_t[0:P, sl], in_=a_t[0:P, sl], func=AF.Gelu_apprx_tanh),
                             R=[a_b], W=[g2b[g]])
                        S.op("dve", lambda e: e.tensor_tensor(out=actw_t[0:P, sl], in0=g2_t[0:P, sl], in1=gw_t[0:P, sl], op=ALU.mult),
                             R=[g2b[g], gw_b], W=[awb[g]])
                        for kk in range(4 * g, 4 * g + 4):
                            gbk = (gi[0] - (k + 1) + kk) % ng
                            dj = di[0] % 4
                            di[0] += 1
                            S.op("act", lambda e: e.activation(out=dg_t[0:P, dj, 0:P], in_=identb_t[0:P, 0:P], func=AF.Copy,
                                                               scale=actw_t[0:P, kk:kk + 1]),
                                 R=[identb_b, awb[g]], W=[dgb[dj]])
                            S.op("pe", lambda e: e.matmul(acc0[0:P, 0:512], lhsT=dg_t[0:P, dj, 0:P], rhs=gb_t[0:P, gbk, D:D + 512],
                                                          start=(kk == 0), stop=(kk == 127)),
                                 R=[dgb[dj], gbb[gbk]], W=[acc0b])
                            S.op("pe", lambda e: e.matmul(acc1[0:P, 0:512], lhsT=dg_t[0:P, dj, 0:P], rhs=gb_t[0:P, gbk, D + 512:2 * D],
                                                          start=(kk == 0), stop=(kk == 127)),
                                 R=[dgb[dj], gbb[gbk]], W=[acc1b])
                        if nxt is not None:
                            nxt.flush(per)
                    S.op("act", lambda e: e.activation(out=acc_t[0:P, 0:512], in_=acc0[0:P, 0:512], func=AF.Copy),
                         R=[acc0b], W=[acc_b])
                    S.op("act", lambda e: e.activation(out=acc_t[0:P, 512:1024], in_=acc1[0:P, 0:512], func=AF.Copy),
                         R=[acc1b], W=[acc_b])
                    for c in range(8):
                        ptr, ptrb = ps_next()
                        S.op("pe", lambda e, c=c: e.transpose(out=ptr[:, 0:P], in_=acc_t[0:P, c * 128:(c + 1) * 128],
                                                              identity=identf[0:P, 0:P]),
                             R=[acc_b, cst_b], W=[ptrb])
                        S.op("dve", lambda e, c=c: e.tensor_tensor(out=x_t[:, c, c0:c0 + P], in0=x_t[:, c, c0:c0 + P],
                                                                   in1=ptr[:, 0:P], op=ALU.add),
                             R=[ptrb, xb[bi][c]], W=[xb[bi][c]])
                r0 = Rec(S)
                front(r0, ptiles[0], 0)
                r0.flush()
                for idx, tt in enumerate(ptiles):
                    nxt = None
                    if idx + 1 < len(ptiles):
                        nxt = Rec(S)
                        front(nxt, ptiles[idx + 1], (idx + 1) % 2)
                    back(tt, idx % 2, nxt)
                    if nxt is not None:
                        nxt.flush()
                S.barrier()

        outb = Buf("out")
        for c in range(8):
            rl = [xb[1 + b][c] for b in range(NRB)]
            S.dma("sp", out_d[c * 128:(c + 1) * 128, :], x_t[:, c, NMETA:L], R=rl, sembuf=outb)
        S.wait_all("sp", [outb] + [xb[1 + b][c] for b in range(NRB) for c in range(8)])
        nc._stats = (S.nins, S.nwait, S.ndsem, dict(S.cnt))
    return nc


def _wstream(w_in, w_conv_out, w_mla_out, w_out):
    tiles = []

    def ztile(c0, m):
        t = np.zeros((128, 8, 128), np.float32)
        t[:, :, :m] = w_in[:, c0:c0 + m].reshape(8, 128, m).transpose(1, 0, 2)
        tiles.append(t.reshape(128, 1024))

    for c in range(4):
        ztile(128 * c, 128)
        ztile(512 + 128 * c, 128)
    ztile(1024, 128)
    ztile(1152, 128)
    ztile(1280, 128)
    ztile(1344, 96)
    for c in range(8):
        ztile(1440 + 128 * c, 128)
        t = np.zeros((128, 8, 128), np.float32)
        t[:, 0:4, :] = w_conv_out[:, 128 * c:128 * (c + 1)].reshape(4, 128, 128).transpose(1, 0, 2)
        tiles.append(t.reshape(128, 1024))
        ztile(2464 + 128 * c, 128)
        t = np.zeros((128, 8, 128), np.float32)
        t[0:64, :, :] = w_mla_out[:, 128 * c:128 * (c + 1)].reshape(8, 64, 128).transpose(1, 0, 2)
        tiles.append(t.reshape(128, 1024))
    for c in range(8):
        t = w_out[:, 128 * c:128 * (c + 1)].reshape(8, 128, 128).transpose(1, 0, 2)
        tiles.append(np.ascontiguousarray(t).reshape(128, 1024))
    assert len(tiles) == NWT
    return np.stack(tiles)


def _prep_shared(inp):
    f = lambda a: np.ascontiguousarray(np.asarray(a, dtype=np.float32))
    sh = {}
    sh["wstream"] = np.stack([_wstream(f(inp["w_in"][l]), f(inp["w_conv_out"][l]), f(inp["w_mla_out"][l]),
                                       f(inp["w_out"][l])) for l in range(DEPTH)])
    sh["wuq"] = f(np.stack([f(inp["w_uq"][l]).reshape(2, 128, 768).transpose(1, 0, 2).reshape(128, 1536)
                            for l in range(DEPTH)]))
    wukv = f(inp["w_ukv"]).reshape(DEPTH, 128, 8, 128)
    sh["wkn"] = f(wukv[:, :, :, 0:64].reshape(DEPTH, 128, 512))
    sh["wv"] = f(wukv[:, :, :, 64:128].reshape(DEPTH, 128, 512))
    vecs = np.zeros((DEPTH, 128, NVEC), np.float32)
    for l in range(DEPTH):
        vecs[l, :, V_MIXG:V_MIXG + 8] = f(inp["mix_norm_g"][l]).reshape(8, 128).T
        vecs[l, :, V_FFNG:V_FFNG + 8] = f(inp["ffn_norm_g"][l]).reshape(8, 128).T
        cw = f(inp["conv_w"][l]).reshape(31, 4, 128)
        vecs[l, :, V_CONVW:V_CONVW + 124] = cw.transpose(2, 1, 0).reshape(128, 124)
        vecs[l, :, V_CONVB:V_CONVB + 4] = f(inp["conv_b"][l]).reshape(4, 128).T
        vecs[l, :, V_LNG:V_LNG + 4] = f(inp["conv_ln_g"][l]).reshape(4, 128).T
        vecs[l, :, V_LNB:V_LNB + 4] = f(inp["conv_ln_b"][l]).reshape(4, 128).T
        vecs[l, :, V_QAG:V_QAG + 2] = f(inp["q_a_norm_g"][l]).reshape(2, 128).T
        vecs[l, :, V_KVAG] = f(inp["kv_a_norm_g"][l])
        vecs[l, 0:96, V_QNG] = f(inp["q_norm_g"][l])
        vecs[l, 0:96, V_KNG] = f(inp["k_norm_g"][l])
    sh["vecs"] = vecs
    sh["wq"] = f(np.stack([f(inp["peer_wq"][l]).reshape(8, 128, 2048).transpose(1, 0, 2).reshape(128, 8 * 2048)
                           for l in range(DEPTH)]))
    ky = f(inp["peer_keys"]).reshape(DEPTH, 16, 128, 128)
    sh["keysT"] = f(ky.transpose(0, 3, 1, 2).reshape(DEPTH, 128, 16 * 128))
    for l in range(DEPTH):
        sh["peer_u%d" % l] = f(inp["peer_u"][l])
        sh["peer_v%d" % l] = f(inp["peer_v"][l])
    pos = np.arange(L, dtype=np.float32)
    inv = (1.0 / (np.float32(10000.0) ** (np.arange(0, 32, 2, dtype=np.float32) / np.float32(32)))).astype(np.float32)
    ang = pos[:, None] * inv[None, :]
    ang = np.concatenate([ang, ang], axis=-1)
    cosT = np.zeros((96, L), np.float32)
    sinT = np.zeros((96, L), np.float32)
    cosT[64:96] = np.cos(ang).T
    sinT[64:96] = np.sin(ang).T
    sh["cosT"] = cosT
    sh["sinT"] = sinT
    cst = np.zeros((128, 240), np.float32)
    cst[:, 0:128] = np.eye(128, dtype=np.float32)
    prot = np.zeros((128, 96), np.float32)
    for m in range(16):
        prot[64 + m + 16, 64 + m] = -1.0
        prot[64 + m, 64 + m + 16] = 1.0
    cst[:, 128:224] = prot
    cst[:, 224:240] = np.arange(16, dtype=np.float32)[None, :]
    sh["consts"] = cst
    sh["metaT"] = f(f(inp["meta_tokens"]).T)
    return sh


_CACHE = {}


def kernel(**inputs):
    x = np.asarray(inputs["x"], dtype=np.float32)
    nb = x.shape[0]
    sh = _prep_shared(inputs)
    if "nc" not in _CACHE:
        _CACHE["nc"] = build_program()
    nc = _CACHE["nc"]
    in_maps = []
    for b in range(nb):
        m = dict(sh)
        m["xT"] = np.ascontiguousarray(x[b].T)
        in_maps.append(m)
    res = run_bass_kernel_spmd(nc, in_maps, core_ids=list(range(nb)))
    out = np.stack([np.asarray(r["outT"], dtype=np.float32).T for r in res.results], axis=0)
    return np.ascontiguousarray(out)
```
